# Optimizing a Trainium2 kernel written in Bass

```python
import math
import jax
import jax.numpy as jnp
from jax import lax
import numpy as np

D_MODEL = 1024
BATCH = 4
SEQ = 4096
DEPTH = 1

N_META = 16
BLOCK = 128
PAD = BLOCK - N_META
EPS = 1e-6
NEG_INF = -1e30
H_A = 8
DK_A = 128
DV_A = 128
CONV_K = 4
CHUNK = 64
H_B = 8
DH_B = 128
FORGET_BIAS = 2.0
N_KEYS = 128
N_EXPERTS = N_KEYS * N_KEYS
H_P = 8
D_QP = 256
TOPK = 16
PEER_BLOCK = 256

A_QK = H_A * DK_A
A_V = H_A * DV_A
B_W = H_B * DH_B
BRANCH_W = A_V
CONV_CH = 2 * A_QK + A_V
IN_SIZES = (CONV_CH, A_V, H_A, H_A, 3 * B_W, H_B, 2 * D_MODEL)
IN_COLS = 2 * A_QK + 2 * A_V + 2 * H_A + 3 * B_W + H_B + 2 * D_MODEL

kernel_name = 'hybrid_gdn_fox_peer_layer'


def rms_norm(x, w):
    xf = x.astype(jnp.float32)
    y = xf * lax.rsqrt(jnp.mean(xf * xf, axis=-1, keepdims=True) + EPS)
    return (y * w.astype(jnp.float32)).astype(x.dtype)


def l2_norm(x):
    xf = x.astype(jnp.float32)
    return (xf * lax.rsqrt(jnp.sum(xf * xf, axis=-1, keepdims=True) + EPS)).astype(x.dtype)


def pad_front(t):
    return jnp.pad(t, [(0, 0), (PAD, 0)] + [(0, 0)] * (t.ndim - 2))


def split_cols(t, sizes):
    return jnp.split(t, [int(s) for s in np.cumsum(sizes)[:-1]], axis=-1)


def causal_dwconv_silu(x, w):
    y = lax.conv_general_dilated(x, w.astype(x.dtype)[:, None, :], window_strides=(1,),
                                 padding=[(CONV_K - 1, 0)],
                                 dimension_numbers=('NWC', 'WIO', 'NWC'),
                                 feature_group_count=x.shape[-1])
    return jax.nn.silu(y)


def gated_delta_rule(q, k, v, beta, g):
    out_dtype = v.dtype
    B_, Lp, H, _ = k.shape
    DV = v.shape[-1]
    N = Lp // CHUNK

    def to_chunks(t):
        t = t.astype(jnp.float32).reshape((B_, N, CHUNK, H) + t.shape[3:])
        return jnp.swapaxes(t, 2, 3)

    q, k, v, beta, g = (to_chunks(t) for t in (q, k, v, beta, g))
    G = jnp.cumsum(g, axis=-1)
    incl = jnp.tril(jnp.ones((CHUNK, CHUNK), bool))
    strict = jnp.tril(jnp.ones((CHUNK, CHUNK), bool), -1)
    diff = G[..., :, None] - G[..., None, :]
    gamma = jnp.where(incl, jnp.exp(jnp.where(incl, diff, 0.0)), 0.0)
    k_beta = k * beta[..., None]
    a_mat = jnp.where(strict, jnp.einsum('bnhcd,bnhsd->bnhcs', k_beta, k) * gamma, 0.0) \
        + jnp.eye(CHUNK, dtype=jnp.float32)
    rhs = jnp.concatenate([v * beta[..., None], k_beta * jnp.exp(G)[..., None]], axis=-1)
    sol = lax.linalg.triangular_solve(a_mat, rhs, left_side=True, lower=True, unit_diagonal=True)
    u, w = sol[..., :DV], sol[..., DV:]
    attn = jnp.einsum('bnhcd,bnhsd->bnhcs', q, k) * gamma
    q_dec = q * jnp.exp(G)[..., None]
    k_tail = k * jnp.exp(G[..., -1:] - G)[..., None]
    chunk_dec = jnp.exp(G[..., -1])

    def step(S, inp):
        qd, w_c, u_c, at, kt, dec = inp
        v_new = u_c - jnp.einsum('bhcd,bhde->bhce', w_c, S)
        o = jnp.einsum('bhcd,bhde->bhce', qd, S) + jnp.einsum('bhcs,bhse->bhce', at, v_new)
        S = S * dec[..., None, None] + jnp.einsum('bhcd,bhce->bhde', kt, v_new)
        return S, o

    xs = tuple(jnp.moveaxis(t, 1, 0) for t in (q_dec, w, u, attn, k_tail, chunk_dec))
    S0 = jnp.zeros((B_, H, k.shape[-1], DV), jnp.float32)
    _, o = lax.scan(step, S0, xs)
    o = jnp.swapaxes(jnp.moveaxis(o, 0, 1), 2, 3).reshape(B_, Lp, H, DV)
    return o.astype(out_dtype)


def forgetting_attention(q, k, v, log_f):
    Lp = q.shape[1]
    scale = DH_B ** -0.5
    cum = jnp.swapaxes(jnp.cumsum(log_f, axis=1), 1, 2)
    pos = jnp.arange(Lp)
    outs = []
    for i in range(Lp // BLOCK):
        lo, hi = i * BLOCK, (i + 1) * BLOCK
        s = jnp.einsum('bthd,bshd->bhts', q[:, lo:hi], k[:, :hi]).astype(jnp.float32) * scale
        bias = cum[:, :, lo:hi, None] - cum[:, :, None, :hi]
        allowed = (pos[None, :hi] <= pos[lo:hi, None]) & (pos[None, :hi] >= PAD)
        p = jax.nn.softmax(jnp.where(allowed, s + bias, NEG_INF), axis=-1).astype(v.dtype)
        outs.append(jnp.einsum('bhts,bshd->bthd', p, v[:, :hi]))
    return jnp.concatenate(outs, axis=1)


def hybrid_mixer(h, w_in, conv_w, a_log, dt_bias, o_norm_a, q_norm_b, k_norm_b, f_bias,
                 w_branch, w_out):
    B_, L, _ = h.shape
    proj = h @ w_in
    qkv_a, z_a, b_a, a_a, qkv_b, f_b, gates = split_cols(proj, IN_SIZES)
    qkv_a = causal_dwconv_silu(pad_front(qkv_a), conv_w)
    Lp = qkv_a.shape[1]
    qa, ka, va = split_cols(qkv_a, (A_QK, A_QK, A_V))
    qa = l2_norm(qa.reshape(B_, Lp, H_A, DK_A)) * (DK_A ** -0.5)
    ka = l2_norm(ka.reshape(B_, Lp, H_A, DK_A))
    va = va.reshape(B_, Lp, H_A, DV_A)
    beta = pad_front(jax.nn.sigmoid(b_a))
    g = pad_front(-jnp.exp(a_log) * jax.nn.softplus(a_a + dt_bias))
    o_a = gated_delta_rule(qa, ka, va, beta, g)[:, PAD:]
    o_a = rms_norm(o_a, o_norm_a) * jax.nn.silu(z_a.reshape(B_, L, H_A, DV_A))
    qb, kb, vb = split_cols(qkv_b, (B_W, B_W, B_W))
    qb = rms_norm(qb.reshape(B_, L, H_B, DH_B), q_norm_b)
    kb = rms_norm(kb.reshape(B_, L, H_B, DH_B), k_norm_b)
    vb = vb.reshape(B_, L, H_B, DH_B)
    log_f = jax.nn.log_sigmoid((f_b + f_bias).astype(jnp.float32))
    o_b = forgetting_attention(pad_front(qb), pad_front(kb), pad_front(vb), pad_front(log_f))[:, PAD:]
    o = jnp.stack([o_a.reshape(B_, L, BRANCH_W), o_b.reshape(B_, L, BRANCH_W)], axis=2)
    y = jnp.einsum('blgc,gcd->blgd', o, w_branch)
    gate = jax.nn.sigmoid(gates.reshape(B_, L, 2, D_MODEL))
    return jnp.sum(gate * y, axis=2) @ w_out


def peer_ffn(h, wq, sub_keys, expert_u, expert_v):
    B_, L, D = h.shape
    T = B_ * L
    n_blk = -(-T // PEER_BLOCK)
    tokens = jnp.pad(h.reshape(T, D), [(0, n_blk * PEER_BLOCK - T), (0, 0)])
    tokens = tokens.reshape(n_blk, PEER_BLOCK, D)

    def one_block(xb):
        qh = (xb @ wq).reshape(PEER_BLOCK, H_P, 2, D_QP // 2)
        s1 = jnp.einsum('thd,hnd->thn', qh[:, :, 0], sub_keys[0]).astype(jnp.float32)
        s2 = jnp.einsum('thd,hnd->thn', qh[:, :, 1], sub_keys[1]).astype(jnp.float32)
        v1, i1 = lax.top_k(s1, TOPK)
        v2, i2 = lax.top_k(s2, TOPK)
        cand = (v1[..., :, None] + v2[..., None, :]).reshape(PEER_BLOCK, H_P, TOPK * TOPK)
        sc, ci = lax.top_k(cand, TOPK)
        idx = jnp.take_along_axis(i1, ci // TOPK, axis=-1) * N_KEYS \
            + jnp.take_along_axis(i2, ci % TOPK, axis=-1)
        gate_w = jax.nn.softmax(sc, axis=-1)
        act = jax.nn.gelu(jnp.einsum('thkd,td->thk', expert_u[idx], xb).astype(jnp.float32),
                          approximate=False)
        coef = (gate_w * act).astype(xb.dtype)
        return jnp.einsum('thk,thkd->td', coef, expert_v[idx])

    out = lax.map(one_block, tokens)
    return out.reshape(n_blk * PEER_BLOCK, D)[:T].reshape(B_, L, D)


def setup_inputs(seed: int = 0) -> dict:
    key = jax.random.key(seed)
    ks = jax.random.split(key, 18)

    def nrm(k, shape, scale):
        return scale * jax.random.normal(k, shape, jnp.float32)

    dt = jnp.exp(jax.random.uniform(ks[6], (DEPTH, H_A), jnp.float32,
                                    math.log(1e-3), math.log(1e-1)))
    return {
        'x': nrm(ks[0], (BATCH, SEQ, D_MODEL), 1.0),
        'meta_tokens': nrm(ks[1], (N_META, D_MODEL), 1.0),
        'norm_mix': 1.0 + nrm(ks[2], (DEPTH, D_MODEL), 0.02),
        'w_in': nrm(ks[3], (DEPTH, D_MODEL, IN_COLS), D_MODEL ** -0.5),
        'conv_w': nrm(ks[4], (DEPTH, CONV_K, CONV_CH), CONV_K ** -0.5),
        'a_log': jnp.log(jax.random.uniform(ks[5], (DEPTH, H_A), jnp.float32, 1.0, 16.0)),
        'dt_bias': dt + jnp.log(-jnp.expm1(-dt)),
        'o_norm_a': 1.0 + nrm(ks[7], (DEPTH, DV_A), 0.02),
        'q_norm_b': 1.0 + nrm(ks[8], (DEPTH, DH_B), 0.02),
        'k_norm_b': 1.0 + nrm(ks[9], (DEPTH, DH_B), 0.02),
        'f_bias': FORGET_BIAS + nrm(ks[10], (DEPTH, H_B), 0.1),
        'w_branch': nrm(ks[11], (DEPTH, 2, BRANCH_W, D_MODEL), BRANCH_W ** -0.5),
        'w_out': nrm(ks[12], (DEPTH, D_MODEL, D_MODEL), D_MODEL ** -0.5),
        'norm_ffn': 1.0 + nrm(ks[13], (DEPTH, D_MODEL), 0.02),
        'peer_wq': nrm(ks[14], (DEPTH, D_MODEL, H_P * D_QP), D_MODEL ** -0.5),
        'peer_sub_keys': nrm(ks[15], (DEPTH, 2, H_P, N_KEYS, D_QP // 2), (D_QP // 2) ** -0.5),
        'expert_u': nrm(ks[16], (DEPTH, N_EXPERTS, D_MODEL), D_MODEL ** -0.5),
        'expert_v': nrm(ks[17], (DEPTH, N_EXPERTS, D_MODEL), D_MODEL ** -0.5),
    }


def reference(x, meta_tokens, norm_mix, w_in, conv_w, a_log, dt_bias, o_norm_a, q_norm_b,
              k_norm_b, f_bias, w_branch, w_out, norm_ffn, peer_wq, peer_sub_keys,
              expert_u, expert_v):
    B_ = x.shape[0]
    meta = jnp.broadcast_to(meta_tokens.astype(x.dtype)[None], (B_, N_META, D_MODEL))
    res = jnp.concatenate([meta, x], axis=1)
    for layer in range(DEPTH):
        res = res + hybrid_mixer(rms_norm(res, norm_mix[layer]), w_in[layer], conv_w[layer],
                                 a_log[layer], dt_bias[layer], o_norm_a[layer],
                                 q_norm_b[layer], k_norm_b[layer], f_bias[layer],
                                 w_branch[layer], w_out[layer])
        res = res + peer_ffn(rms_norm(res, norm_ffn[layer]), peer_wq[layer],
                             peer_sub_keys[layer], expert_u[layer], expert_v[layer])
    return res[:, N_META:]
```

```python
import numpy as np
import ml_dtypes
from contextlib import ExitStack
import concourse.bass as bass
import concourse.mybir as mybir
from concourse.bass_utils import run_bass_kernel_spmd

F32 = mybir.dt.float32
BF16 = mybir.dt.bfloat16
U32 = mybir.dt.uint32
ALU = mybir.AluOpType
AF = mybir.ActivationFunctionType
AX = mybir.AxisListType

D = 1024
SEQ = 4096
NMETA = 16
PADN = 112
LP = 4224
NB = 33
NH = 8
EPS = 1e-6
NEXP_CH = 128
TBLK = [(i * 512, 512) for i in range(8)] + [(4096, 128)]

O_CONV, O_Z, O_BA, O_AA, O_QKVB, O_FB, O_G = 0, 3072, 4096, 4104, 4112, 7184, 7192

C_IDENT, C_TRI, C_ONES, C_MSTRICT, C_MINCL, C_IOTA, C_BD, C_PADNEG, C_P, C_OMP, C_END = (
    0, 128, 256, 384, 512, 640, 768, 896, 897, 898, 899)


class Dep:
    __slots__ = ("w", "r", "excl")

    def __init__(self, excl=False):
        self.w = {}
        self.r = {}
        self.excl = excl


class Sched:
    def __init__(self, nc, es):
        self.nc = nc
        self.eh = {'pe': nc.tensor, 'act': nc.scalar, 'dve': nc.vector, 'pool': nc.gpsimd, 'sp': nc.sync}
        self.esem = {k: es.enter_context(nc.semaphore("S_" + k)) for k in self.eh}
        self.ecnt = {k: 0 for k in self.eh}
        self.known = {k: {} for k in self.eh}
        self.rings = {}
        for q, n in (('sp', 28), ('pool', 12)):
            self.rings[q] = dict(sems=[es.enter_context(nc.semaphore(f"R_{q}{i}")) for i in range(n)],
                                 cnt=[0] * n, nxt=0)
        self.ninst = 0
        self.nwait = {k: 0 for k in self.eh}

    def _wait(self, eng, toks):
        kn = self.known[eng]
        for sem, v in toks.items():
            if kn.get(sem, 0) < v:
                self.eh[eng].wait_ge(sem, v)
                self.nwait[eng] += 1
                kn[sem] = v

    @staticmethod
    def _deps(reads, writes):
        toks = {}

        def add(d):
            for s, v in d.items():
                if toks.get(s, 0) < v:
                    toks[s] = v
        for d in reads:
            add(d.w)
        for d in writes:
            add(d.w)
            add(d.r)
        return toks

    @staticmethod
    def _finish(tok, reads, writes):
        s, v = tok
        for d in reads:
            if d.r.get(s, 0) < v:
                d.r[s] = v
        for d in writes:
            d.w = {s: v}
            d.r = {}

    def op(self, eng, fn, reads=(), writes=()):
        ex = [d for d in reads if d.excl]
        if ex:
            reads = [d for d in reads if not d.excl]
            writes = list(writes) + ex
        toks = self._deps(reads, writes)
        if eng == 'pe':
            toks.pop(self.esem['pe'], None)
        self._wait(eng, toks)
        inst = fn(self.eh[eng])
        self.ecnt[eng] += 1
        inst.then_inc(self.esem[eng], 1)
        self._finish((self.esem[eng], self.ecnt[eng]), reads, writes)
        self.ninst += 1

    def dma(self, q, out, in_, reads=(), writes=()):
        ring = self.rings[q]
        i = ring['nxt']
        ring['nxt'] = (i + 1) % len(ring['sems'])
        sem = ring['sems'][i]
        toks = self._deps(reads, writes)
        if ring['cnt'][i] > 0 and toks.get(sem, 0) < ring['cnt'][i]:
            toks[sem] = ring['cnt'][i]
        self._wait(q, toks)
        self.eh[q].dma_start(out=out, in_=in_).then_inc(sem, 16)
        ring['cnt'][i] += 16
        self._finish((sem, ring['cnt'][i]), reads, writes)
        self.ninst += 1

    def all_tokens(self):
        toks = {self.esem[k]: self.ecnt[k] for k in self.eh if self.ecnt[k] > 0}
        for ring in self.rings.values():
            for s, c in zip(ring['sems'], ring['cnt']):
                if c > 0:
                    toks[s] = c
        return toks

    def barrier(self):
        toks = self.all_tokens()
        for eng in self.eh:
            self._wait(eng, dict(toks))


class TD:
    def __init__(self, t):
        self.t = t
        self.d = Dep()


KNOB = dict(skip=0, dh=NH, dn=NB, feat=99, skipE=0, dump=1023)


def build_program(debug=False, upto=99):
    _sk = KNOB['skip']
    nc = bass.Bass("TRN2", target_bir_lowering=False)
    es = ExitStack()

    def din(name, shape, dt=F32):
        return nc.dram_tensor(name, list(shape), dt, kind="ExternalInput").ap()

    def dscr(name, shape, dt=F32):
        kind = "ExternalOutput" if (debug and name in ("s_oa", "s_obT", "s_res2", "s_qTa", "s_kTa", "s_vTa",
                                                       "s_qTb", "s_kTb")) else "Internal"
        return nc.dram_tensor(name, list(shape), dt, kind=kind).ap()

    x_in = din("x", [SEQ, D])
    meta_in = din("meta", [NMETA, D])
    nmix_in = din("nmix", [128, D])
    nffn_in = din("nffn", [128, D])
    wfm_in = din("w_fm", [D, 5120])
    wtm_in = din("w_tm", [D, 4120])
    convw_in = din("convw", [128, 24 * 4])
    hv_in = din("hvec", [128, 24])
    onw_in = din("onw", [128, 128])
    qkw_in = din("qkw", [128, 2])
    wbr_in = din("wbr", [2, D, D])
    wout_in = din("wout", [D, D])
    wq_in = din("wq", [D, 2048])
    sk_in = din("subk", [16, 128, 128])
    eu_in = din("eu", [16384, D])
    ev_in = din("ev", [16384, D])
    cf_in = din("cf32", [128, C_END])
    cm_in = din("cmask", [128, 4 * 512], BF16)
    y_out = nc.dram_tensor("y", [2048, D], F32, kind="ExternalOutput").ap()

    s_qTa = dscr("s_qTa", [NH, 128, LP])
    s_kTa = dscr("s_kTa", [NH, 128, LP])
    s_vTa = dscr("s_vTa", [NH, 128, LP])
    s_qTb = dscr("s_qTb", [NH, 128, LP], BF16)
    s_kTb = dscr("s_kTb", [NH, 128, LP], BF16)
    s_vb = dscr("s_vb", [LP, D], BF16)
    s_zs = dscr("s_zs", [LP, D])
    s_gt = dscr("s_gt", [LP, 2 * D])
    s_oa = dscr("s_oa", [LP, D])
    s_obT = dscr("s_obT", [NH, 128, LP], BF16)
    s_res2 = dscr("s_res2", [SEQ, D])
    s_r2m = dscr("s_r2m", [2048, D])
    s_UV = dscr("s_UV", [NEXP_CH, 128, 2 * D], BF16)
    s_GT = dscr("s_GT", [8, LP])
    s_NGT = dscr("s_NGT", [8, LP])
    dGT = Dep()
    s_i1T = dscr("s_i1T", [16, 128, 128])
    s_i2T = dscr("s_i2T", [16, 128, 128])
    s_gwT = dscr("s_gwT", [16, 128, 128 * 16], BF16)

    S = Sched(nc, es)
    op, dma = S.op, S.dma

    def sb(name, shape, dt=F32, stack=None):
        return TD((stack or es).enter_context(nc.sbuf_tensor("sb_" + name, list(shape), dt)))

    def ps(name, shape, dt=F32, stack=None):
        t_ = TD((stack or es).enter_context(nc.psum_tensor("ps_" + name, list(shape), dt)))
        t_.d.excl = True
        return t_

    _pad0 = sb("pad0", [128, 16])
    cf = sb("cf", [128, C_END])
    cmk = sb("cmk", [128, 4 * 512], BF16)
    identb = sb("identb", [128, 128], BF16)
    onesb = sb("onesb", [128, 128], BF16)
    hv = sb("hv", [128, 24])
    dma('sp', cf.t[:], cf_in, writes=[cf.d])
    dma('sp', cmk.t[:], cm_in, writes=[cmk.d])
    dma('sp', hv.t[:], hv_in, writes=[hv.d])
    ident = cf.t[:, C_IDENT:C_IDENT + 128]
    tri = cf.t[:, C_TRI:C_TRI + 128]
    ones = cf.t[:, C_ONES:C_ONES + 128]
    mstrict = cf.t[:, C_MSTRICT:C_MSTRICT + 128]
    mincl = cf.t[:, C_MINCL:C_MINCL + 128]
    iota = cf.t[:, C_IOTA:C_IOTA + 128]
    bdm = cf.t[:, C_BD:C_BD + 128]
    padneg = cf.t[:, C_PADNEG:C_PADNEG + 1]
    pcol = cf.t[:, C_P:C_P + 1]
    ompcol = cf.t[:, C_OMP:C_OMP + 1]
    op('dve', lambda e: e.tensor_copy(out=identb.t[:], in_=ident), reads=[cf.d], writes=[identb.d])
    op('dve', lambda e: e.tensor_copy(out=onesb.t[:], in_=ones), reads=[cf.d], writes=[onesb.d])

    beta = sb("beta", [128, NB, 8])
    nbeta = sb("nbeta", [128, NB, 8])
    gtok = sb("gtok", [128, NB, 8])
    lf = sb("lf", [128, NB, 8])

    def rsqrt_act(out, in_, scale, eps, rd, wr, post_bias=0.0):
        op('act', lambda e: e.activation(out=out, in_=in_, func=AF.Ln, bias=eps, scale=scale), reads=rd, writes=wr)
        op('act', lambda e: e.activation(out=out, in_=out, func=AF.Exp, bias=post_bias, scale=-0.5),
           reads=wr, writes=wr)

    with ExitStack() as ph:
        ub = [sb(f"e_ub{i}", [128, D], F32, ph) for i in range(2)]
        utb = [sb(f"e_utb{i}", [128, 8 * 128], BF16, ph) for i in range(2)]
        vbb = [sb(f"e_vbb{i}", [128, D], BF16, ph) for i in range(3)]
        pt = [ps(f"e_pt{i}", [128, 512], F32, ph) for i in range(4)]
        for a in range(0 if (_sk or KNOB['skipE']) else NEXP_CH):
            u = ub[a % 2]
            dma('sp', u.t[:], eu_in[a * 128:(a + 1) * 128, :], writes=[u.d])
            ut = utb[a % 2]
            for half in range(2):
                p = pt[(a * 2 + half) % 4]

                def tr4(e, p=p, u=u, half=half):
                    r = None
                    for k in range(4):
                        dc = half * 4 + k
                        r = e.transpose(out=p.t[:, k * 128:(k + 1) * 128], in_=u.t[:, dc * 128:(dc + 1) * 128],
                                        identity=ident)
                    return r
                op('pe', tr4, reads=[u.d, cf.d], writes=[p.d])
                eng = 'act' if half == 0 else 'dve'
                if eng == 'act':
                    op('act', lambda e, p=p, ut=ut, half=half: e.copy(out=ut.t[:, half * 512:(half + 1) * 512], in_=p.t[:]),
                       reads=[p.d], writes=[ut.d])
                else:
                    op('dve', lambda e, p=p, ut=ut, half=half: e.tensor_copy(out=ut.t[:, half * 512:(half + 1) * 512], in_=p.t[:]),
                       reads=[p.d], writes=[ut.d])
            dma('sp', s_UV[a][:, 0:D], ut.t[:], reads=[ut.d])
            v = vbb[a % 3]
            dma('pool', v.t[:], ev_in[a * 128:(a + 1) * 128, :], writes=[v.d])
            dma('sp', s_UV[a][:, D:2 * D], v.t[:], reads=[v.d])
        S.barrier()
        if upto <= 0:
            return nc, S, es

    ph1 = ExitStack()
    hT = sb("hT", [128, 8, LP], BF16, ph1)
    hTd = [Dep() for _ in range(NB)]
    with ExitStack() as ph:
        nmix = sb("nmix", [128, D], F32, ph)
        dma('sp', nmix.t[:], nmix_in, writes=[nmix.d])
        xt = [sb(f"xt{i}", [128, D], F32, ph) for i in range(2)]
        junk = [sb(f"junk{i}", [128, D], F32, ph) for i in range(2)]
        hn = [sb(f"hn{i}", [128, D], BF16, ph) for i in range(2)]
        ssq = [sb(f"ssq{i}", [128, 1], F32, ph) for i in range(2)]
        ptr = [ps(f"ptr{i}", [128, 1024], BF16, ph) for i in range(2)]
        op('dve', lambda e: e.memset(xt[0].t[:], 0.0), writes=[xt[0].d])
        for i in range(0 if _sk else NB):
            x_ = xt[i % 2]
            if i == 0:
                dma('sp', x_.t[PADN:128, :], meta_in, writes=[x_.d])
            else:
                dma('sp', x_.t[:], x_in[(i - 1) * 128:i * 128, :], writes=[x_.d])
            sq = ssq[i % 2]
            jk = junk[i % 2]
            op('act', lambda e, x_=x_, jk=jk, sq=sq: e.activation(out=jk.t[:], in_=x_.t[:], func=AF.Square, accum_out=sq.t[:]),
               reads=[x_.d], writes=[jk.d, sq.d])
            rsqrt_act(sq.t[:], sq.t[:], 1.0 / D, EPS, [sq.d], [sq.d])
            h_ = hn[i % 2]
            op('dve', lambda e, x_=x_, sq=sq, h_=h_: e.scalar_tensor_tensor(out=h_.t[:], in0=x_.t[:], scalar=sq.t[:, 0:1], in1=nmix.t[:],
                                                                       op0=ALU.mult, op1=ALU.mult),
               reads=[x_.d, sq.d, nmix.d], writes=[h_.d])
            p = ptr[i % 2]

            def tr8(e, p=p, h_=h_):
                r = None
                for dc in range(8):
                    r = e.transpose(out=p.t[:, dc * 128:(dc + 1) * 128], in_=h_.t[:, dc * 128:(dc + 1) * 128],
                                    identity=identb.t[:])
                return r
            op('pe', tr8, reads=[h_.d, identb.d], writes=[p.d])
            op('act', lambda e, p=p, i=i: e.copy(out=hT.t[:, :, i * 128:(i + 1) * 128],
                                                 in_=p.t[:].rearrange("p (c t) -> p c t", c=8)),
               reads=[p.d], writes=[hTd[i]])
        S.barrier()
        if upto <= 1:
            return nc, S, es

    with ExitStack() as ph:
        wtm = sb("wtm", [128, 8, 4120], BF16, ph)
        wtmd = [Dep() for _ in range(9)]
        for c in range(9):
            c0 = c * 512
            cw = min(512, 4120 - c0)
            dma('pool', wtm.t[:, :, c0:c0 + cw],
                wtm_in[:, c0:c0 + cw].rearrange("(dc p) c -> p dc c", p=128), writes=[wtmd[c]])
        pp = [ps(f"pp{i}", [128, 512], F32, ph) for i in range(4)]
        ob = [sb(f"ob{i}", [128, 512], F32, ph) for i in range(4)]
        obb = [sb(f"obb{i}", [128, 512], BF16, ph) for i in range(2)]
        sm = sb("sm", [128, 24], F32, ph)
        nea = sb("nea", [128, 8], F32, ph)
        op('act', lambda e: e.activation(out=nea.t[:], in_=hv.t[:, 0:8], func=AF.Exp), reads=[hv.d], writes=[nea.d])
        op('dve', lambda e: e.tensor_scalar(out=nea.t[:], in0=nea.t[:], scalar1=-1.0, scalar2=None, op0=ALU.mult),
           reads=[nea.d], writes=[nea.d])
        k = 0
        for i in range(0 if _sk else NB):
            for c in range(9):
                c0 = c * 512
                cw = min(512, 4120 - c0)
                p = pp[k % 4]

                def mm(e, p=p, i=i, c0=c0, cw=cw):
                    r = None
                    for dc in range(8):
                        r = e.matmul(p.t[:, 0:cw], lhsT=hT.t[:, dc, i * 128:(i + 1) * 128], rhs=wtm.t[:, dc, c0:c0 + cw],
                                     start=(dc == 0), stop=(dc == 7))
                    return r
                op('pe', mm, reads=[hTd[i], wtmd[c]], writes=[p.d])
                rows = slice(i * 128, (i + 1) * 128)
                if c < 2:
                    o = obb[k % 2]
                    op('act', lambda e, o=o, p=p: e.copy(out=o.t[:], in_=p.t[:]), reads=[p.d], writes=[o.d])
                    dma('sp', s_vb[rows, c0:c0 + 512], o.t[:], reads=[o.d])
                elif c < 4:
                    o = ob[k % 4]
                    op('act', lambda e, o=o, p=p: e.activation(out=o.t[:], in_=p.t[:], func=AF.Silu), reads=[p.d], writes=[o.d])
                    dma('sp', s_zs[rows, c0 - 1024:c0 - 1024 + 512], o.t[:], reads=[o.d])
                elif c < 8:
                    o = ob[k % 4]
                    op('act', lambda e, o=o, p=p: e.activation(out=o.t[:], in_=p.t[:], func=AF.Sigmoid), reads=[p.d], writes=[o.d])
                    dma('sp', s_gt[rows, c0 - 2048:c0 - 2048 + 512], o.t[:], reads=[o.d])
                else:
                    op('act', lambda e, p=p, i=i: e.activation(out=beta.t[:, i, :], in_=p.t[:, 0:8], func=AF.Sigmoid),
                       reads=[p.d], writes=[beta.d])
                    op('dve', lambda e, p=p: e.tensor_tensor(out=sm.t[:, 8:24], in0=p.t[:, 8:24], in1=hv.t[:, 8:24], op=ALU.add),
                       reads=[p.d, hv.d], writes=[sm.d])
                    op('act', lambda e: e.activation(out=sm.t[:, 8:16], in_=sm.t[:, 8:16], func=AF.Exp), reads=[sm.d], writes=[sm.d])
                    op('act', lambda e: e.activation(out=sm.t[:, 16:24], in_=sm.t[:, 16:24], func=AF.Exp, scale=-1.0),
                       reads=[sm.d], writes=[sm.d])
                    op('act', lambda e: e.activation(out=sm.t[:, 8:24], in_=sm.t[:, 8:24], func=AF.Ln, bias=1.0),
                       reads=[sm.d], writes=[sm.d])
                    op('dve', lambda e, i=i: e.tensor_scalar(out=lf.t[:, i, :], in0=sm.t[:, 16:24], scalar1=-1.0, scalar2=None, op0=ALU.mult),
                       reads=[sm.d], writes=[lf.d])
                    op('dve', lambda e, i=i: e.tensor_tensor(out=gtok.t[:, i, :], in0=sm.t[:, 8:16], in1=nea.t[:], op=ALU.mult),
                       reads=[sm.d, nea.d], writes=[gtok.d])
                    op('dve', lambda e, i=i: e.tensor_scalar(out=nbeta.t[:, i, :], in0=beta.t[:, i, :], scalar1=-1.0, scalar2=None, op0=ALU.mult),
                       reads=[beta.d], writes=[nbeta.d])
                k += 1
        S.barrier()
        if upto <= 2:
            return nc, S, es
    with ExitStack() as ph:
        convw = sb("convw", [128, 96], F32, ph)
        qkw = sb("qkw", [128, 2], F32, ph)
        dma('sp', convw.t[:], convw_in, writes=[convw.d])
        dma('sp', qkw.t[:], qkw_in, writes=[qkw.d])
        wsl = [sb(f"wsl{i}", [128, 8, 128], BF16, ph) for i in range(2)]
        raw = [sb(f"raw{i}", [128, LP], F32, ph) for i in range(2)]
        acc = sb("acc", [128, LP], F32, ph)
        sqb = sb("sqb", [128, LP], F32, ph)
        rsb = sb("rsb", [128, LP], F32, ph)
        outb = sb("outb", [128, LP], BF16, ph)
        pp = [ps(f"fp{i}", [128, 512], F32, ph) for i in range(4)]
        pn = [ps(f"fn{i}", [128, 512], F32, ph) for i in range(2)]
        k = 0
        for g in range(0 if _sk else 40):
            w = wsl[g % 2]
            dma('pool', w.t[:], wfm_in[:, g * 128:(g + 1) * 128].rearrange("(dc p) c -> p dc c", p=128), writes=[w.d])
            r_ = raw[g % 2]
            for (t0, tw) in TBLK:
                p = pp[k % 4]
                k += 1

                def mm(e, p=p, w=w, t0=t0, tw=tw):
                    r = None
                    for dc in range(8):
                        r = e.matmul(p.t[:, 0:tw], lhsT=w.t[:, dc, :], rhs=hT.t[:, dc, t0:t0 + tw], start=(dc == 0), stop=(dc == 7))
                    return r
                op('pe', mm, reads=[w.d] + hTd, writes=[p.d])
                op('act', lambda e, p=p, r_=r_, t0=t0, tw=tw: e.copy(out=r_.t[:, t0:t0 + tw], in_=p.t[:, 0:tw]),
                   reads=[p.d], writes=[r_.d])
            if g < 24:
                cw = lambda i: convw.t[:, g * 4 + i:g * 4 + i + 1]
                op('dve', lambda e, r_=r_, c=cw(3): e.tensor_scalar(out=acc.t[:], in0=r_.t[:], scalar1=c, scalar2=None, op0=ALU.mult),
                   reads=[r_.d, convw.d], writes=[acc.d])
                for s_ in (1, 2, 3):
                    op('dve', lambda e, r_=r_, s_=s_, c=cw(3 - s_): e.scalar_tensor_tensor(
                        out=acc.t[:, s_:], in0=r_.t[:, 0:LP - s_], scalar=c, in1=acc.t[:, s_:], op0=ALU.mult, op1=ALU.add),
                       reads=[r_.d, convw.d], writes=[acc.d])
                op('act', lambda e: e.activation(out=acc.t[:], in_=acc.t[:], func=AF.Silu), reads=[acc.d], writes=[acc.d])
                src = acc
            else:
                src = r_
            if g >= 16 and g < 24:
                dma('sp', s_vTa[g - 16], acc.t[:], reads=[acc.d])
                continue
            op('pool', lambda e, src=src: e.tensor_tensor(out=sqb.t[:], in0=src.t[:], in1=src.t[:], op=ALU.mult),
               reads=[src.d], writes=[sqb.d])
            for bi, (t0, tw) in enumerate(TBLK):
                p = pn[bi % 2]
                op('pe', lambda e, p=p, t0=t0, tw=tw: e.matmul(p.t[:, 0:tw], lhsT=ones, rhs=sqb.t[:, t0:t0 + tw], start=True, stop=True),
                   reads=[sqb.d, cf.d], writes=[p.d])
                if g < 8:
                    rsqrt_act(rsb.t[:, t0:t0 + tw], p.t[:, 0:tw], 1.0, EPS, [p.d], [rsb.d], post_bias=float(np.log(128.0 ** -0.5)))
                elif g < 16:
                    rsqrt_act(rsb.t[:, t0:t0 + tw], p.t[:, 0:tw], 1.0, EPS, [p.d], [rsb.d])
                else:
                    rsqrt_act(rsb.t[:, t0:t0 + tw], p.t[:, 0:tw], 1.0 / 128.0, EPS, [p.d], [rsb.d])
            if g < 16:
                op('dve', lambda e: e.tensor_tensor(out=sqb.t[:], in0=acc.t[:], in1=rsb.t[:], op=ALU.mult),
                   reads=[acc.d, rsb.d], writes=[sqb.d])
                dst = s_qTa[g] if g < 8 else s_kTa[g - 8]
                dma('sp', dst, sqb.t[:], reads=[sqb.d])
            else:
                j = 0 if g < 32 else 1
                op('dve', lambda e, r_=r_, j=j: e.scalar_tensor_tensor(out=outb.t[:], in0=r_.t[:], scalar=qkw.t[:, j:j + 1], in1=rsb.t[:],
                                                                      op0=ALU.mult, op1=ALU.mult),
                   reads=[r_.d, rsb.d, qkw.d], writes=[outb.d])
                dst = s_qTb[g - 24] if g < 32 else s_kTb[g - 32]
                dma('sp', dst, outb.t[:], reads=[outb.d])
        S.barrier()
        if upto <= 3:
            return nc, S, es
    ph1.close()

    Gc = sb("Gc", [128, NB, 8])
    Gl = sb("Gl", [128, NB, 8])
    eG = sb("eG", [128, NB, 8])
    negeG = sb("negeG", [128, NB, 8])
    etail = sb("etail", [128, NB, 8])
    eGl = sb("eGl", [128, NB, 8])
    cumT = sb("cumT", [128, NB, 8])
    carry = sb("carry", [128, NB + 1, 8])
    with ExitStack() as ph:
        pw = [ps(f"pw{i}", [128, 512], F32, ph) for i in range(2)]
        op('dve', lambda e: e.memset(carry.t[:, 0, :], 0.0), writes=[carry.d])
        for b in range(NB):
            p = pw[b % 2]

            def mm(e, p=p, b=b):
                e.matmul(p.t[:, 0:8], lhsT=tri, rhs=gtok.t[:, b, :], start=True, stop=True)
                e.matmul(p.t[:, 8:16], lhsT=ones, rhs=gtok.t[:, b, :], start=True, stop=True)
                e.matmul(p.t[:, 16:24], lhsT=tri, rhs=lf.t[:, b, :], start=True, stop=True)
                return e.matmul(p.t[:, 24:32], lhsT=ones, rhs=lf.t[:, b, :], start=True, stop=True)
            op('pe', mm, reads=[gtok.d, lf.d, cf.d], writes=[p.d])
            op('act', lambda e, p=p, b=b: e.copy(out=Gc.t[:, b, :], in_=p.t[:, 0:8]), reads=[p.d], writes=[Gc.d])
            op('act', lambda e, p=p, b=b: e.copy(out=Gl.t[:, b, :], in_=p.t[:, 8:16]), reads=[p.d], writes=[Gl.d])
            op('dve', lambda e, p=p, b=b: e.tensor_tensor(out=cumT.t[:, b, :], in0=p.t[:, 16:24], in1=carry.t[:, b, :], op=ALU.add),
               reads=[p.d, carry.d], writes=[cumT.d])
            op('dve', lambda e, p=p, b=b: e.tensor_tensor(out=carry.t[:, b + 1, :], in0=p.t[:, 24:32], in1=carry.t[:, b, :], op=ALU.add),
               reads=[p.d, carry.d], writes=[carry.d])
        op('act', lambda e: e.activation(out=eG.t[:], in_=Gc.t[:], func=AF.Exp), reads=[Gc.d], writes=[eG.d])
        op('dve', lambda e: e.tensor_scalar(out=negeG.t[:], in0=eG.t[:], scalar1=-1.0, scalar2=None, op0=ALU.mult),
           reads=[eG.d], writes=[negeG.d])
        op('dve', lambda e: e.tensor_tensor(out=etail.t[:], in0=Gl.t[:], in1=Gc.t[:], op=ALU.subtract),
           reads=[Gl.d, Gc.d], writes=[etail.d])
        op('act', lambda e: e.activation(out=etail.t[:], in_=etail.t[:], func=AF.Exp), reads=[etail.d], writes=[etail.d])
        op('act', lambda e: e.activation(out=eGl.t[:], in_=Gl.t[:], func=AF.Exp), reads=[Gl.d], writes=[eGl.d])
        if debug:
            dbg_small = nc.dram_tensor("dbg_small", [10, 128, NB * 8], F32, kind="ExternalOutput").ap()
            for ii, tt in enumerate([beta, nbeta, gtok, lf, Gc, Gl, eG, etail, eGl, cumT]):
                if not (KNOB['dump'] >> ii) & 1:
                    continue
                dma('sp', dbg_small[ii], tt.t[:].rearrange("p b h -> p (b h)"), reads=[tt.d])
        S.barrier()
        if upto <= 4:
            return nc, S, es

    with ExitStack() as ph:
        GH = 4
        onw = sb("onw", [128, 128], F32, ph)
        dma('sp', onw.t[:], onw_in, writes=[onw.d])
        pbank = [ps(f"dp{i}", [128, 512], F32, ph) for i in range(2 * GH)]
        names = ["E", "EmS", "EmI", "X0", "X1", "Y0", "Y1", "P0", "P1", "attnT", "Ktail", "Vtok", "R", "vnew", "tmp", "o", "zs", "oa"]
        HS = []
        for gi in range(GH):
            HS.append(dict(
                KT=[sb(f"dKT{gi}_{p_}", [128, 128], F32, ph) for p_ in range(2)],
                QT=[sb(f"dQT{gi}_{p_}", [128, 128], F32, ph) for p_ in range(2)],
                VT=[sb(f"dVT{gi}_{p_}", [128, 128], F32, ph) for p_ in range(2)],
                S=sb(f"dS{gi}", [128, 128], F32, ph),
                W=[{n_: sb(f"dw{gi}_{p_}_{n_}", [128, 128], F32, ph) for n_ in names} for p_ in range(2)],
                ssq=[sb(f"dssq{gi}_{p_}", [128, 1], F32, ph) for p_ in range(2)],
                slots=[(pbank[2 * gi + q_].t[:, 0:256], pbank[2 * gi + q_].d) for q_ in range(2)], si=[0]))

        def mm(out, od, lhsT, rhs, rd):
            op('pe', lambda e: e.matmul(out, lhsT=lhsT, rhs=rhs, start=True, stop=True), reads=rd, writes=[od])

        def head_gen(h, hs):
            Sst = hs['S']

            def slot():
                s_ = hs['slots'][hs['si'][0] % 2]
                hs['si'][0] += 1
                return s_
            op('dve', lambda e: e.memset(Sst.t[:], 0.0), writes=[Sst.d])
            yield
            for n in range(NB):
                par = n % 2
                w = hs['W'][par]
                KT, QT, VT = hs['KT'][par], hs['QT'][par], hs['VT'][par]
                bs = slice(n * 128, (n + 1) * 128)
                dma('sp', KT.t[:], s_kTa[h][:, bs], writes=[KT.d])
                dma('sp', QT.t[:], s_qTa[h][:, bs], writes=[QT.d])
                dma('sp', VT.t[:], s_vTa[h][:, bs], writes=[VT.d])
                if n >= 1:
                    dma('sp', w["zs"].t[:], s_zs[bs, h * 128:(h + 1) * 128], writes=[w["zs"].d])
                col = lambda T: T.t[:, n, h:h + 1]
                op('pool', lambda e: e.tensor_scalar(out=w["tmp"].t[:], in0=ident, scalar1=col(Gc), scalar2=None, op0=ALU.mult),
                   reads=[cf.d, Gc.d], writes=[w["tmp"].d])
                pk, pkd = slot()

                def mkk(e):
                    e.matmul(pk[:, 0:128], lhsT=KT.t[:], rhs=KT.t[:], start=True, stop=True)
                    return e.matmul(pk[:, 128:256], lhsT=KT.t[:], rhs=QT.t[:], start=True, stop=True)
                op('pe', mkk, reads=[KT.d, QT.d], writes=[pkd])
                pd_, pdd = slot()
                mm(pd_[:, 0:128], pdd, ones, w["tmp"].t[:], [cf.d, w["tmp"].d])
                yield
                op('dve', lambda e: e.tensor_scalar(out=w["E"].t[:], in0=pd_[:, 0:128], scalar1=col(Gc), scalar2=0.0,
                                                   op0=ALU.subtract, op1=ALU.min), reads=[pdd, Gc.d], writes=[w["E"].d])
                op('act', lambda e: e.activation(out=w["E"].t[:], in_=w["E"].t[:], func=AF.Exp), reads=[w["E"].d], writes=[w["E"].d])
                op('dve', lambda e: e.tensor_scalar(out=w["R"].t[:], in0=pk[:, 0:128], scalar1=col(beta), scalar2=-1.0, op0=ALU.mult, op1=ALU.mult),
                   reads=[pkd, beta.d], writes=[w["R"].d])
                yield
                op('pool', lambda e: e.tensor_tensor(out=w["EmS"].t[:], in0=w["E"].t[:], in1=mstrict, op=ALU.mult),
                   reads=[w["E"].d, cf.d], writes=[w["EmS"].d])
                op('pool', lambda e: e.tensor_tensor(out=w["EmI"].t[:], in0=w["E"].t[:], in1=mincl, op=ALU.mult),
                   reads=[w["E"].d, cf.d], writes=[w["EmI"].d])
                op('pool', lambda e: e.tensor_tensor(out=w["X0"].t[:], in0=w["R"].t[:], in1=w["EmS"].t[:], op=ALU.mult),
                   reads=[w["R"].d, w["EmS"].d], writes=[w["X0"].d])
                yield
                op('dve', lambda e: e.tensor_tensor(out=w["attnT"].t[:], in0=pk[:, 128:256], in1=w["EmI"].t[:], op=ALU.mult),
                   reads=[pkd, w["EmI"].d], writes=[w["attnT"].d])
                py, pyd = slot()
                mm(py[:, 0:128], pyd, w["X0"].t[:], ident, [w["X0"].d, cf.d])
                op('pool', lambda e: e.tensor_tensor(out=w["P0"].t[:], in0=w["X0"].t[:], in1=ident, op=ALU.add),
                   reads=[w["X0"].d, cf.d], writes=[w["P0"].d])
                yield
                op('act', lambda e: e.copy(out=w["Y0"].t[:], in_=py[:, 0:128]), reads=[pyd], writes=[w["Y0"].d])
                pt_, ptd = slot()

                def mtr(e):
                    e.matmul(pt_[:, 0:128], lhsT=KT.t[:], rhs=ident, start=True, stop=True)
                    return e.matmul(pt_[:, 128:256], lhsT=VT.t[:], rhs=ident, start=True, stop=True)
                op('pe', mtr, reads=[KT.d, VT.d, cf.d], writes=[ptd])
                yield
                op('dve', lambda e: e.tensor_scalar(out=w["Ktail"].t[:], in0=pt_[:, 0:128], scalar1=col(etail), scalar2=None, op0=ALU.mult),
                   reads=[ptd, etail.d], writes=[w["Ktail"].d])
                op('dve', lambda e: e.tensor_copy(out=w["Vtok"].t[:], in_=pt_[:, 128:256]), reads=[ptd], writes=[w["Vtok"].d])
                yield
                Xc, Yc, Pc = w["X0"], w["Y0"], w["P0"]
                for j in range(1, 7):
                    Xn, Yn, Pn = w[f"X{j % 2}"], w[f"Y{j % 2}"], w[f"P{j % 2}"]
                    p1, p1d = slot()
                    mm(p1[:, 0:128], p1d, Xc.t[:], Yc.t[:], [Xc.d, Yc.d])
                    if j <= 5:
                        mm(p1[:, 128:256], p1d, Yc.t[:], Xc.t[:], [Xc.d, Yc.d])
                    yield
                    op('act', lambda e: e.copy(out=Yn.t[:], in_=p1[:, 0:128]), reads=[p1d], writes=[Yn.d])
                    if j <= 5:
                        op('dve', lambda e: e.tensor_copy(out=Xn.t[:], in_=p1[:, 128:256]), reads=[p1d], writes=[Xn.d])
                    p2, p2d = slot()
                    mm(p2[:, 0:128], p2d, Yn.t[:], Pc.t[:], [Yn.d, Pc.d])
                    yield
                    op('dve', lambda e: e.tensor_tensor(out=Pn.t[:], in0=p2[:, 0:128], in1=Pc.t[:], op=ALU.add),
                       reads=[p2d, Pc.d], writes=[Pn.d])
                    Xc, Yc, Pc = Xn, Yn, Pn
                TT = Pc
                pr, prd = slot()

                def mks(e):
                    e.matmul(pr[:, 0:128], lhsT=KT.t[:], rhs=Sst.t[:], start=True, stop=True)
                    return e.matmul(pr[:, 128:256], lhsT=QT.t[:], rhs=Sst.t[:], start=True, stop=True)
                op('pe', mks, reads=[KT.d, QT.d, Sst.d], writes=[prd])
                yield
                op('dve', lambda e: e.scalar_tensor_tensor(out=w["R"].t[:], in0=pr[:, 0:128], scalar=col(negeG), in1=w["Vtok"].t[:],
                                                          op0=ALU.mult, op1=ALU.add),
                   reads=[prd, negeG.d, w["Vtok"].d], writes=[w["R"].d])
                if n >= 1:
                    op('act', lambda e: e.activation(out=w["tmp"].t[:], in_=pr[:, 128:256], func=AF.Copy, scale=col(eG)),
                       reads=[prd, eG.d], writes=[w["tmp"].d])
                pv, pvd = slot()
                mm(pv[:, 0:128], pvd, TT.t[:], w["R"].t[:], [TT.d, w["R"].d])
                yield
                op('dve', lambda e: e.tensor_scalar(out=w["vnew"].t[:], in0=pv[:, 0:128], scalar1=col(beta), scalar2=None, op0=ALU.mult),
                   reads=[pvd, beta.d], writes=[w["vnew"].d])
                pu, pud = slot()
                mm(pu[:, 0:128], pud, w["Ktail"].t[:], w["vnew"].t[:], [w["Ktail"].d, w["vnew"].d])
                if n >= 1:
                    mm(pu[:, 128:256], pud, w["attnT"].t[:], w["vnew"].t[:], [w["attnT"].d, w["vnew"].d])
                yield
                op('dve', lambda e: e.scalar_tensor_tensor(out=Sst.t[:], in0=Sst.t[:], scalar=col(eGl), in1=pu[:, 0:128],
                                                          op0=ALU.mult, op1=ALU.add),
                   reads=[pud, eGl.d, Sst.d], writes=[Sst.d])
                if n >= 1:
                    op('dve', lambda e: e.tensor_tensor(out=w["o"].t[:], in0=pu[:, 128:256], in1=w["tmp"].t[:], op=ALU.add),
                       reads=[pud, w["tmp"].d], writes=[w["o"].d])
                    sq = hs['ssq'][par]
                    op('act', lambda e: e.activation(out=w["tmp"].t[:], in_=w["o"].t[:], func=AF.Square, accum_out=sq.t[:]),
                       reads=[w["o"].d], writes=[w["tmp"].d, sq.d])
                    yield
                    rsqrt_act(sq.t[:], sq.t[:], 1.0 / 128.0, EPS, [sq.d], [sq.d])
                    yield
                    op('dve', lambda e: e.scalar_tensor_tensor(out=w["oa"].t[:], in0=w["o"].t[:], scalar=sq.t[:, 0:1], in1=onw.t[:],
                                                              op0=ALU.mult, op1=ALU.mult),
                       reads=[w["o"].d, sq.d, onw.d], writes=[w["oa"].d])
                    op('pool', lambda e: e.tensor_tensor(out=w["oa"].t[:], in0=w["oa"].t[:], in1=w["zs"].t[:], op=ALU.mult),
                       reads=[w["oa"].d, w["zs"].d], writes=[w["oa"].d])
                    dma('sp', s_oa[bs, h * 128:(h + 1) * 128], w["oa"].t[:], reads=[w["oa"].d])
                yield

        for g0 in range(0, NH, GH):
            gens = [head_gen(g0 + gi, HS[gi]) for gi in range(GH)]
            while gens:
                for g_ in list(gens):
                    try:
                        next(g_)
                    except StopIteration:
                        gens.remove(g_)
        S.barrier()
        if upto <= 5:
            return nc, S, es
    with ExitStack() as ph:
        QBs = [sb(f"QB{i}", [128, LP], BF16, ph) for i in range(2)]
        KBs = [sb(f"KB{i}", [128, LP], BF16, ph) for i in range(2)]
        VBts = [sb(f"VBt{i}", [128, NB, 128], BF16, ph) for i in range(2)]
        biasT = [sb(f"biasT{i}", [128, NB + 3], F32, ph) for i in range(2)]
        pS = [ps(f"aS{i}", [128, 512], F32, ph) for i in range(3)]
        pO = [ps(f"aO{i}", [128, 512], F32, ph) for i in range(2)]
        pZ = [ps(f"aZ{i}", [128, 512], F32, ph) for i in range(2)]
        PT = [sb(f"PT{i}", [128, 512], BF16, ph) for i in range(4)]
        rec = [sb(f"rec{i}", [128, 512], F32, ph) for i in range(2)]
        obt = [sb(f"obt{i}", [128, 512], BF16, ph) for i in range(2)]
        sc_ = float(128.0 ** -0.5)
        steps = [(h, r, j) for h in range(NH) for r in range(8) for j in range(5 + 4 * r)]
        loaded = set()

        def ensure_head(h):
            if h in loaded:
                return
            loaded.add(h)
            dma('sp', QBs[h % 2].t[:], s_qTb[h], writes=[QBs[h % 2].d])
            dma('sp', KBs[h % 2].t[:], s_kTb[h], writes=[KBs[h % 2].d])
            dma('sp', VBts[h % 2].t[:], s_vb[:, h * 128:(h + 1) * 128].rearrange("(b p) e -> p b e", p=128), writes=[VBts[h % 2].d])

        def emit_S(i):
            h, r, j = steps[i]
            ensure_head(h)
            t0 = 128 * (1 + 4 * r)
            pS_ = pS[i % 3]
            QB, KB = QBs[h % 2], KBs[h % 2]
            op('pe', lambda e: e.matmul(pS_.t[:], lhsT=KB.t[:, j * 128:(j + 1) * 128], rhs=QB.t[:, t0:t0 + 512], start=True, stop=True),
               reads=[KB.d, QB.d], writes=[pS_.d])

        def emit_rest(i):
            h, r, j = steps[i]
            jq = 1 + 4 * r
            t0 = 128 * jq
            nkb = jq + 4
            rr = (h * 8 + r) % 2
            bt = biasT[rr]
            if j == 0:
                op('dve', lambda e: e.tensor_scalar(out=bt.t[:, 0:nkb], in0=cumT.t[:, 0:nkb, h], scalar1=carry.t[:, jq + 2, h:h + 1], scalar2=-1.0,
                                                   op0=ALU.subtract, op1=ALU.mult), reads=[cumT.d, carry.d], writes=[bt.d])
                op('dve', lambda e: e.tensor_tensor(out=bt.t[:, 0:1], in0=bt.t[:, 0:1], in1=padneg, op=ALU.add),
                   reads=[bt.d, cf.d], writes=[bt.d])
            pS_ = pS[i % 3]
            PT_ = PT[i % 4]
            pO_, pZ_ = pO[rr], pZ[rr]
            VBt = VBts[h % 2]
            op('act', lambda e: e.activation(out=PT_.t[:], in_=pS_.t[:], func=AF.Exp, bias=bt.t[:, j:j + 1], scale=sc_),
               reads=[pS_.d, bt.d], writes=[PT_.d])
            if j >= jq:
                q_ = j - jq
                op('dve', lambda e: e.tensor_tensor(out=PT_.t[:], in0=PT_.t[:], in1=cmk.t[:, q_ * 512:(q_ + 1) * 512], op=ALU.mult),
                   reads=[PT_.d, cmk.d], writes=[PT_.d])

            def mpv(e):
                e.matmul(pO_.t[:], lhsT=VBt.t[:, j, :], rhs=PT_.t[:], start=(j == 0), stop=(j == nkb - 1))
                return e.matmul(pZ_.t[:], lhsT=onesb.t[:], rhs=PT_.t[:], start=(j == 0), stop=(j == nkb - 1))
            op('pe', mpv, reads=[PT_.d, VBt.d, onesb.d], writes=[pO_.d, pZ_.d])
            if j == nkb - 1:
                rc = rec[rr]
                ob_ = obt[rr]
                op('dve', lambda e: e.reciprocal(out=rc.t[:], in_=pZ_.t[:]), reads=[pZ_.d], writes=[rc.d])
                op('dve', lambda e: e.tensor_tensor(out=ob_.t[:], in0=pO_.t[:], in1=rc.t[:], op=ALU.mult),
                   reads=[pO_.d, rc.d], writes=[ob_.d])
                dma('sp', s_obT[h][:, t0:t0 + 512], ob_.t[:], reads=[ob_.d])

        LA = 2
        for i in range(LA):
            emit_S(i)
        for i in range(len(steps)):
            if i + LA < len(steps):
                emit_S(i + LA)
            emit_rest(i)
        S.barrier()
        if upto <= 6:
            return nc, S, es

    with ExitStack() as ph:
        wba = sb("wba", [128, 8, D], BF16, ph)
        wbb = sb("wbb", [128, 8, D], BF16, ph)
        wo = sb("wo", [128, 8, D], BF16, ph)
        for hh in range(2):
            cs = slice(hh * 512, (hh + 1) * 512)
            dma('pool', wba.t[:, :, cs], wbr_in[0][:, cs].rearrange("(h p) c -> p h c", p=128), writes=[wba.d])
            dma('pool', wbb.t[:, :, cs], wbr_in[1][:, cs].rearrange("(h p) c -> p h c", p=128), writes=[wbb.d])
            dma('pool', wo.t[:, :, cs], wout_in[:, cs].rearrange("(h p) c -> p h c", p=128), writes=[wo.d])
        oat = [sb(f"oat{i}", [128, D], F32, ph) for i in range(2)]
        oab = [sb(f"oab{i}", [128, D], BF16, ph) for i in range(2)]
        oaT = [sb(f"oaT{i}", [128, 8, 128], BF16, ph) for i in range(2)]
        obT = [sb(f"obT{i}", [128, 8, 128], BF16, ph) for i in range(2)]
        gt = [sb(f"gt{i}", [128, 2 * D], F32, ph) for i in range(2)]
        xr = [sb(f"xr{i}", [128, D], F32, ph) for i in range(2)]
        t1 = [sb(f"t1_{i}", [128, 512], F32, ph) for i in range(2)]
        t2 = [sb(f"t2_{i}", [128, 512], F32, ph) for i in range(2)]
        mb = [sb(f"mb{i}", [128, D], BF16, ph) for i in range(2)]
        mT = [sb(f"mT{i}", [128, 8, 128], BF16, ph) for i in range(2)]
        r2 = [sb(f"r2_{i}", [128, D], F32, ph) for i in range(2)]
        ptr = ps("mptr", [128, 1024], BF16, ph)
        pya = [ps(f"pya{i}", [128, 512], F32, ph) for i in range(2)]
        pyb = [ps(f"pyb{i}", [128, 512], F32, ph) for i in range(2)]
        pmx = [ps(f"pmx{i}", [128, 512], F32, ph) for i in range(2)]
        ptr2 = ps("mptr2", [128, 1024], BF16, ph)
        def mstage_A(i):
            b_ = i % 2
            rows = slice(i * 128, (i + 1) * 128)
            dma('sp', oat[b_].t[:], s_oa[rows, :], writes=[oat[b_].d])
            dma('sp', obT[b_].t[:], s_obT[:, :, i * 128:(i + 1) * 128].rearrange("h e t -> e h t"), writes=[obT[b_].d])
            dma('sp', gt[b_].t[:], s_gt[rows, :], writes=[gt[b_].d])
            dma('sp', xr[b_].t[:], x_in[(i - 1) * 128:i * 128, :], writes=[xr[b_].d])
            op('act', lambda e, b_=b_: e.copy(out=oab[b_].t[:], in_=oat[b_].t[:]), reads=[oat[b_].d], writes=[oab[b_].d])

            def tr8(e, src):
                r = None
                for dc in range(8):
                    r = e.transpose(out=ptr.t[:, dc * 128:(dc + 1) * 128], in_=src.t[:, dc * 128:(dc + 1) * 128], identity=identb.t[:])
                return r
            op('pe', lambda e, b_=b_: tr8(e, oab[b_]), reads=[oab[b_].d, identb.d], writes=[ptr.d])
            op('dve', lambda e, b_=b_: e.tensor_copy(out=oaT[b_].t[:], in_=ptr.t[:].rearrange("p (c t) -> p c t", c=8)),
               reads=[ptr.d], writes=[oaT[b_].d])
            for hf in range(2):
                cs = slice(hf * 512, (hf + 1) * 512)

                def mmy(e, hf=hf, cs=cs, b_=b_):
                    r = None
                    for h in range(8):
                        e.matmul(pya[hf].t[:], lhsT=oaT[b_].t[:, h, :], rhs=wba.t[:, h, cs], start=(h == 0), stop=(h == 7))
                    for h in range(8):
                        r = e.matmul(pyb[hf].t[:], lhsT=obT[b_].t[:, h, :], rhs=wbb.t[:, h, cs], start=(h == 0), stop=(h == 7))
                    return r
                op('pe', mmy, reads=[oaT[b_].d, obT[b_].d, wba.d, wbb.d], writes=[pya[hf].d, pyb[hf].d])
                op('dve', lambda e, hf=hf, cs=cs, b_=b_: e.tensor_tensor(out=t1[hf].t[:], in0=pya[hf].t[:], in1=gt[b_].t[:, cs], op=ALU.mult),
                   reads=[pya[hf].d, gt[b_].d], writes=[t1[hf].d])
                op('dve', lambda e, hf=hf, b_=b_: e.tensor_tensor(out=t2[hf].t[:], in0=pyb[hf].t[:],
                                                                 in1=gt[b_].t[:, D + hf * 512:D + (hf + 1) * 512], op=ALU.mult),
                   reads=[pyb[hf].d, gt[b_].d], writes=[t2[hf].d])
                op('pool', lambda e, hf=hf, cs=cs, b_=b_: e.tensor_tensor(out=mb[b_].t[:, cs], in0=t1[hf].t[:], in1=t2[hf].t[:], op=ALU.add),
                   reads=[t1[hf].d, t2[hf].d], writes=[mb[b_].d])

        def tr8b(e, src):
            r = None
            for dc in range(8):
                r = e.transpose(out=ptr2.t[:, dc * 128:(dc + 1) * 128], in_=src.t[:, dc * 128:(dc + 1) * 128], identity=identb.t[:])
            return r

        def mstage_B(i):
            b_ = i % 2
            op('pe', lambda e, b_=b_: tr8b(e, mb[b_]), reads=[mb[b_].d, identb.d], writes=[ptr2.d])
            op('act', lambda e, b_=b_: e.copy(out=mT[b_].t[:], in_=ptr2.t[:].rearrange("p (c t) -> p c t", c=8)),
               reads=[ptr2.d], writes=[mT[b_].d])
            for hf in range(2):
                cs = slice(hf * 512, (hf + 1) * 512)

                def mmo(e, hf=hf, cs=cs, b_=b_):
                    r = None
                    for dc in range(8):
                        r = e.matmul(pmx[hf].t[:], lhsT=mT[b_].t[:, dc, :], rhs=wo.t[:, dc, cs], start=(dc == 0), stop=(dc == 7))
                    return r
                op('pe', mmo, reads=[mT[b_].d, wo.d], writes=[pmx[hf].d])
                op('dve', lambda e, hf=hf, cs=cs, b_=b_: e.tensor_tensor(out=r2[b_].t[:, cs], in0=pmx[hf].t[:], in1=xr[b_].t[:, cs], op=ALU.add),
                   reads=[pmx[hf].d, xr[b_].d], writes=[r2[b_].d])
            dma('sp', s_res2[(i - 1) * 128:i * 128, :], r2[b_].t[:], reads=[r2[b_].d])

        mstage_A(1)
        for i in range(1, NB):
            if i + 1 < NB:
                mstage_A(i + 1)
            mstage_B(i)
        S.barrier()
        if upto <= 7:
            return nc, S, es
    if upto < 5:
        S.barrier()
        if upto <= 8:
            return nc, S, es
        return nc, S, es
    ph5 = ExitStack()
    h2T = sb("h2T", [128, 8, 2048], BF16, ph5)
    h2Td = [Dep() for _ in range(16)]
    dscrd = [Dep() for _ in range(16)]
    with ExitStack() as ph:
        nffn = sb("nffn", [128, D], F32, ph)
        dma('sp', nffn.t[:], nffn_in, writes=[nffn.d])
        wq = sb("wq", [128, 8, 2048], BF16, ph)
        wqd = [Dep() for _ in range(4)]
        for c in range(4):
            dma('pool', wq.t[:, :, c * 512:(c + 1) * 512], wq_in[:, c * 512:(c + 1) * 512].rearrange("(dc p) c -> p dc c", p=128),
                writes=[wqd[c]])
        ksT = sb("ksT", [128, 16, 128], F32, ph)
        skl = [sb(f"skl{i}", [128, 128], F32, ph) for i in range(2)]
        pq = [ps(f"pq{i}", [128, 512], F32, ph) for i in range(2)]
        psc = [ps(f"psc{i}", [128, 512], F32, ph) for i in range(2)]
        ptr5 = ps("ptr5", [128, 1024], BF16, ph)
        for g in range(16):
            sk_ = skl[g % 2]
            dma('sp', sk_.t[:], sk_in[g], writes=[sk_.d])
            p = pq[g % 2]
            op('pe', lambda e, p=p, sk_=sk_: e.matmul(p.t[:, 0:128], lhsT=sk_.t[:], rhs=ident, start=True, stop=True),
               reads=[sk_.d, cf.d], writes=[p.d])
            op('act', lambda e, p=p, g=g: e.copy(out=ksT.t[:, g, :], in_=p.t[:, 0:128]), reads=[p.d], writes=[ksT.d])
        ta = [sb(f"fa{i}", [128, D], F32, ph) for i in range(2)]
        tb_ = [sb(f"fb{i}", [128, D], F32, ph) for i in range(2)]
        junk5 = sb("junk5", [128, D], F32, ph)
        ssq5 = [sb(f"ssq5_{i}", [128, 1], F32, ph) for i in range(2)]
        h2b = [sb(f"h2b{i}", [128, D], BF16, ph) for i in range(2)]
        qg4 = [sb(f"qg4_{i}", [128, 512], F32, ph) for i in range(2)]
        scs = [sb(f"sc{i}", [128, 16, 128], F32, ph) for i in range(2)]
        sc2s = sb("sc2s", [128, 16, 128], F32, ph)
        v16d = [Dep() for _ in range(16)]
        i16d = [Dep() for _ in range(16)]
        sc2d = [Dep() for _ in range(16)]
        c8d = [Dep() for _ in range(8)]
        ctd = [Dep() for _ in range(8)]
        v16 = sb("v16", [128, 16, 16], F32, ph)
        i16 = sb("i16", [128, 16, 16], U32, ph)
        i16f = sb("i16f", [128, 16, 16], F32, ph)
        cand = sb("cand", [128, 8, 256], F32, ph)
        c8 = sb("c8", [128, 8, 16], F32, ph)
        tau = sb("tau", [128, 8], F32, ph)
        mx = sb("mx", [128, 8], F32, ph)
        zz = sb("zz", [128, 8], F32, ph)
        msk = sb("msk", [128, 8, 256], F32, ph)
        gw = sb("gw", [128, 8, 256], F32, ph)
        gwc = sb("gwc", [128, 16, 128], F32, ph)
        i1c = sb("i1c", [128, 128], F32, ph)
        i2c = sb("i2c", [128, 128], F32, ph)
        iT = [sb(f"iT{i}", [128, 256], F32, ph) for i in range(2)]
        gwTs = [sb(f"gwTs{i}", [128, 128, 16], BF16, ph) for i in range(2)]
        def stage_A(j):
            sc = scs[j % 2]
            a_, b_ = ta[j % 2], tb_[j % 2]
            dma('sp', a_.t[:], s_res2[j * 128:(j + 1) * 128, :], writes=[a_.d])
            dma('sp', b_.t[:], s_res2[(16 + j) * 128:(17 + j) * 128, :], writes=[b_.d])
            op('dve', lambda e, a_=a_: e.tensor_scalar(out=a_.t[:], in0=a_.t[:], scalar1=ompcol, scalar2=None, op0=ALU.mult),
               reads=[a_.d, cf.d], writes=[a_.d])
            op('dve', lambda e, a_=a_, b_=b_: e.scalar_tensor_tensor(out=a_.t[:], in0=b_.t[:], scalar=pcol, in1=a_.t[:], op0=ALU.mult, op1=ALU.add),
               reads=[a_.d, b_.d, cf.d], writes=[a_.d])
            dma('sp', s_r2m[j * 128:(j + 1) * 128, :], a_.t[:], reads=[a_.d], writes=[dscrd[j]])
            sq = ssq5[j % 2]
            op('act', lambda e, a_=a_, sq=sq: e.activation(out=junk5.t[:], in_=a_.t[:], func=AF.Square, accum_out=sq.t[:]),
               reads=[a_.d], writes=[junk5.d, sq.d])
            rsqrt_act(sq.t[:], sq.t[:], 1.0 / D, EPS, [sq.d], [sq.d])
            hb = h2b[j % 2]
            op('dve', lambda e, a_=a_, sq=sq, hb=hb: e.scalar_tensor_tensor(out=hb.t[:], in0=a_.t[:], scalar=sq.t[:, 0:1], in1=nffn.t[:],
                                                                          op0=ALU.mult, op1=ALU.mult),
               reads=[a_.d, sq.d, nffn.d], writes=[hb.d])

            def tr8(e, hb=hb):
                r = None
                for dc in range(8):
                    r = e.transpose(out=ptr5.t[:, dc * 128:(dc + 1) * 128], in_=hb.t[:, dc * 128:(dc + 1) * 128], identity=identb.t[:])
                return r
            op('pe', tr8, reads=[hb.d, identb.d], writes=[ptr5.d])
            tcs = slice(j * 128, (j + 1) * 128)
            op('act', lambda e, tcs=tcs: e.copy(out=h2T.t[:, :, tcs], in_=ptr5.t[:].rearrange("p (c t) -> p c t", c=8)),
               reads=[ptr5.d], writes=[h2Td[j]])
            for g4 in range(4):
                p = pq[g4 % 2]

                def mq(e, p=p, g4=g4, tcs=tcs):
                    r = None
                    for gi in range(4):
                        g = g4 * 4 + gi
                        for dc in range(8):
                            r = e.matmul(p.t[:, gi * 128:(gi + 1) * 128], lhsT=wq.t[:, dc, g * 128:(g + 1) * 128], rhs=h2T.t[:, dc, tcs],
                                         start=(dc == 0), stop=(dc == 7))
                    return r
                op('pe', mq, reads=[h2Td[j], wqd[g4]], writes=[p.d])
                q_ = qg4[g4 % 2]
                op('act', lambda e, p=p, q_=q_: e.copy(out=q_.t[:], in_=p.t[:]), reads=[p.d], writes=[q_.d])
                p2 = psc[g4 % 2]

                def msc(e, p2=p2, q_=q_, g4=g4):
                    r = None
                    for gi in range(4):
                        r = e.matmul(p2.t[:, gi * 128:(gi + 1) * 128], lhsT=q_.t[:, gi * 128:(gi + 1) * 128], rhs=ksT.t[:, g4 * 4 + gi, :],
                                     start=True, stop=True)
                    return r
                op('pe', msc, reads=[q_.d, ksT.d], writes=[p2.d])
                op('dve', lambda e, p2=p2, g4=g4: e.tensor_copy(out=sc.t[:, g4 * 4:(g4 + 1) * 4, :],
                                                                in_=p2.t[:].rearrange("p (g n) -> p g n", g=4)),
                   reads=[p2.d], writes=[sc.d])

        def stage_B(j):
            sc = scs[j % 2]
            for g in range(16):
                op('dve', lambda e, g=g: e.max(out=v16.t[:, g, 0:8], in_=sc.t[:, g, :]), reads=[sc.d], writes=[v16d[g]])
            for g in range(16):
                op('dve', lambda e, g=g: e.max_index(out=i16.t[:, g, 0:8], in_max=v16.t[:, g, 0:8], in_values=sc.t[:, g, :]),
                   reads=[sc.d, v16d[g]], writes=[i16d[g]])
            for g in range(16):
                op('dve', lambda e, g=g: e.match_replace(out=sc2s.t[:, g, :], in_to_replace=v16.t[:, g, 0:8], in_values=sc.t[:, g, :], imm_value=-1e30),
                   reads=[sc.d, v16d[g]], writes=[sc2d[g]])
            for g in range(16):
                op('dve', lambda e, g=g: e.max(out=v16.t[:, g, 8:16], in_=sc2s.t[:, g, :]), reads=[sc2d[g]], writes=[v16d[g]])
            for g in range(16):
                op('dve', lambda e, g=g: e.max_index(out=i16.t[:, g, 8:16], in_max=v16.t[:, g, 8:16], in_values=sc2s.t[:, g, :]),
                   reads=[sc2d[g], v16d[g]], writes=[i16d[g]])
            op('dve', lambda e: e.tensor_copy(out=i16f.t[:], in_=i16.t[:]), reads=i16d, writes=[i16f.d])
            v16r = v16.t[:].rearrange("p (h two) k -> p h two k", two=2)
            i16r = i16f.t[:].rearrange("p (h two) k -> p h two k", two=2)
            cand4 = cand.t[:].rearrange("p h (a b) -> p h a b", a=16)
            op('dve', lambda e: e.tensor_tensor(out=cand4, in0=v16r[:, :, 0, :].unsqueeze(3).to_broadcast([128, 8, 16, 16]),
                                               in1=v16r[:, :, 1, :].unsqueeze(2).to_broadcast([128, 8, 16, 16]), op=ALU.add),
               reads=v16d, writes=[cand.d])
            for hh in range(8):
                op('dve', lambda e, hh=hh: e.max(out=c8.t[:, hh, 0:8], in_=cand.t[:, hh, :]), reads=[cand.d], writes=[c8d[hh]])
            for hh in range(8):
                op('dve', lambda e, hh=hh: e.match_replace(out=msk.t[:, hh, :], in_to_replace=c8.t[:, hh, 0:8], in_values=cand.t[:, hh, :], imm_value=-1e30),
                   reads=[cand.d, c8d[hh]], writes=[ctd[hh]])
            for hh in range(8):
                op('dve', lambda e, hh=hh: e.max(out=c8.t[:, hh, 8:16], in_=msk.t[:, hh, :]), reads=[ctd[hh]], writes=[c8d[hh]])
            op('dve', lambda e: e.tensor_reduce(out=tau.t[:], in_=c8.t[:, :, 8:16], axis=AX.X, op=ALU.min), reads=c8d, writes=[tau.d])
            op('dve', lambda e: e.tensor_reduce(out=mx.t[:], in_=c8.t[:, :, 0:8], axis=AX.X, op=ALU.max), reads=c8d, writes=[mx.d])
            op('dve', lambda e: e.tensor_tensor(out=msk.t[:], in0=cand.t[:], in1=tau.t[:].unsqueeze(2).to_broadcast([128, 8, 256]), op=ALU.is_ge),
               reads=[cand.d, tau.d], writes=[msk.d] + ctd)
            op('dve', lambda e: e.tensor_tensor(out=gw.t[:], in0=cand.t[:], in1=mx.t[:].unsqueeze(2).to_broadcast([128, 8, 256]), op=ALU.subtract),
               reads=[cand.d, mx.d], writes=[gw.d])
            op('act', lambda e: e.activation(out=gw.t[:], in_=gw.t[:], func=AF.Exp), reads=[gw.d], writes=[gw.d])
            op('dve', lambda e: e.tensor_tensor(out=gw.t[:], in0=gw.t[:], in1=msk.t[:], op=ALU.mult), reads=[gw.d, msk.d], writes=[gw.d])
            op('dve', lambda e: e.tensor_reduce(out=zz.t[:], in_=gw.t[:], axis=AX.X, op=ALU.add), reads=[gw.d], writes=[zz.d])
            op('dve', lambda e: e.reciprocal(out=zz.t[:], in_=zz.t[:]), reads=[zz.d], writes=[zz.d])
            op('dve', lambda e: e.tensor_tensor(out=gw.t[:], in0=gw.t[:], in1=zz.t[:].unsqueeze(2).to_broadcast([128, 8, 256]), op=ALU.mult),
               reads=[gw.d, zz.d], writes=[gw.d])
            op('dve', lambda e: e.tensor_copy(out=i1c.t[:].rearrange("p (h k) -> p h k", h=8), in_=i16r[:, :, 0, :]), reads=[i16f.d], writes=[i1c.d])
            op('dve', lambda e: e.tensor_copy(out=i2c.t[:].rearrange("p (h k) -> p h k", h=8), in_=i16r[:, :, 1, :]), reads=[i16f.d], writes=[i2c.d])
            op('pool', lambda e: e.tensor_copy(out=gwc.t[:].rearrange("p a (h b) -> p a h b", h=8),
                                              in_=gw.t[:].rearrange("p h (a b) -> p a h b", a=16)), reads=[gw.d], writes=[gwc.d])
            p = pq[0]

            def mti(e, p=p):
                e.matmul(p.t[:, 0:128], lhsT=i1c.t[:], rhs=ident, start=True, stop=True)
                return e.matmul(p.t[:, 128:256], lhsT=i2c.t[:], rhs=ident, start=True, stop=True)
            op('pe', mti, reads=[i1c.d, i2c.d, cf.d], writes=[p.d])
            it_ = iT[j % 2]
            op('act', lambda e, p=p, it_=it_: e.copy(out=it_.t[:], in_=p.t[:, 0:256]), reads=[p.d], writes=[it_.d])
            dma('sp', s_i1T[j], it_.t[:, 0:128], reads=[it_.d], writes=[dscrd[j]])
            dma('sp', s_i2T[j], it_.t[:, 128:256], reads=[it_.d], writes=[dscrd[j]])
            gts = gwTs[j % 2]
            for k4 in range(4):
                p = pq[1] if k4 % 2 else psc[1]

                def mtg(e, p=p, k4=k4):
                    r = None
                    for ki in range(4):
                        r = e.matmul(p.t[:, ki * 128:(ki + 1) * 128], lhsT=gwc.t[:, k4 * 4 + ki, :], rhs=ident, start=True, stop=True)
                    return r
                op('pe', mtg, reads=[gwc.d, cf.d], writes=[p.d])
                op('act', lambda e, p=p, k4=k4, gts=gts: e.copy(out=gts.t[:, :, k4 * 4:(k4 + 1) * 4].rearrange("p t k -> p k t"),
                                                               in_=p.t[:].rearrange("p (k t) -> p k t", k=4)),
                   reads=[p.d], writes=[gts.d])
            dma('sp', s_gwT[j], gts.t[:].rearrange("p t k -> p (t k)"), reads=[gts.d], writes=[dscrd[j]])

        stage_A(0)
        for j in range(16):
            if j + 1 < 16:
                stage_A(j + 1)
            stage_B(j)
        S.barrier()
        if upto <= 9:
            return nc, S, es

    with ExitStack() as ph:
        WT = sb("WT", [128, 256, 128], BF16, ph)
        i1s_ = [sb(f"i1s{i}", [128, 32], F32, ph) for i in range(2)]
        i2s_ = [sb(f"i2s{i}", [128, 32], F32, ph) for i in range(2)]
        gws_ = [sb(f"gws{i}", [128, 32, 16], BF16, ph) for i in range(2)]
        OH1_ = [sb(f"OH1{i}", [128, 32, 128], BF16, ph) for i in range(2)]
        OH2_ = [sb(f"OH2{i}", [128, 32, 128], BF16, ph) for i in range(2)]
        GWbd_ = [sb(f"GWbd{i}", [128, 32, 128], BF16, ph) for i in range(2)]
        Msb_ = [sb(f"Msb{i}", [128, 32, 128], BF16, ph) for i in range(1)] * 2
        uvb = [sb(f"p_uv{i}", [128, 2 * D], BF16, ph) for i in range(3)]
        actb = [sb(f"p_act{i}", [128, 256], F32, ph) for i in range(2)]
        wab = [sb(f"p_wa{i}", [128, 256], BF16, ph) for i in range(2)]
        r2t = [sb(f"p_r2t{i}", [128, D], F32, ph) for i in range(1)] * 2
        yo = [sb(f"p_yo{i}", [128, D], F32, ph) for i in range(1)] * 2
        pA = [ps(f"pA{i}", [128, 512], F32, ph) for i in range(2)]
        po = [[ps(f"po{ts}{hf}", [128, 512], F32, ph) for hf in range(2)] for ts in range(2)]
        yd = Dep()
        bd3 = bdm.rearrange("p (h k) -> p h k", h=8)
        for tb in range(8):
            tcols = slice(tb * 256, (tb + 1) * 256)
            for sub in range(8):
                j = tb * 2 + sub // 4
                toff = (sub % 4) * 32
                i1s, i2s, gws, OH1, OH2, GWbd, Msb = (x_[sub % 2] for x_ in (i1s_, i2s_, gws_, OH1_, OH2_, GWbd_, Msb_))
                dma('sp', i1s.t[:], s_i1T[j][:, toff:toff + 32], reads=[dscrd[j]], writes=[i1s.d])
                dma('sp', i2s.t[:], s_i2T[j][:, toff:toff + 32], reads=[dscrd[j]], writes=[i2s.d])
                dma('sp', gws.t[:], s_gwT[j][:, toff * 16:(toff + 32) * 16].rearrange("p (t k) -> p t k", k=16),
                    reads=[dscrd[j]], writes=[gws.d])
                op('dve', lambda e: e.tensor_tensor(out=OH1.t[:], in0=iota.unsqueeze(1).to_broadcast([128, 32, 128]),
                                                   in1=i1s.t[:].unsqueeze(2).to_broadcast([128, 32, 128]), op=ALU.is_equal),
                   reads=[cf.d, i1s.d], writes=[OH1.d])
                op('dve', lambda e: e.tensor_tensor(out=OH2.t[:], in0=iota.unsqueeze(1).to_broadcast([128, 32, 128]),
                                                    in1=i2s.t[:].unsqueeze(2).to_broadcast([128, 32, 128]), op=ALU.is_equal),
                   reads=[cf.d, i2s.d], writes=[OH2.d])
                op('dve', lambda e: e.tensor_tensor(out=GWbd.t[:].rearrange("p t (h k) -> p t h k", h=8),
                                                   in0=gws.t[:].unsqueeze(2).to_broadcast([128, 32, 8, 16]),
                                                   in1=bd3.unsqueeze(1).to_broadcast([128, 32, 8, 16]), op=ALU.mult),
                   reads=[gws.d, cf.d], writes=[GWbd.d])
                for t4 in range(8):
                    p = pA[t4 % 2]

                    def mM(e, p=p, t4=t4):
                        r = None
                        for q_ in range(4):
                            t = t4 * 4 + q_
                            r = e.matmul(p.t[:, q_ * 128:(q_ + 1) * 128], lhsT=GWbd.t[:, t, :], rhs=OH2.t[:, t, :], start=True, stop=True)
                        return r
                    op('pe', mM, reads=[GWbd.d, OH2.d], writes=[p.d])
                    op('act', lambda e, p=p, t4=t4: e.copy(out=Msb.t[:, t4 * 4:(t4 + 1) * 4, :], in_=p.t[:].rearrange("p (q b) -> p q b", q=4)),
                       reads=[p.d], writes=[Msb.d])
                for t4 in range(8):
                    p = pA[t4 % 2]

                    def mW(e, p=p, t4=t4):
                        r = None
                        for q_ in range(4):
                            t = t4 * 4 + q_
                            r = e.matmul(p.t[:, q_ * 128:(q_ + 1) * 128], lhsT=Msb.t[:, t, :], rhs=OH1.t[:, t, :], start=True, stop=True)
                        return r
                    op('pe', mW, reads=[Msb.d, OH1.d], writes=[p.d])
                    tg = sub * 32 + t4 * 4
                    op('dve', lambda e, p=p, tg=tg: e.tensor_copy(out=WT.t[:, tg:tg + 4, :], in_=p.t[:].rearrange("p (q a) -> p q a", q=4)),
                       reads=[p.d], writes=[WT.d])
            def emit_A(a):
                uv = uvb[a % 3]
                dma('sp', uv.t[:], s_UV[a], writes=[uv.d])
                p = pA[a % 2]

                def mA(e):
                    r = None
                    for dc in range(8):
                        r = e.matmul(p.t[:, 0:256], lhsT=uv.t[:, dc * 128:(dc + 1) * 128], rhs=h2T.t[:, dc, tcols], start=(dc == 0), stop=(dc == 7))
                    return r
                op('pe', mA, reads=[uv.d] + h2Td, writes=[p.d])

            def emit_O(a):
                uv = uvb[a % 3]
                p = pA[a % 2]
                ac = actb[a % 2]
                wa = wab[a % 2]
                op('act', lambda e: e.activation(out=ac.t[:], in_=p.t[:, 0:256], func=AF.Gelu), reads=[p.d], writes=[ac.d])
                op('dve', lambda e: e.tensor_tensor(out=wa.t[:], in0=ac.t[:], in1=WT.t[:, :, a], op=ALU.mult),
                   reads=[ac.d, WT.d], writes=[wa.d])

                def mO(e):
                    r = None
                    for ts in range(2):
                        for hf in range(2):
                            r = e.matmul(po[ts][hf].t[:], lhsT=wa.t[:, ts * 128:(ts + 1) * 128], rhs=uv.t[:, D + hf * 512:D + (hf + 1) * 512],
                                         start=(a == 0), stop=(a == NEXP_CH - 1))
                    return r
                op('pe', mO, reads=[wa.d, uv.d], writes=[po[0][0].d, po[0][1].d, po[1][0].d, po[1][1].d])
            emit_A(0)
            for a in range(NEXP_CH):
                if a + 1 < NEXP_CH:
                    emit_A(a + 1)
                emit_O(a)
            for ts in range(2):
                jt = tb * 2 + ts
                rt = r2t[ts]
                yo_ = yo[ts]
                dma('sp', rt.t[:], s_r2m[jt * 128:(jt + 1) * 128, :], reads=[dscrd[jt]], writes=[rt.d])
                for hf in range(2):
                    cs = slice(hf * 512, (hf + 1) * 512)
                    op('dve', lambda e, ts=ts, hf=hf, cs=cs, rt=rt, yo_=yo_: e.tensor_tensor(out=yo_.t[:, cs], in0=po[ts][hf].t[:], in1=rt.t[:, cs], op=ALU.add),
                       reads=[po[ts][hf].d, rt.d], writes=[yo_.d])
                dma('sp', y_out[jt * 128:(jt + 1) * 128, :], yo_.t[:], reads=[yo_.d], writes=[yd])
        S.barrier()
    ph5.close()
    return nc, S, es


def _consts():
    cf = np.zeros((128, C_END), np.float32)
    s = np.arange(128)[:, None]
    c = np.arange(128)[None, :]
    cf[:, C_IDENT:C_IDENT + 128] = np.eye(128)
    cf[:, C_TRI:C_TRI + 128] = (s <= c)
    cf[:, C_ONES:C_ONES + 128] = 1.0
    cf[:, C_MSTRICT:C_MSTRICT + 128] = (c > s)
    cf[:, C_MINCL:C_MINCL + 128] = (c >= s)
    cf[:, C_IOTA:C_IOTA + 128] = np.broadcast_to(np.arange(128, dtype=np.float32)[None, :], (128, 128))
    cf[:, C_BD:C_BD + 128] = ((s // 16) == (c // 16))
    cf[:, C_PADNEG] = np.where(np.arange(128) < PADN, -30000.0, 0.0)
    cm = np.zeros((128, 4 * 512), np.float32)
    t = np.arange(512)[None, :]
    for q in range(4):
        cm[:, q * 512:(q + 1) * 512] = (t >= 128 * q + s)
    return cf, cm.astype(ml_dtypes.bfloat16)


def prep_inputs(x, meta_tokens, norm_mix, w_in, conv_w, a_log, dt_bias, o_norm_a, q_norm_b, k_norm_b, f_bias,
                w_branch, w_out, norm_ffn, peer_wq, peer_sub_keys, expert_u, expert_v, cores=range(8)):
    f = lambda a: np.ascontiguousarray(np.asarray(a, dtype=np.float32))
    w = f(w_in[0])
    w_fm = np.ascontiguousarray(np.concatenate([w[:, 0:3072], w[:, 4112:5136], w[:, 5136:6160]], axis=1))
    w_tm = np.ascontiguousarray(np.concatenate([w[:, 6160:7184], w[:, 3072:4096], w[:, 7192:9240], w[:, 4096:4104],
                                                w[:, 4104:4112], w[:, 7184:7192]], axis=1))
    cw = f(conv_w[0])
    convw = np.ascontiguousarray(cw.reshape(4, 24, 128).transpose(2, 1, 0).reshape(128, 96))
    hvec = np.ascontiguousarray(np.tile(np.concatenate([f(a_log[0]), f(dt_bias[0]), f(f_bias[0])])[None, :], (128, 1)))
    onw = np.ascontiguousarray(np.tile(f(o_norm_a[0])[None, :], (128, 1)))
    qkw = np.ascontiguousarray(np.stack([f(q_norm_b[0]), f(k_norm_b[0])], axis=1))
    sk = f(peer_sub_keys[0])
    subk = np.ascontiguousarray(sk.transpose(1, 0, 2, 3).reshape(16, 128, 128))
    cf, cm = _consts()
    shared = dict(meta=f(meta_tokens), nmix=np.ascontiguousarray(np.tile(f(norm_mix[0])[None, :], (128, 1))),
                  nffn=np.ascontiguousarray(np.tile(f(norm_ffn[0])[None, :], (128, 1))),
                  w_fm=w_fm, w_tm=w_tm, convw=convw, hvec=hvec, onw=onw, qkw=qkw, wbr=f(w_branch[0]), wout=f(w_out[0]),
                  wq=f(peer_wq[0]), subk=subk, eu=f(expert_u[0]), ev=f(expert_v[0]), cmask=cm)
    maps = []
    xs = np.asarray(x, dtype=np.float32)
    for c in cores:
        cfc = cf.copy()
        cfc[:, C_P] = float(c % 2)
        cfc[:, C_OMP] = 1.0 - float(c % 2)
        m = dict(shared)
        m["x"] = np.ascontiguousarray(xs[c // 2])
        m["cf32"] = cfc
        maps.append(m)
    return maps


_NC_CACHE = {}


def kernel(**inputs):
    if "nc" not in _NC_CACHE:
        _NC_CACHE["nc"] = build_program(debug=False)[0]
    nc = _NC_CACHE["nc"]
    maps = prep_inputs(**inputs)
    res = run_bass_kernel_spmd(nc, maps, core_ids=list(range(8)))
    out = np.zeros((4, SEQ, D), np.float32)
    for c in range(8):
        out[c // 2, (c % 2) * 2048:(c % 2) * 2048 + 2048] = np.asarray(res.results[c]["y"], dtype=np.float32)
    return out
```

```python
import numpy as np
import ml_dtypes
from contextlib import ExitStack
import concourse.bass as bass
import concourse.mybir as mybir
from concourse.bass_utils import run_bass_kernel_spmd

F32 = mybir.dt.float32
BF16 = mybir.dt.bfloat16
U32 = mybir.dt.uint32
ALU = mybir.AluOpType
AF = mybir.ActivationFunctionType
AX = mybir.AxisListType

D = 1024
SEQ = 4096
NMETA = 16
PADN = 112
LP = 4224
NB = 33
NH = 8
EPS = 1e-6
NEXP_CH = 128
TBLK = [(i * 512, 512) for i in range(8)] + [(4096, 128)]

O_CONV, O_Z, O_BA, O_AA, O_QKVB, O_FB, O_G = 0, 3072, 4096, 4104, 4112, 7184, 7192

C_IDENT, C_TRI, C_ONES, C_MSTRICT, C_MINCL, C_IOTA, C_BD, C_PADNEG, C_P, C_OMP, C_END = (
    0, 128, 256, 384, 512, 640, 768, 896, 897, 898, 899)


class Dep:
    __slots__ = ("w", "r", "excl")

    def __init__(self, excl=False):
        self.w = {}
        self.r = {}
        self.excl = excl


class Sched:
    def __init__(self, nc, es):
        self.nc = nc
        self.eh = {'pe': nc.tensor, 'act': nc.scalar, 'dve': nc.vector, 'pool': nc.gpsimd, 'sp': nc.sync}
        self.esem = {k: es.enter_context(nc.semaphore("S_" + k)) for k in self.eh}
        self.ecnt = {k: 0 for k in self.eh}
        self.known = {k: {} for k in self.eh}
        self.rings = {}
        for q, n in (('sp', 28), ('pool', 12)):
            self.rings[q] = dict(sems=[es.enter_context(nc.semaphore(f"R_{q}{i}")) for i in range(n)],
                                 cnt=[0] * n, nxt=0)
        self.ninst = 0
        self.nwait = {k: 0 for k in self.eh}

    def _wait(self, eng, toks):
        kn = self.known[eng]
        for sem, v in toks.items():
            if kn.get(sem, 0) < v:
                self.eh[eng].wait_ge(sem, v)
                self.nwait[eng] += 1
                kn[sem] = v

    @staticmethod
    def _deps(reads, writes):
        toks = {}

        def add(d):
            for s, v in d.items():
                if toks.get(s, 0) < v:
                    toks[s] = v
        for d in reads:
            add(d.w)
        for d in writes:
            add(d.w)
            add(d.r)
        return toks

    @staticmethod
    def _finish(tok, reads, writes):
        s, v = tok
        for d in reads:
            if d.r.get(s, 0) < v:
                d.r[s] = v
        for d in writes:
            d.w = {s: v}
            d.r = {}

    def op(self, eng, fn, reads=(), writes=()):
        ex = [d for d in reads if d.excl]
        if ex:
            reads = [d for d in reads if not d.excl]
            writes = list(writes) + ex
        toks = self._deps(reads, writes)
        if eng == 'pe':
            toks.pop(self.esem['pe'], None)
        self._wait(eng, toks)
        inst = fn(self.eh[eng])
        self.ecnt[eng] += 1
        inst.then_inc(self.esem[eng], 1)
        self._finish((self.esem[eng], self.ecnt[eng]), reads, writes)
        self.ninst += 1

    def dma(self, q, out, in_, reads=(), writes=()):
        ring = self.rings[q]
        i = ring['nxt']
        ring['nxt'] = (i + 1) % len(ring['sems'])
        sem = ring['sems'][i]
        toks = self._deps(reads, writes)
        if ring['cnt'][i] > 0 and toks.get(sem, 0) < ring['cnt'][i]:
            toks[sem] = ring['cnt'][i]
        self._wait(q, toks)
        self.eh[q].dma_start(out=out, in_=in_).then_inc(sem, 16)
        ring['cnt'][i] += 16
        self._finish((sem, ring['cnt'][i]), reads, writes)
        self.ninst += 1

    def all_tokens(self):
        toks = {self.esem[k]: self.ecnt[k] for k in self.eh if self.ecnt[k] > 0}
        for ring in self.rings.values():
            for s, c in zip(ring['sems'], ring['cnt']):
                if c > 0:
                    toks[s] = c
        return toks

    def barrier(self):
        toks = self.all_tokens()
        for eng in self.eh:
            self._wait(eng, dict(toks))


class TD:
    def __init__(self, t):
        self.t = t
        self.d = Dep()


KNOB = dict(skip=0, dh=NH, dn=NB, feat=99, skipE=0, dump=1023)


def build_program(debug=False, upto=99):
    _sk = KNOB['skip']
    nc = bass.Bass("TRN2", target_bir_lowering=False)
    es = ExitStack()

    def din(name, shape, dt=F32):
        return nc.dram_tensor(name, list(shape), dt, kind="ExternalInput").ap()

    def dscr(name, shape, dt=F32):
        kind = "ExternalOutput" if (debug and name in ("s_oa", "s_obT", "s_res2", "s_qTa", "s_kTa", "s_vTa",
                                                       "s_qTb", "s_kTb")) else "Internal"
        return nc.dram_tensor(name, list(shape), dt, kind=kind).ap()

    x_in = din("x", [SEQ, D])
    meta_in = din("meta", [NMETA, D])
    nmix_in = din("nmix", [128, D])
    nffn_in = din("nffn", [128, D])
    wfm_in = din("w_fm", [D, 5120])
    wtm_in = din("w_tm", [D, 4120])
    convw_in = din("convw", [128, 24 * 4])
    hv_in = din("hvec", [128, 24])
    onw_in = din("onw", [128, 128])
    qkw_in = din("qkw", [128, 2])
    wbr_in = din("wbr", [2, D, D])
    wout_in = din("wout", [D, D])
    wq_in = din("wq", [D, 2048])
    sk_in = din("subk", [16, 128, 128])
    eu_in = din("eu", [16384, D])
    ev_in = din("ev", [16384, D])
    cf_in = din("cf32", [128, C_END])
    cm_in = din("cmask", [128, 4 * 512], BF16)
    y_out = nc.dram_tensor("y", [2048, D], F32, kind="ExternalOutput").ap()

    s_qTa = dscr("s_qTa", [NH, 128, LP])
    s_kTa = dscr("s_kTa", [NH, 128, LP])
    s_vTa = dscr("s_vTa", [NH, 128, LP])
    s_qTb = dscr("s_qTb", [NH, 128, LP], BF16)
    s_kTb = dscr("s_kTb", [NH, 128, LP], BF16)
    s_vb = dscr("s_vb", [LP, D], BF16)
    s_zs = dscr("s_zs", [LP, D])
    s_gt = dscr("s_gt", [LP, 2 * D])
    s_oa = dscr("s_oa", [LP, D])
    s_obT = dscr("s_obT", [NH, 128, LP], BF16)
    s_res2 = dscr("s_res2", [SEQ, D])
    s_r2m = dscr("s_r2m", [2048, D])
    s_UT = dscr("s_UT", [NEXP_CH, 128, 8 * 128], BF16)
    s_VB = dscr("s_VB", [NEXP_CH, 128, D], BF16)
    s_GT = dscr("s_GT", [8, LP])
    s_NGT = dscr("s_NGT", [8, LP])
    dGT = Dep()
    s_i1T = dscr("s_i1T", [16, 128, 128])
    s_i2T = dscr("s_i2T", [16, 128, 128])
    s_gwT = dscr("s_gwT", [16, 128, 128 * 16], BF16)

    S = Sched(nc, es)
    op, dma = S.op, S.dma

    def sb(name, shape, dt=F32, stack=None):
        return TD((stack or es).enter_context(nc.sbuf_tensor("sb_" + name, list(shape), dt)))

    def ps(name, shape, dt=F32, stack=None):
        t_ = TD((stack or es).enter_context(nc.psum_tensor("ps_" + name, list(shape), dt)))
        t_.d.excl = True
        return t_

    _pad0 = sb("pad0", [128, 16])
    cf = sb("cf", [128, C_END])
    cmk = sb("cmk", [128, 4 * 512], BF16)
    identb = sb("identb", [128, 128], BF16)
    onesb = sb("onesb", [128, 128], BF16)
    hv = sb("hv", [128, 24])
    dma('sp', cf.t[:], cf_in, writes=[cf.d])
    dma('sp', cmk.t[:], cm_in, writes=[cmk.d])
    dma('sp', hv.t[:], hv_in, writes=[hv.d])
    ident = cf.t[:, C_IDENT:C_IDENT + 128]
    tri = cf.t[:, C_TRI:C_TRI + 128]
    ones = cf.t[:, C_ONES:C_ONES + 128]
    mstrict = cf.t[:, C_MSTRICT:C_MSTRICT + 128]
    mincl = cf.t[:, C_MINCL:C_MINCL + 128]
    iota = cf.t[:, C_IOTA:C_IOTA + 128]
    bdm = cf.t[:, C_BD:C_BD + 128]
    padneg = cf.t[:, C_PADNEG:C_PADNEG + 1]
    pcol = cf.t[:, C_P:C_P + 1]
    ompcol = cf.t[:, C_OMP:C_OMP + 1]
    op('dve', lambda e: e.tensor_copy(out=identb.t[:], in_=ident), reads=[cf.d], writes=[identb.d])
    op('dve', lambda e: e.tensor_copy(out=onesb.t[:], in_=ones), reads=[cf.d], writes=[onesb.d])

    beta = sb("beta", [128, NB, 8])
    nbeta = sb("nbeta", [128, NB, 8])
    gtok = sb("gtok", [128, NB, 8])
    lf = sb("lf", [128, NB, 8])

    def rsqrt_act(out, in_, scale, eps, rd, wr, post_bias=0.0):
        op('act', lambda e: e.activation(out=out, in_=in_, func=AF.Ln, bias=eps, scale=scale), reads=rd, writes=wr)
        op('act', lambda e: e.activation(out=out, in_=out, func=AF.Exp, bias=post_bias, scale=-0.5),
           reads=wr, writes=wr)

    with ExitStack() as ph:
        ub = [sb(f"e_ub{i}", [128, D], F32, ph) for i in range(2)]
        utb = [sb(f"e_utb{i}", [128, 8 * 128], BF16, ph) for i in range(2)]
        vbb = [sb(f"e_vbb{i}", [128, D], BF16, ph) for i in range(3)]
        pt = [ps(f"e_pt{i}", [128, 512], F32, ph) for i in range(4)]
        for a in range(0 if (_sk or KNOB['skipE']) else NEXP_CH):
            u = ub[a % 2]
            dma('sp', u.t[:], eu_in[a * 128:(a + 1) * 128, :], writes=[u.d])
            ut = utb[a % 2]
            for half in range(2):
                p = pt[(a * 2 + half) % 4]

                def tr4(e, p=p, u=u, half=half):
                    r = None
                    for k in range(4):
                        dc = half * 4 + k
                        r = e.transpose(out=p.t[:, k * 128:(k + 1) * 128], in_=u.t[:, dc * 128:(dc + 1) * 128],
                                        identity=ident)
                    return r
                op('pe', tr4, reads=[u.d, cf.d], writes=[p.d])
                eng = 'act' if half == 0 else 'dve'
                if eng == 'act':
                    op('act', lambda e, p=p, ut=ut, half=half: e.copy(out=ut.t[:, half * 512:(half + 1) * 512], in_=p.t[:]),
                       reads=[p.d], writes=[ut.d])
                else:
                    op('dve', lambda e, p=p, ut=ut, half=half: e.tensor_copy(out=ut.t[:, half * 512:(half + 1) * 512], in_=p.t[:]),
                       reads=[p.d], writes=[ut.d])
            dma('sp', s_UT[a], ut.t[:], reads=[ut.d])
            v = vbb[a % 3]
            dma('pool', v.t[:], ev_in[a * 128:(a + 1) * 128, :], writes=[v.d])
            dma('sp', s_VB[a], v.t[:], reads=[v.d])
        S.barrier()
        if upto <= 0:
            return nc, S, es

    ph1 = ExitStack()
    hT = sb("hT", [128, 8, LP], BF16, ph1)
    hTd = [Dep() for _ in range(NB)]
    with ExitStack() as ph:
        nmix = sb("nmix", [128, D], F32, ph)
        dma('sp', nmix.t[:], nmix_in, writes=[nmix.d])
        xt = [sb(f"xt{i}", [128, D], F32, ph) for i in range(2)]
        junk = [sb(f"junk{i}", [128, D], F32, ph) for i in range(2)]
        hn = [sb(f"hn{i}", [128, D], BF16, ph) for i in range(2)]
        ssq = [sb(f"ssq{i}", [128, 1], F32, ph) for i in range(2)]
        ptr = [ps(f"ptr{i}", [128, 1024], BF16, ph) for i in range(2)]
        op('dve', lambda e: e.memset(xt[0].t[:], 0.0), writes=[xt[0].d])
        for i in range(0 if _sk else NB):
            x_ = xt[i % 2]
            if i == 0:
                dma('sp', x_.t[PADN:128, :], meta_in, writes=[x_.d])
            else:
                dma('sp', x_.t[:], x_in[(i - 1) * 128:i * 128, :], writes=[x_.d])
            sq = ssq[i % 2]
            jk = junk[i % 2]
            op('act', lambda e, x_=x_, jk=jk, sq=sq: e.activation(out=jk.t[:], in_=x_.t[:], func=AF.Square, accum_out=sq.t[:]),
               reads=[x_.d], writes=[jk.d, sq.d])
            rsqrt_act(sq.t[:], sq.t[:], 1.0 / D, EPS, [sq.d], [sq.d])
            h_ = hn[i % 2]
            op('dve', lambda e, x_=x_, sq=sq, h_=h_: e.scalar_tensor_tensor(out=h_.t[:], in0=x_.t[:], scalar=sq.t[:, 0:1], in1=nmix.t[:],
                                                                       op0=ALU.mult, op1=ALU.mult),
               reads=[x_.d, sq.d, nmix.d], writes=[h_.d])
            p = ptr[i % 2]

            def tr8(e, p=p, h_=h_):
                r = None
                for dc in range(8):
                    r = e.transpose(out=p.t[:, dc * 128:(dc + 1) * 128], in_=h_.t[:, dc * 128:(dc + 1) * 128],
                                    identity=identb.t[:])
                return r
            op('pe', tr8, reads=[h_.d, identb.d], writes=[p.d])
            op('act', lambda e, p=p, i=i: e.copy(out=hT.t[:, :, i * 128:(i + 1) * 128],
                                                 in_=p.t[:].rearrange("p (c t) -> p c t", c=8)),
               reads=[p.d], writes=[hTd[i]])
        S.barrier()
        if upto <= 1:
            return nc, S, es

    with ExitStack() as ph:
        wtm = sb("wtm", [128, 8, 4120], BF16, ph)
        wtmd = [Dep() for _ in range(9)]
        for c in range(9):
            c0 = c * 512
            cw = min(512, 4120 - c0)
            dma('pool', wtm.t[:, :, c0:c0 + cw],
                wtm_in[:, c0:c0 + cw].rearrange("(dc p) c -> p dc c", p=128), writes=[wtmd[c]])
        pp = [ps(f"pp{i}", [128, 512], F32, ph) for i in range(4)]
        ob = [sb(f"ob{i}", [128, 512], F32, ph) for i in range(4)]
        obb = [sb(f"obb{i}", [128, 512], BF16, ph) for i in range(2)]
        sm = sb("sm", [128, 24], F32, ph)
        nea = sb("nea", [128, 8], F32, ph)
        op('act', lambda e: e.activation(out=nea.t[:], in_=hv.t[:, 0:8], func=AF.Exp), reads=[hv.d], writes=[nea.d])
        op('dve', lambda e: e.tensor_scalar(out=nea.t[:], in0=nea.t[:], scalar1=-1.0, scalar2=None, op0=ALU.mult),
           reads=[nea.d], writes=[nea.d])
        k = 0
        for i in range(0 if _sk else NB):
            for c in range(9):
                c0 = c * 512
                cw = min(512, 4120 - c0)
                p = pp[k % 4]

                def mm(e, p=p, i=i, c0=c0, cw=cw):
                    r = None
                    for dc in range(8):
                        r = e.matmul(p.t[:, 0:cw], lhsT=hT.t[:, dc, i * 128:(i + 1) * 128], rhs=wtm.t[:, dc, c0:c0 + cw],
                                     start=(dc == 0), stop=(dc == 7))
                    return r
                op('pe', mm, reads=[hTd[i], wtmd[c]], writes=[p.d])
                rows = slice(i * 128, (i + 1) * 128)
                if c < 2:
                    o = obb[k % 2]
                    op('act', lambda e, o=o, p=p: e.copy(out=o.t[:], in_=p.t[:]), reads=[p.d], writes=[o.d])
                    dma('sp', s_vb[rows, c0:c0 + 512], o.t[:], reads=[o.d])
                elif c < 4:
                    o = ob[k % 4]
                    op('act', lambda e, o=o, p=p: e.activation(out=o.t[:], in_=p.t[:], func=AF.Silu), reads=[p.d], writes=[o.d])
                    dma('sp', s_zs[rows, c0 - 1024:c0 - 1024 + 512], o.t[:], reads=[o.d])
                elif c < 8:
                    o = ob[k % 4]
                    op('act', lambda e, o=o, p=p: e.activation(out=o.t[:], in_=p.t[:], func=AF.Sigmoid), reads=[p.d], writes=[o.d])
                    dma('sp', s_gt[rows, c0 - 2048:c0 - 2048 + 512], o.t[:], reads=[o.d])
                else:
                    op('act', lambda e, p=p, i=i: e.activation(out=beta.t[:, i, :], in_=p.t[:, 0:8], func=AF.Sigmoid),
                       reads=[p.d], writes=[beta.d])
                    op('dve', lambda e, p=p: e.tensor_tensor(out=sm.t[:, 8:24], in0=p.t[:, 8:24], in1=hv.t[:, 8:24], op=ALU.add),
                       reads=[p.d, hv.d], writes=[sm.d])
                    op('act', lambda e: e.activation(out=sm.t[:, 8:16], in_=sm.t[:, 8:16], func=AF.Exp), reads=[sm.d], writes=[sm.d])
                    op('act', lambda e: e.activation(out=sm.t[:, 16:24], in_=sm.t[:, 16:24], func=AF.Exp, scale=-1.0),
                       reads=[sm.d], writes=[sm.d])
                    op('act', lambda e: e.activation(out=sm.t[:, 8:24], in_=sm.t[:, 8:24], func=AF.Ln, bias=1.0),
                       reads=[sm.d], writes=[sm.d])
                    op('dve', lambda e, i=i: e.tensor_scalar(out=lf.t[:, i, :], in0=sm.t[:, 16:24], scalar1=-1.0, scalar2=None, op0=ALU.mult),
                       reads=[sm.d], writes=[lf.d])
                    op('dve', lambda e, i=i: e.tensor_tensor(out=gtok.t[:, i, :], in0=sm.t[:, 8:16], in1=nea.t[:], op=ALU.mult),
                       reads=[sm.d, nea.d], writes=[gtok.d])
                    op('dve', lambda e, i=i: e.tensor_scalar(out=nbeta.t[:, i, :], in0=beta.t[:, i, :], scalar1=-1.0, scalar2=None, op0=ALU.mult),
                       reads=[beta.d], writes=[nbeta.d])
                k += 1
        S.barrier()
        if upto <= 2:
            return nc, S, es
    with ExitStack() as ph:
        convw = sb("convw", [128, 96], F32, ph)
        qkw = sb("qkw", [128, 2], F32, ph)
        dma('sp', convw.t[:], convw_in, writes=[convw.d])
        dma('sp', qkw.t[:], qkw_in, writes=[qkw.d])
        wsl = [sb(f"wsl{i}", [128, 8, 128], BF16, ph) for i in range(2)]
        raw = [sb(f"raw{i}", [128, LP], F32, ph) for i in range(2)]
        acc = sb("acc", [128, LP], F32, ph)
        sqb = sb("sqb", [128, LP], F32, ph)
        rsb = sb("rsb", [128, LP], F32, ph)
        outb = sb("outb", [128, LP], BF16, ph)
        pp = [ps(f"fp{i}", [128, 512], F32, ph) for i in range(4)]
        pn = [ps(f"fn{i}", [128, 512], F32, ph) for i in range(2)]
        k = 0
        for g in range(0 if _sk else 40):
            w = wsl[g % 2]
            dma('pool', w.t[:], wfm_in[:, g * 128:(g + 1) * 128].rearrange("(dc p) c -> p dc c", p=128), writes=[w.d])
            r_ = raw[g % 2]
            for (t0, tw) in TBLK:
                p = pp[k % 4]
                k += 1

                def mm(e, p=p, w=w, t0=t0, tw=tw):
                    r = None
                    for dc in range(8):
                        r = e.matmul(p.t[:, 0:tw], lhsT=w.t[:, dc, :], rhs=hT.t[:, dc, t0:t0 + tw], start=(dc == 0), stop=(dc == 7))
                    return r
                op('pe', mm, reads=[w.d] + hTd, writes=[p.d])
                op('act', lambda e, p=p, r_=r_, t0=t0, tw=tw: e.copy(out=r_.t[:, t0:t0 + tw], in_=p.t[:, 0:tw]),
                   reads=[p.d], writes=[r_.d])
            if g < 24:
                cw = lambda i: convw.t[:, g * 4 + i:g * 4 + i + 1]
                op('dve', lambda e, r_=r_, c=cw(3): e.tensor_scalar(out=acc.t[:], in0=r_.t[:], scalar1=c, scalar2=None, op0=ALU.mult),
                   reads=[r_.d, convw.d], writes=[acc.d])
                for s_ in (1, 2, 3):
                    op('dve', lambda e, r_=r_, s_=s_, c=cw(3 - s_): e.scalar_tensor_tensor(
                        out=acc.t[:, s_:], in0=r_.t[:, 0:LP - s_], scalar=c, in1=acc.t[:, s_:], op0=ALU.mult, op1=ALU.add),
                       reads=[r_.d, convw.d], writes=[acc.d])
                op('act', lambda e: e.activation(out=acc.t[:], in_=acc.t[:], func=AF.Silu), reads=[acc.d], writes=[acc.d])
                src = acc
            else:
                src = r_
            if g >= 16 and g < 24:
                dma('sp', s_vTa[g - 16], acc.t[:], reads=[acc.d])
                continue
            op('pool', lambda e, src=src: e.tensor_tensor(out=sqb.t[:], in0=src.t[:], in1=src.t[:], op=ALU.mult),
               reads=[src.d], writes=[sqb.d])
            for bi, (t0, tw) in enumerate(TBLK):
                p = pn[bi % 2]
                op('pe', lambda e, p=p, t0=t0, tw=tw: e.matmul(p.t[:, 0:tw], lhsT=ones, rhs=sqb.t[:, t0:t0 + tw], start=True, stop=True),
                   reads=[sqb.d, cf.d], writes=[p.d])
                if g < 8:
                    rsqrt_act(rsb.t[:, t0:t0 + tw], p.t[:, 0:tw], 1.0, EPS, [p.d], [rsb.d], post_bias=float(np.log(128.0 ** -0.5)))
                elif g < 16:
                    rsqrt_act(rsb.t[:, t0:t0 + tw], p.t[:, 0:tw], 1.0, EPS, [p.d], [rsb.d])
                else:
                    rsqrt_act(rsb.t[:, t0:t0 + tw], p.t[:, 0:tw], 1.0 / 128.0, EPS, [p.d], [rsb.d])
            if g < 16:
                op('dve', lambda e: e.tensor_tensor(out=sqb.t[:], in0=acc.t[:], in1=rsb.t[:], op=ALU.mult),
                   reads=[acc.d, rsb.d], writes=[sqb.d])
                dst = s_qTa[g] if g < 8 else s_kTa[g - 8]
                dma('sp', dst, sqb.t[:], reads=[sqb.d])
            else:
                j = 0 if g < 32 else 1
                op('dve', lambda e, r_=r_, j=j: e.scalar_tensor_tensor(out=outb.t[:], in0=r_.t[:], scalar=qkw.t[:, j:j + 1], in1=rsb.t[:],
                                                                      op0=ALU.mult, op1=ALU.mult),
                   reads=[r_.d, rsb.d, qkw.d], writes=[outb.d])
                dst = s_qTb[g - 24] if g < 32 else s_kTb[g - 32]
                dma('sp', dst, outb.t[:], reads=[outb.d])
        S.barrier()
        if upto <= 3:
            return nc, S, es
    ph1.close()

    Gc = sb("Gc", [128, NB, 8])
    Gl = sb("Gl", [128, NB, 8])
    eG = sb("eG", [128, NB, 8])
    negeG = sb("negeG", [128, NB, 8])
    etail = sb("etail", [128, NB, 8])
    eGl = sb("eGl", [128, NB, 8])
    cumT = sb("cumT", [128, NB, 8])
    carry = sb("carry", [128, NB + 1, 8])
    with ExitStack() as ph:
        pw = [ps(f"pw{i}", [128, 512], F32, ph) for i in range(2)]
        op('dve', lambda e: e.memset(carry.t[:, 0, :], 0.0), writes=[carry.d])
        for b in range(NB):
            p = pw[b % 2]

            def mm(e, p=p, b=b):
                e.matmul(p.t[:, 0:8], lhsT=tri, rhs=gtok.t[:, b, :], start=True, stop=True)
                e.matmul(p.t[:, 8:16], lhsT=ones, rhs=gtok.t[:, b, :], start=True, stop=True)
                e.matmul(p.t[:, 16:24], lhsT=tri, rhs=lf.t[:, b, :], start=True, stop=True)
                return e.matmul(p.t[:, 24:32], lhsT=ones, rhs=lf.t[:, b, :], start=True, stop=True)
            op('pe', mm, reads=[gtok.d, lf.d, cf.d], writes=[p.d])
            op('act', lambda e, p=p, b=b: e.copy(out=Gc.t[:, b, :], in_=p.t[:, 0:8]), reads=[p.d], writes=[Gc.d])
            op('act', lambda e, p=p, b=b: e.copy(out=Gl.t[:, b, :], in_=p.t[:, 8:16]), reads=[p.d], writes=[Gl.d])
            op('dve', lambda e, p=p, b=b: e.tensor_tensor(out=cumT.t[:, b, :], in0=p.t[:, 16:24], in1=carry.t[:, b, :], op=ALU.add),
               reads=[p.d, carry.d], writes=[cumT.d])
            op('dve', lambda e, p=p, b=b: e.tensor_tensor(out=carry.t[:, b + 1, :], in0=p.t[:, 24:32], in1=carry.t[:, b, :], op=ALU.add),
               reads=[p.d, carry.d], writes=[carry.d])
        op('act', lambda e: e.activation(out=eG.t[:], in_=Gc.t[:], func=AF.Exp), reads=[Gc.d], writes=[eG.d])
        op('dve', lambda e: e.tensor_scalar(out=negeG.t[:], in0=eG.t[:], scalar1=-1.0, scalar2=None, op0=ALU.mult),
           reads=[eG.d], writes=[negeG.d])
        op('dve', lambda e: e.tensor_tensor(out=etail.t[:], in0=Gl.t[:], in1=Gc.t[:], op=ALU.subtract),
           reads=[Gl.d, Gc.d], writes=[etail.d])
        op('act', lambda e: e.activation(out=etail.t[:], in_=etail.t[:], func=AF.Exp), reads=[etail.d], writes=[etail.d])
        op('act', lambda e: e.activation(out=eGl.t[:], in_=Gl.t[:], func=AF.Exp), reads=[Gl.d], writes=[eGl.d])
        if debug:
            dbg_small = nc.dram_tensor("dbg_small", [10, 128, NB * 8], F32, kind="ExternalOutput").ap()
            for ii, tt in enumerate([beta, nbeta, gtok, lf, Gc, Gl, eG, etail, eGl, cumT]):
                if not (KNOB['dump'] >> ii) & 1:
                    continue
                dma('sp', dbg_small[ii], tt.t[:].rearrange("p b h -> p (b h)"), reads=[tt.d])
        S.barrier()
        if upto <= 4:
            return nc, S, es

    with ExitStack() as ph:
        GH = 4
        onw = sb("onw", [128, 128], F32, ph)
        dma('sp', onw.t[:], onw_in, writes=[onw.d])
        pbank = [ps(f"dp{i}", [128, 512], F32, ph) for i in range(2 * GH)]
        names = ["E", "EmS", "EmI", "X0", "X1", "Y0", "Y1", "P0", "P1", "attnT", "Ktail", "Vtok", "R", "vnew", "tmp", "o", "zs", "oa"]
        HS = []
        for gi in range(GH):
            HS.append(dict(
                KT=[sb(f"dKT{gi}_{p_}", [128, 128], F32, ph) for p_ in range(2)],
                QT=[sb(f"dQT{gi}_{p_}", [128, 128], F32, ph) for p_ in range(2)],
                VT=[sb(f"dVT{gi}_{p_}", [128, 128], F32, ph) for p_ in range(2)],
                S=sb(f"dS{gi}", [128, 128], F32, ph),
                W=[{n_: sb(f"dw{gi}_{p_}_{n_}", [128, 128], F32, ph) for n_ in names} for p_ in range(2)],
                ssq=[sb(f"dssq{gi}_{p_}", [128, 1], F32, ph) for p_ in range(2)],
                slots=[(pbank[2 * gi + q_].t[:, 0:256], pbank[2 * gi + q_].d) for q_ in range(2)], si=[0]))

        def mm(out, od, lhsT, rhs, rd):
            op('pe', lambda e: e.matmul(out, lhsT=lhsT, rhs=rhs, start=True, stop=True), reads=rd, writes=[od])

        def head_gen(h, hs):
            Sst = hs['S']

            def slot():
                s_ = hs['slots'][hs['si'][0] % 2]
                hs['si'][0] += 1
                return s_
            op('dve', lambda e: e.memset(Sst.t[:], 0.0), writes=[Sst.d])
            yield
            for n in range(NB):
                par = n % 2
                w = hs['W'][par]
                KT, QT, VT = hs['KT'][par], hs['QT'][par], hs['VT'][par]
                bs = slice(n * 128, (n + 1) * 128)
                dma('sp', KT.t[:], s_kTa[h][:, bs], writes=[KT.d])
                dma('sp', QT.t[:], s_qTa[h][:, bs], writes=[QT.d])
                dma('sp', VT.t[:], s_vTa[h][:, bs], writes=[VT.d])
                if n >= 1:
                    dma('sp', w["zs"].t[:], s_zs[bs, h * 128:(h + 1) * 128], writes=[w["zs"].d])
                col = lambda T: T.t[:, n, h:h + 1]
                op('pool', lambda e: e.tensor_scalar(out=w["tmp"].t[:], in0=ident, scalar1=col(Gc), scalar2=None, op0=ALU.mult),
                   reads=[cf.d, Gc.d], writes=[w["tmp"].d])
                pk, pkd = slot()

                def mkk(e):
                    e.matmul(pk[:, 0:128], lhsT=KT.t[:], rhs=KT.t[:], start=True, stop=True)
                    return e.matmul(pk[:, 128:256], lhsT=KT.t[:], rhs=QT.t[:], start=True, stop=True)
                op('pe', mkk, reads=[KT.d, QT.d], writes=[pkd])
                pd_, pdd = slot()
                mm(pd_[:, 0:128], pdd, ones, w["tmp"].t[:], [cf.d, w["tmp"].d])
                yield
                op('dve', lambda e: e.tensor_scalar(out=w["E"].t[:], in0=pd_[:, 0:128], scalar1=col(Gc), scalar2=0.0,
                                                   op0=ALU.subtract, op1=ALU.min), reads=[pdd, Gc.d], writes=[w["E"].d])
                op('act', lambda e: e.activation(out=w["E"].t[:], in_=w["E"].t[:], func=AF.Exp), reads=[w["E"].d], writes=[w["E"].d])
                op('dve', lambda e: e.tensor_scalar(out=w["R"].t[:], in0=pk[:, 0:128], scalar1=col(beta), scalar2=-1.0, op0=ALU.mult, op1=ALU.mult),
                   reads=[pkd, beta.d], writes=[w["R"].d])
                yield
                op('pool', lambda e: e.tensor_tensor(out=w["EmS"].t[:], in0=w["E"].t[:], in1=mstrict, op=ALU.mult),
                   reads=[w["E"].d, cf.d], writes=[w["EmS"].d])
                op('pool', lambda e: e.tensor_tensor(out=w["EmI"].t[:], in0=w["E"].t[:], in1=mincl, op=ALU.mult),
                   reads=[w["E"].d, cf.d], writes=[w["EmI"].d])
                op('pool', lambda e: e.tensor_tensor(out=w["X0"].t[:], in0=w["R"].t[:], in1=w["EmS"].t[:], op=ALU.mult),
                   reads=[w["R"].d, w["EmS"].d], writes=[w["X0"].d])
                yield
                op('dve', lambda e: e.tensor_tensor(out=w["attnT"].t[:], in0=pk[:, 128:256], in1=w["EmI"].t[:], op=ALU.mult),
                   reads=[pkd, w["EmI"].d], writes=[w["attnT"].d])
                py, pyd = slot()
                mm(py[:, 0:128], pyd, w["X0"].t[:], ident, [w["X0"].d, cf.d])
                op('pool', lambda e: e.tensor_tensor(out=w["P0"].t[:], in0=w["X0"].t[:], in1=ident, op=ALU.add),
                   reads=[w["X0"].d, cf.d], writes=[w["P0"].d])
                yield
                op('act', lambda e: e.copy(out=w["Y0"].t[:], in_=py[:, 0:128]), reads=[pyd], writes=[w["Y0"].d])
                pt_, ptd = slot()

                def mtr(e):
                    e.matmul(pt_[:, 0:128], lhsT=KT.t[:], rhs=ident, start=True, stop=True)
                    return e.matmul(pt_[:, 128:256], lhsT=VT.t[:], rhs=ident, start=True, stop=True)
                op('pe', mtr, reads=[KT.d, VT.d, cf.d], writes=[ptd])
                yield
                op('dve', lambda e: e.tensor_scalar(out=w["Ktail"].t[:], in0=pt_[:, 0:128], scalar1=col(etail), scalar2=None, op0=ALU.mult),
                   reads=[ptd, etail.d], writes=[w["Ktail"].d])
                op('dve', lambda e: e.tensor_copy(out=w["Vtok"].t[:], in_=pt_[:, 128:256]), reads=[ptd], writes=[w["Vtok"].d])
                yield
                Xc, Yc, Pc = w["X0"], w["Y0"], w["P0"]
                for j in range(1, 7):
                    Xn, Yn, Pn = w[f"X{j % 2}"], w[f"Y{j % 2}"], w[f"P{j % 2}"]
                    p1, p1d = slot()
                    mm(p1[:, 0:128], p1d, Xc.t[:], Yc.t[:], [Xc.d, Yc.d])
                    if j <= 5:
                        mm(p1[:, 128:256], p1d, Yc.t[:], Xc.t[:], [Xc.d, Yc.d])
                    yield
                    op('act', lambda e: e.copy(out=Yn.t[:], in_=p1[:, 0:128]), reads=[p1d], writes=[Yn.d])
                    if j <= 5:
                        op('dve', lambda e: e.tensor_copy(out=Xn.t[:], in_=p1[:, 128:256]), reads=[p1d], writes=[Xn.d])
                    p2, p2d = slot()
                    mm(p2[:, 0:128], p2d, Yn.t[:], Pc.t[:], [Yn.d, Pc.d])
                    yield
                    op('dve', lambda e: e.tensor_tensor(out=Pn.t[:], in0=p2[:, 0:128], in1=Pc.t[:], op=ALU.add),
                       reads=[p2d, Pc.d], writes=[Pn.d])
                    Xc, Yc, Pc = Xn, Yn, Pn
                TT = Pc
                pr, prd = slot()

                def mks(e):
                    e.matmul(pr[:, 0:128], lhsT=KT.t[:], rhs=Sst.t[:], start=True, stop=True)
                    return e.matmul(pr[:, 128:256], lhsT=QT.t[:], rhs=Sst.t[:], start=True, stop=True)
                op('pe', mks, reads=[KT.d, QT.d, Sst.d], writes=[prd])
                yield
                op('dve', lambda e: e.scalar_tensor_tensor(out=w["R"].t[:], in0=pr[:, 0:128], scalar=col(negeG), in1=w["Vtok"].t[:],
                                                          op0=ALU.mult, op1=ALU.add),
                   reads=[prd, negeG.d, w["Vtok"].d], writes=[w["R"].d])
                if n >= 1:
                    op('act', lambda e: e.activation(out=w["tmp"].t[:], in_=pr[:, 128:256], func=AF.Copy, scale=col(eG)),
                       reads=[prd, eG.d], writes=[w["tmp"].d])
                pv, pvd = slot()
                mm(pv[:, 0:128], pvd, TT.t[:], w["R"].t[:], [TT.d, w["R"].d])
                yield
                op('dve', lambda e: e.tensor_scalar(out=w["vnew"].t[:], in0=pv[:, 0:128], scalar1=col(beta), scalar2=None, op0=ALU.mult),
                   reads=[pvd, beta.d], writes=[w["vnew"].d])
                pu, pud = slot()
                mm(pu[:, 0:128], pud, w["Ktail"].t[:], w["vnew"].t[:], [w["Ktail"].d, w["vnew"].d])
                if n >= 1:
                    mm(pu[:, 128:256], pud, w["attnT"].t[:], w["vnew"].t[:], [w["attnT"].d, w["vnew"].d])
                yield
                op('dve', lambda e: e.scalar_tensor_tensor(out=Sst.t[:], in0=Sst.t[:], scalar=col(eGl), in1=pu[:, 0:128],
                                                          op0=ALU.mult, op1=ALU.add),
                   reads=[pud, eGl.d, Sst.d], writes=[Sst.d])
                if n >= 1:
                    op('dve', lambda e: e.tensor_tensor(out=w["o"].t[:], in0=pu[:, 128:256], in1=w["tmp"].t[:], op=ALU.add),
                       reads=[pud, w["tmp"].d], writes=[w["o"].d])
                    sq = hs['ssq'][par]
                    op('act', lambda e: e.activation(out=w["tmp"].t[:], in_=w["o"].t[:], func=AF.Square, accum_out=sq.t[:]),
                       reads=[w["o"].d], writes=[w["tmp"].d, sq.d])
                    yield
                    rsqrt_act(sq.t[:], sq.t[:], 1.0 / 128.0, EPS, [sq.d], [sq.d])
                    yield
                    op('dve', lambda e: e.scalar_tensor_tensor(out=w["oa"].t[:], in0=w["o"].t[:], scalar=sq.t[:, 0:1], in1=onw.t[:],
                                                              op0=ALU.mult, op1=ALU.mult),
                       reads=[w["o"].d, sq.d, onw.d], writes=[w["oa"].d])
                    op('pool', lambda e: e.tensor_tensor(out=w["oa"].t[:], in0=w["oa"].t[:], in1=w["zs"].t[:], op=ALU.mult),
                       reads=[w["oa"].d, w["zs"].d], writes=[w["oa"].d])
                    dma('sp', s_oa[bs, h * 128:(h + 1) * 128], w["oa"].t[:], reads=[w["oa"].d])
                yield

        for g0 in range(0, NH, GH):
            gens = [head_gen(g0 + gi, HS[gi]) for gi in range(GH)]
            while gens:
                for g_ in list(gens):
                    try:
                        next(g_)
                    except StopIteration:
                        gens.remove(g_)
        S.barrier()
        if upto <= 5:
            return nc, S, es
    with ExitStack() as ph:
        QBs = [sb(f"QB{i}", [128, LP], BF16, ph) for i in range(2)]
        KBs = [sb(f"KB{i}", [128, LP], BF16, ph) for i in range(2)]
        VBts = [sb(f"VBt{i}", [128, NB, 128], BF16, ph) for i in range(2)]
        biasT = [sb(f"biasT{i}", [128, NB + 3], F32, ph) for i in range(2)]
        pS = [ps(f"aS{i}", [128, 512], F32, ph) for i in range(3)]
        pO = [ps(f"aO{i}", [128, 512], F32, ph) for i in range(2)]
        pZ = [ps(f"aZ{i}", [128, 512], F32, ph) for i in range(2)]
        PT = [sb(f"PT{i}", [128, 512], BF16, ph) for i in range(4)]
        rec = [sb(f"rec{i}", [128, 512], F32, ph) for i in range(2)]
        obt = [sb(f"obt{i}", [128, 512], BF16, ph) for i in range(2)]
        sc_ = float(128.0 ** -0.5)
        steps = [(h, r, j) for h in range(NH) for r in range(8) for j in range(5 + 4 * r)]
        loaded = set()

        def ensure_head(h):
            if h in loaded:
                return
            loaded.add(h)
            dma('sp', QBs[h % 2].t[:], s_qTb[h], writes=[QBs[h % 2].d])
            dma('sp', KBs[h % 2].t[:], s_kTb[h], writes=[KBs[h % 2].d])
            dma('sp', VBts[h % 2].t[:], s_vb[:, h * 128:(h + 1) * 128].rearrange("(b p) e -> p b e", p=128), writes=[VBts[h % 2].d])

        def emit_S(i):
            h, r, j = steps[i]
            ensure_head(h)
            t0 = 128 * (1 + 4 * r)
            pS_ = pS[i % 3]
            QB, KB = QBs[h % 2], KBs[h % 2]
            op('pe', lambda e: e.matmul(pS_.t[:], lhsT=KB.t[:, j * 128:(j + 1) * 128], rhs=QB.t[:, t0:t0 + 512], start=True, stop=True),
               reads=[KB.d, QB.d], writes=[pS_.d])

        def emit_rest(i):
            h, r, j = steps[i]
            jq = 1 + 4 * r
            t0 = 128 * jq
            nkb = jq + 4
            rr = (h * 8 + r) % 2
            bt = biasT[rr]
            if j == 0:
                op('dve', lambda e: e.tensor_scalar(out=bt.t[:, 0:nkb], in0=cumT.t[:, 0:nkb, h], scalar1=carry.t[:, jq + 2, h:h + 1], scalar2=-1.0,
                                                   op0=ALU.subtract, op1=ALU.mult), reads=[cumT.d, carry.d], writes=[bt.d])
                op('dve', lambda e: e.tensor_tensor(out=bt.t[:, 0:1], in0=bt.t[:, 0:1], in1=padneg, op=ALU.add),
                   reads=[bt.d, cf.d], writes=[bt.d])
            pS_ = pS[i % 3]
            PT_ = PT[i % 4]
            pO_, pZ_ = pO[rr], pZ[rr]
            VBt = VBts[h % 2]
            op('act', lambda e: e.activation(out=PT_.t[:], in_=pS_.t[:], func=AF.Exp, bias=bt.t[:, j:j + 1], scale=sc_),
               reads=[pS_.d, bt.d], writes=[PT_.d])
            if j >= jq:
                q_ = j - jq
                op('dve', lambda e: e.tensor_tensor(out=PT_.t[:], in0=PT_.t[:], in1=cmk.t[:, q_ * 512:(q_ + 1) * 512], op=ALU.mult),
                   reads=[PT_.d, cmk.d], writes=[PT_.d])

            def mpv(e):
                e.matmul(pO_.t[:], lhsT=VBt.t[:, j, :], rhs=PT_.t[:], start=(j == 0), stop=(j == nkb - 1))
                return e.matmul(pZ_.t[:], lhsT=onesb.t[:], rhs=PT_.t[:], start=(j == 0), stop=(j == nkb - 1))
            op('pe', mpv, reads=[PT_.d, VBt.d, onesb.d], writes=[pO_.d, pZ_.d])
            if j == nkb - 1:
                rc = rec[rr]
                ob_ = obt[rr]
                op('dve', lambda e: e.reciprocal(out=rc.t[:], in_=pZ_.t[:]), reads=[pZ_.d], writes=[rc.d])
                op('dve', lambda e: e.tensor_tensor(out=ob_.t[:], in0=pO_.t[:], in1=rc.t[:], op=ALU.mult),
                   reads=[pO_.d, rc.d], writes=[ob_.d])
                dma('sp', s_obT[h][:, t0:t0 + 512], ob_.t[:], reads=[ob_.d])

        LA = 2
        for i in range(LA):
            emit_S(i)
        for i in range(len(steps)):
            if i + LA < len(steps):
                emit_S(i + LA)
            emit_rest(i)
        S.barrier()
        if upto <= 6:
            return nc, S, es

    with ExitStack() as ph:
        wba = sb("wba", [128, 8, D], BF16, ph)
        wbb = sb("wbb", [128, 8, D], BF16, ph)
        wo = sb("wo", [128, 8, D], BF16, ph)
        for hh in range(2):
            cs = slice(hh * 512, (hh + 1) * 512)
            dma('pool', wba.t[:, :, cs], wbr_in[0][:, cs].rearrange("(h p) c -> p h c", p=128), writes=[wba.d])
            dma('pool', wbb.t[:, :, cs], wbr_in[1][:, cs].rearrange("(h p) c -> p h c", p=128), writes=[wbb.d])
            dma('pool', wo.t[:, :, cs], wout_in[:, cs].rearrange("(h p) c -> p h c", p=128), writes=[wo.d])
        oat = [sb(f"oat{i}", [128, D], F32, ph) for i in range(2)]
        oab = [sb(f"oab{i}", [128, D], BF16, ph) for i in range(2)]
        oaT = [sb(f"oaT{i}", [128, 8, 128], BF16, ph) for i in range(2)]
        obT = [sb(f"obT{i}", [128, 8, 128], BF16, ph) for i in range(2)]
        gt = [sb(f"gt{i}", [128, 2 * D], F32, ph) for i in range(2)]
        xr = [sb(f"xr{i}", [128, D], F32, ph) for i in range(2)]
        t1 = [sb(f"t1_{i}", [128, 512], F32, ph) for i in range(2)]
        t2 = [sb(f"t2_{i}", [128, 512], F32, ph) for i in range(2)]
        mb = [sb(f"mb{i}", [128, D], BF16, ph) for i in range(2)]
        mT = [sb(f"mT{i}", [128, 8, 128], BF16, ph) for i in range(2)]
        r2 = [sb(f"r2_{i}", [128, D], F32, ph) for i in range(2)]
        ptr = ps("mptr", [128, 1024], BF16, ph)
        pya = [ps(f"pya{i}", [128, 512], F32, ph) for i in range(2)]
        pyb = [ps(f"pyb{i}", [128, 512], F32, ph) for i in range(2)]
        pmx = [ps(f"pmx{i}", [128, 512], F32, ph) for i in range(2)]
        ptr2 = ps("mptr2", [128, 1024], BF16, ph)
        def mstage_A(i):
            b_ = i % 2
            rows = slice(i * 128, (i + 1) * 128)
            dma('sp', oat[b_].t[:], s_oa[rows, :], writes=[oat[b_].d])
            dma('sp', obT[b_].t[:], s_obT[:, :, i * 128:(i + 1) * 128].rearrange("h e t -> e h t"), writes=[obT[b_].d])
            dma('sp', gt[b_].t[:], s_gt[rows, :], writes=[gt[b_].d])
            dma('sp', xr[b_].t[:], x_in[(i - 1) * 128:i * 128, :], writes=[xr[b_].d])
            op('act', lambda e, b_=b_: e.copy(out=oab[b_].t[:], in_=oat[b_].t[:]), reads=[oat[b_].d], writes=[oab[b_].d])

            def tr8(e, src):
                r = None
                for dc in range(8):
                    r = e.transpose(out=ptr.t[:, dc * 128:(dc + 1) * 128], in_=src.t[:, dc * 128:(dc + 1) * 128], identity=identb.t[:])
                return r
            op('pe', lambda e, b_=b_: tr8(e, oab[b_]), reads=[oab[b_].d, identb.d], writes=[ptr.d])
            op('dve', lambda e, b_=b_: e.tensor_copy(out=oaT[b_].t[:], in_=ptr.t[:].rearrange("p (c t) -> p c t", c=8)),
               reads=[ptr.d], writes=[oaT[b_].d])
            for hf in range(2):
                cs = slice(hf * 512, (hf + 1) * 512)

                def mmy(e, hf=hf, cs=cs, b_=b_):
                    r = None
                    for h in range(8):
                        e.matmul(pya[hf].t[:], lhsT=oaT[b_].t[:, h, :], rhs=wba.t[:, h, cs], start=(h == 0), stop=(h == 7))
                    for h in range(8):
                        r = e.matmul(pyb[hf].t[:], lhsT=obT[b_].t[:, h, :], rhs=wbb.t[:, h, cs], start=(h == 0), stop=(h == 7))
                    return r
                op('pe', mmy, reads=[oaT[b_].d, obT[b_].d, wba.d, wbb.d], writes=[pya[hf].d, pyb[hf].d])
                op('dve', lambda e, hf=hf, cs=cs, b_=b_: e.tensor_tensor(out=t1[hf].t[:], in0=pya[hf].t[:], in1=gt[b_].t[:, cs], op=ALU.mult),
                   reads=[pya[hf].d, gt[b_].d], writes=[t1[hf].d])
                op('dve', lambda e, hf=hf, b_=b_: e.tensor_tensor(out=t2[hf].t[:], in0=pyb[hf].t[:],
                                                                 in1=gt[b_].t[:, D + hf * 512:D + (hf + 1) * 512], op=ALU.mult),
                   reads=[pyb[hf].d, gt[b_].d], writes=[t2[hf].d])
                op('pool', lambda e, hf=hf, cs=cs, b_=b_: e.tensor_tensor(out=mb[b_].t[:, cs], in0=t1[hf].t[:], in1=t2[hf].t[:], op=ALU.add),
                   reads=[t1[hf].d, t2[hf].d], writes=[mb[b_].d])

        def tr8b(e, src):
            r = None
            for dc in range(8):
                r = e.transpose(out=ptr2.t[:, dc * 128:(dc + 1) * 128], in_=src.t[:, dc * 128:(dc + 1) * 128], identity=identb.t[:])
            return r

        def mstage_B(i):
            b_ = i % 2
            op('pe', lambda e, b_=b_: tr8b(e, mb[b_]), reads=[mb[b_].d, identb.d], writes=[ptr2.d])
            op('act', lambda e, b_=b_: e.copy(out=mT[b_].t[:], in_=ptr2.t[:].rearrange("p (c t) -> p c t", c=8)),
               reads=[ptr2.d], writes=[mT[b_].d])
            for hf in range(2):
                cs = slice(hf * 512, (hf + 1) * 512)

                def mmo(e, hf=hf, cs=cs, b_=b_):
                    r = None
                    for dc in range(8):
                        r = e.matmul(pmx[hf].t[:], lhsT=mT[b_].t[:, dc, :], rhs=wo.t[:, dc, cs], start=(dc == 0), stop=(dc == 7))
                    return r
                op('pe', mmo, reads=[mT[b_].d, wo.d], writes=[pmx[hf].d])
                op('dve', lambda e, hf=hf, cs=cs, b_=b_: e.tensor_tensor(out=r2[b_].t[:, cs], in0=pmx[hf].t[:], in1=xr[b_].t[:, cs], op=ALU.add),
                   reads=[pmx[hf].d, xr[b_].d], writes=[r2[b_].d])
            dma('sp', s_res2[(i - 1) * 128:i * 128, :], r2[b_].t[:], reads=[r2[b_].d])

        mstage_A(1)
        for i in range(1, NB):
            if i + 1 < NB:
                mstage_A(i + 1)
            mstage_B(i)
        S.barrier()
        if upto <= 7:
            return nc, S, es
    if upto < 5:
        S.barrier()
        if upto <= 8:
            return nc, S, es
        return nc, S, es
    ph5 = ExitStack()
    h2T = sb("h2T", [128, 8, 2048], BF16, ph5)
    h2Td = [Dep() for _ in range(16)]
    dscrd = [Dep() for _ in range(16)]
    with ExitStack() as ph:
        nffn = sb("nffn", [128, D], F32, ph)
        dma('sp', nffn.t[:], nffn_in, writes=[nffn.d])
        wq = sb("wq", [128, 8, 2048], BF16, ph)
        wqd = [Dep() for _ in range(4)]
        for c in range(4):
            dma('pool', wq.t[:, :, c * 512:(c + 1) * 512], wq_in[:, c * 512:(c + 1) * 512].rearrange("(dc p) c -> p dc c", p=128),
                writes=[wqd[c]])
        ksT = sb("ksT", [128, 16, 128], F32, ph)
        skl = [sb(f"skl{i}", [128, 128], F32, ph) for i in range(2)]
        pq = [ps(f"pq{i}", [128, 512], F32, ph) for i in range(2)]
        psc = [ps(f"psc{i}", [128, 512], F32, ph) for i in range(2)]
        ptr5 = ps("ptr5", [128, 1024], BF16, ph)
        for g in range(16):
            sk_ = skl[g % 2]
            dma('sp', sk_.t[:], sk_in[g], writes=[sk_.d])
            p = pq[g % 2]
            op('pe', lambda e, p=p, sk_=sk_: e.matmul(p.t[:, 0:128], lhsT=sk_.t[:], rhs=ident, start=True, stop=True),
               reads=[sk_.d, cf.d], writes=[p.d])
            op('act', lambda e, p=p, g=g: e.copy(out=ksT.t[:, g, :], in_=p.t[:, 0:128]), reads=[p.d], writes=[ksT.d])
        ta = [sb(f"fa{i}", [128, D], F32, ph) for i in range(2)]
        tb_ = [sb(f"fb{i}", [128, D], F32, ph) for i in range(2)]
        junk5 = sb("junk5", [128, D], F32, ph)
        ssq5 = [sb(f"ssq5_{i}", [128, 1], F32, ph) for i in range(2)]
        h2b = [sb(f"h2b{i}", [128, D], BF16, ph) for i in range(2)]
        qg4 = [sb(f"qg4_{i}", [128, 512], F32, ph) for i in range(2)]
        scs = [sb(f"sc{i}", [128, 16, 128], F32, ph) for i in range(2)]
        sc2s = sb("sc2s", [128, 16, 128], F32, ph)
        v16d = [Dep() for _ in range(16)]
        i16d = [Dep() for _ in range(16)]
        sc2d = [Dep() for _ in range(16)]
        c8d = [Dep() for _ in range(8)]
        ctd = [Dep() for _ in range(8)]
        v16 = sb("v16", [128, 16, 16], F32, ph)
        i16 = sb("i16", [128, 16, 16], U32, ph)
        i16f = sb("i16f", [128, 16, 16], F32, ph)
        cand = sb("cand", [128, 8, 256], F32, ph)
        c8 = sb("c8", [128, 8, 16], F32, ph)
        tau = sb("tau", [128, 8], F32, ph)
        mx = sb("mx", [128, 8], F32, ph)
        zz = sb("zz", [128, 8], F32, ph)
        msk = sb("msk", [128, 8, 256], F32, ph)
        gw = sb("gw", [128, 8, 256], F32, ph)
        gwc = sb("gwc", [128, 16, 128], F32, ph)
        i1c = sb("i1c", [128, 128], F32, ph)
        i2c = sb("i2c", [128, 128], F32, ph)
        iT = [sb(f"iT{i}", [128, 256], F32, ph) for i in range(2)]
        gwTs = [sb(f"gwTs{i}", [128, 128, 16], BF16, ph) for i in range(2)]
        def stage_A(j):
            sc = scs[j % 2]
            a_, b_ = ta[j % 2], tb_[j % 2]
            dma('sp', a_.t[:], s_res2[j * 128:(j + 1) * 128, :], writes=[a_.d])
            dma('sp', b_.t[:], s_res2[(16 + j) * 128:(17 + j) * 128, :], writes=[b_.d])
            op('dve', lambda e, a_=a_: e.tensor_scalar(out=a_.t[:], in0=a_.t[:], scalar1=ompcol, scalar2=None, op0=ALU.mult),
               reads=[a_.d, cf.d], writes=[a_.d])
            op('dve', lambda e, a_=a_, b_=b_: e.scalar_tensor_tensor(out=a_.t[:], in0=b_.t[:], scalar=pcol, in1=a_.t[:], op0=ALU.mult, op1=ALU.add),
               reads=[a_.d, b_.d, cf.d], writes=[a_.d])
            dma('sp', s_r2m[j * 128:(j + 1) * 128, :], a_.t[:], reads=[a_.d], writes=[dscrd[j]])
            sq = ssq5[j % 2]
            op('act', lambda e, a_=a_, sq=sq: e.activation(out=junk5.t[:], in_=a_.t[:], func=AF.Square, accum_out=sq.t[:]),
               reads=[a_.d], writes=[junk5.d, sq.d])
            rsqrt_act(sq.t[:], sq.t[:], 1.0 / D, EPS, [sq.d], [sq.d])
            hb = h2b[j % 2]
            op('dve', lambda e, a_=a_, sq=sq, hb=hb: e.scalar_tensor_tensor(out=hb.t[:], in0=a_.t[:], scalar=sq.t[:, 0:1], in1=nffn.t[:],
                                                                          op0=ALU.mult, op1=ALU.mult),
               reads=[a_.d, sq.d, nffn.d], writes=[hb.d])

            def tr8(e, hb=hb):
                r = None
                for dc in range(8):
                    r = e.transpose(out=ptr5.t[:, dc * 128:(dc + 1) * 128], in_=hb.t[:, dc * 128:(dc + 1) * 128], identity=identb.t[:])
                return r
            op('pe', tr8, reads=[hb.d, identb.d], writes=[ptr5.d])
            tcs = slice(j * 128, (j + 1) * 128)
            op('act', lambda e, tcs=tcs: e.copy(out=h2T.t[:, :, tcs], in_=ptr5.t[:].rearrange("p (c t) -> p c t", c=8)),
               reads=[ptr5.d], writes=[h2Td[j]])
            for g4 in range(4):
                p = pq[g4 % 2]

                def mq(e, p=p, g4=g4, tcs=tcs):
                    r = None
                    for gi in range(4):
                        g = g4 * 4 + gi
                        for dc in range(8):
                            r = e.matmul(p.t[:, gi * 128:(gi + 1) * 128], lhsT=wq.t[:, dc, g * 128:(g + 1) * 128], rhs=h2T.t[:, dc, tcs],
                                         start=(dc == 0), stop=(dc == 7))
                    return r
                op('pe', mq, reads=[h2Td[j], wqd[g4]], writes=[p.d])
                q_ = qg4[g4 % 2]
                op('act', lambda e, p=p, q_=q_: e.copy(out=q_.t[:], in_=p.t[:]), reads=[p.d], writes=[q_.d])
                p2 = psc[g4 % 2]

                def msc(e, p2=p2, q_=q_, g4=g4):
                    r = None
                    for gi in range(4):
                        r = e.matmul(p2.t[:, gi * 128:(gi + 1) * 128], lhsT=q_.t[:, gi * 128:(gi + 1) * 128], rhs=ksT.t[:, g4 * 4 + gi, :],
                                     start=True, stop=True)
                    return r
                op('pe', msc, reads=[q_.d, ksT.d], writes=[p2.d])
                op('dve', lambda e, p2=p2, g4=g4: e.tensor_copy(out=sc.t[:, g4 * 4:(g4 + 1) * 4, :],
                                                                in_=p2.t[:].rearrange("p (g n) -> p g n", g=4)),
                   reads=[p2.d], writes=[sc.d])

        def stage_B(j):
            sc = scs[j % 2]
            for g in range(16):
                op('dve', lambda e, g=g: e.max(out=v16.t[:, g, 0:8], in_=sc.t[:, g, :]), reads=[sc.d], writes=[v16d[g]])
            for g in range(16):
                op('dve', lambda e, g=g: e.max_index(out=i16.t[:, g, 0:8], in_max=v16.t[:, g, 0:8], in_values=sc.t[:, g, :]),
                   reads=[sc.d, v16d[g]], writes=[i16d[g]])
            for g in range(16):
                op('dve', lambda e, g=g: e.match_replace(out=sc2s.t[:, g, :], in_to_replace=v16.t[:, g, 0:8], in_values=sc.t[:, g, :], imm_value=-1e30),
                   reads=[sc.d, v16d[g]], writes=[sc2d[g]])
            for g in range(16):
                op('dve', lambda e, g=g: e.max(out=v16.t[:, g, 8:16], in_=sc2s.t[:, g, :]), reads=[sc2d[g]], writes=[v16d[g]])
            for g in range(16):
                op('dve', lambda e, g=g: e.max_index(out=i16.t[:, g, 8:16], in_max=v16.t[:, g, 8:16], in_values=sc2s.t[:, g, :]),
                   reads=[sc2d[g], v16d[g]], writes=[i16d[g]])
            op('dve', lambda e: e.tensor_copy(out=i16f.t[:], in_=i16.t[:]), reads=i16d, writes=[i16f.d])
            v16r = v16.t[:].rearrange("p (h two) k -> p h two k", two=2)
            i16r = i16f.t[:].rearrange("p (h two) k -> p h two k", two=2)
            cand4 = cand.t[:].rearrange("p h (a b) -> p h a b", a=16)
            op('dve', lambda e: e.tensor_tensor(out=cand4, in0=v16r[:, :, 0, :].unsqueeze(3).to_broadcast([128, 8, 16, 16]),
                                               in1=v16r[:, :, 1, :].unsqueeze(2).to_broadcast([128, 8, 16, 16]), op=ALU.add),
               reads=v16d, writes=[cand.d])
            for hh in range(8):
                op('dve', lambda e, hh=hh: e.max(out=c8.t[:, hh, 0:8], in_=cand.t[:, hh, :]), reads=[cand.d], writes=[c8d[hh]])
            for hh in range(8):
                op('dve', lambda e, hh=hh: e.match_replace(out=msk.t[:, hh, :], in_to_replace=c8.t[:, hh, 0:8], in_values=cand.t[:, hh, :], imm_value=-1e30),
                   reads=[cand.d, c8d[hh]], writes=[ctd[hh]])
            for hh in range(8):
                op('dve', lambda e, hh=hh: e.max(out=c8.t[:, hh, 8:16], in_=msk.t[:, hh, :]), reads=[ctd[hh]], writes=[c8d[hh]])
            op('dve', lambda e: e.tensor_reduce(out=tau.t[:], in_=c8.t[:, :, 8:16], axis=AX.X, op=ALU.min), reads=c8d, writes=[tau.d])
            op('dve', lambda e: e.tensor_reduce(out=mx.t[:], in_=c8.t[:, :, 0:8], axis=AX.X, op=ALU.max), reads=c8d, writes=[mx.d])
            op('dve', lambda e: e.tensor_tensor(out=msk.t[:], in0=cand.t[:], in1=tau.t[:].unsqueeze(2).to_broadcast([128, 8, 256]), op=ALU.is_ge),
               reads=[cand.d, tau.d], writes=[msk.d] + ctd)
            op('dve', lambda e: e.tensor_tensor(out=gw.t[:], in0=cand.t[:], in1=mx.t[:].unsqueeze(2).to_broadcast([128, 8, 256]), op=ALU.subtract),
               reads=[cand.d, mx.d], writes=[gw.d])
            op('act', lambda e: e.activation(out=gw.t[:], in_=gw.t[:], func=AF.Exp), reads=[gw.d], writes=[gw.d])
            op('dve', lambda e: e.tensor_tensor(out=gw.t[:], in0=gw.t[:], in1=msk.t[:], op=ALU.mult), reads=[gw.d, msk.d], writes=[gw.d])
            op('dve', lambda e: e.tensor_reduce(out=zz.t[:], in_=gw.t[:], axis=AX.X, op=ALU.add), reads=[gw.d], writes=[zz.d])
            op('dve', lambda e: e.reciprocal(out=zz.t[:], in_=zz.t[:]), reads=[zz.d], writes=[zz.d])
            op('dve', lambda e: e.tensor_tensor(out=gw.t[:], in0=gw.t[:], in1=zz.t[:].unsqueeze(2).to_broadcast([128, 8, 256]), op=ALU.mult),
               reads=[gw.d, zz.d], writes=[gw.d])
            op('dve', lambda e: e.tensor_copy(out=i1c.t[:].rearrange("p (h k) -> p h k", h=8), in_=i16r[:, :, 0, :]), reads=[i16f.d], writes=[i1c.d])
            op('dve', lambda e: e.tensor_copy(out=i2c.t[:].rearrange("p (h k) -> p h k", h=8), in_=i16r[:, :, 1, :]), reads=[i16f.d], writes=[i2c.d])
            op('pool', lambda e: e.tensor_copy(out=gwc.t[:].rearrange("p a (h b) -> p a h b", h=8),
                                              in_=gw.t[:].rearrange("p h (a b) -> p a h b", a=16)), reads=[gw.d], writes=[gwc.d])
            p = pq[0]

            def mti(e, p=p):
                e.matmul(p.t[:, 0:128], lhsT=i1c.t[:], rhs=ident, start=True, stop=True)
                return e.matmul(p.t[:, 128:256], lhsT=i2c.t[:], rhs=ident, start=True, stop=True)
            op('pe', mti, reads=[i1c.d, i2c.d, cf.d], writes=[p.d])
            it_ = iT[j % 2]
            op('act', lambda e, p=p, it_=it_: e.copy(out=it_.t[:], in_=p.t[:, 0:256]), reads=[p.d], writes=[it_.d])
            dma('sp', s_i1T[j], it_.t[:, 0:128], reads=[it_.d], writes=[dscrd[j]])
            dma('sp', s_i2T[j], it_.t[:, 128:256], reads=[it_.d], writes=[dscrd[j]])
            gts = gwTs[j % 2]
            for k4 in range(4):
                p = pq[1] if k4 % 2 else psc[1]

                def mtg(e, p=p, k4=k4):
                    r = None
                    for ki in range(4):
                        r = e.matmul(p.t[:, ki * 128:(ki + 1) * 128], lhsT=gwc.t[:, k4 * 4 + ki, :], rhs=ident, start=True, stop=True)
                    return r
                op('pe', mtg, reads=[gwc.d, cf.d], writes=[p.d])
                op('act', lambda e, p=p, k4=k4, gts=gts: e.copy(out=gts.t[:, :, k4 * 4:(k4 + 1) * 4].rearrange("p t k -> p k t"),
                                                               in_=p.t[:].rearrange("p (k t) -> p k t", k=4)),
                   reads=[p.d], writes=[gts.d])
            dma('sp', s_gwT[j], gts.t[:].rearrange("p t k -> p (t k)"), reads=[gts.d], writes=[dscrd[j]])

        stage_A(0)
        for j in range(16):
            if j + 1 < 16:
                stage_A(j + 1)
            stage_B(j)
        S.barrier()
        if upto <= 9:
            return nc, S, es

    with ExitStack() as ph:
        WT = sb("WT", [128, 256, 128], BF16, ph)
        i1s_ = [sb(f"i1s{i}", [128, 32], F32, ph) for i in range(2)]
        i2s_ = [sb(f"i2s{i}", [128, 32], F32, ph) for i in range(2)]
        gws_ = [sb(f"gws{i}", [128, 32, 16], BF16, ph) for i in range(2)]
        OH1_ = [sb(f"OH1{i}", [128, 32, 128], BF16, ph) for i in range(2)]
        OH2_ = [sb(f"OH2{i}", [128, 32, 128], BF16, ph) for i in range(2)]
        GWbd_ = [sb(f"GWbd{i}", [128, 32, 128], BF16, ph) for i in range(2)]
        Msb_ = [sb(f"Msb{i}", [128, 32, 128], BF16, ph) for i in range(1)] * 2
        utb = [sb(f"p_ut{i}", [128, 8 * 128], BF16, ph) for i in range(3)]
        vtb = [sb(f"p_vt{i}", [128, D], BF16, ph) for i in range(3)]
        utd = [[Dep(), Dep()] for _ in range(3)]
        vtd = [[Dep(), Dep()] for _ in range(3)]
        actb = [sb(f"p_act{i}", [128, 256], F32, ph) for i in range(2)]
        wab = [sb(f"p_wa{i}", [128, 256], BF16, ph) for i in range(2)]
        r2t = [sb(f"p_r2t{i}", [128, D], F32, ph) for i in range(1)] * 2
        yo = [sb(f"p_yo{i}", [128, D], F32, ph) for i in range(1)] * 2
        pA = [ps(f"pA{i}", [128, 512], F32, ph) for i in range(2)]
        po = [[ps(f"po{ts}{hf}", [128, 512], F32, ph) for hf in range(2)] for ts in range(2)]
        yd = Dep()
        bd3 = bdm.rearrange("p (h k) -> p h k", h=8)
        for tb in range(8):
            tcols = slice(tb * 256, (tb + 1) * 256)
            for sub in range(8):
                j = tb * 2 + sub // 4
                toff = (sub % 4) * 32
                i1s, i2s, gws, OH1, OH2, GWbd, Msb = (x_[sub % 2] for x_ in (i1s_, i2s_, gws_, OH1_, OH2_, GWbd_, Msb_))
                dma('sp', i1s.t[:], s_i1T[j][:, toff:toff + 32], reads=[dscrd[j]], writes=[i1s.d])
                dma('sp', i2s.t[:], s_i2T[j][:, toff:toff + 32], reads=[dscrd[j]], writes=[i2s.d])
                dma('sp', gws.t[:], s_gwT[j][:, toff * 16:(toff + 32) * 16].rearrange("p (t k) -> p t k", k=16),
                    reads=[dscrd[j]], writes=[gws.d])
                op('dve', lambda e: e.tensor_tensor(out=OH1.t[:], in0=iota.unsqueeze(1).to_broadcast([128, 32, 128]),
                                                   in1=i1s.t[:].unsqueeze(2).to_broadcast([128, 32, 128]), op=ALU.is_equal),
                   reads=[cf.d, i1s.d], writes=[OH1.d])
                op('dve', lambda e: e.tensor_tensor(out=OH2.t[:], in0=iota.unsqueeze(1).to_broadcast([128, 32, 128]),
                                                    in1=i2s.t[:].unsqueeze(2).to_broadcast([128, 32, 128]), op=ALU.is_equal),
                   reads=[cf.d, i2s.d], writes=[OH2.d])
                op('dve', lambda e: e.tensor_tensor(out=GWbd.t[:].rearrange("p t (h k) -> p t h k", h=8),
                                                   in0=gws.t[:].unsqueeze(2).to_broadcast([128, 32, 8, 16]),
                                                   in1=bd3.unsqueeze(1).to_broadcast([128, 32, 8, 16]), op=ALU.mult),
                   reads=[gws.d, cf.d], writes=[GWbd.d])
                for t4 in range(8):
                    p = pA[t4 % 2]

                    def mM(e, p=p, t4=t4):
                        r = None
                        for q_ in range(4):
                            t = t4 * 4 + q_
                            r = e.matmul(p.t[:, q_ * 128:(q_ + 1) * 128], lhsT=GWbd.t[:, t, :], rhs=OH2.t[:, t, :], start=True, stop=True)
                        return r
                    op('pe', mM, reads=[GWbd.d, OH2.d], writes=[p.d])
                    op('act', lambda e, p=p, t4=t4: e.copy(out=Msb.t[:, t4 * 4:(t4 + 1) * 4, :], in_=p.t[:].rearrange("p (q b) -> p q b", q=4)),
                       reads=[p.d], writes=[Msb.d])
                for t4 in range(8):
                    p = pA[t4 % 2]

                    def mW(e, p=p, t4=t4):
                        r = None
                        for q_ in range(4):
                            t = t4 * 4 + q_
                            r = e.matmul(p.t[:, q_ * 128:(q_ + 1) * 128], lhsT=Msb.t[:, t, :], rhs=OH1.t[:, t, :], start=True, stop=True)
                        return r
                    op('pe', mW, reads=[Msb.d, OH1.d], writes=[p.d])
                    tg = sub * 32 + t4 * 4
                    op('dve', lambda e, p=p, tg=tg: e.tensor_copy(out=WT.t[:, tg:tg + 4, :], in_=p.t[:].rearrange("p (q a) -> p q a", q=4)),
                       reads=[p.d], writes=[WT.d])
            def emit_A(a):
                ut = utb[a % 3]
                vt = vtb[a % 3]
                ud, vd = utd[a % 3], vtd[a % 3]
                dma('sp', ut.t[0:64, :], s_UT[a][0:64, :], writes=[ud[0]])
                dma('sp', ut.t[64:128, :], s_UT[a][64:128, :], writes=[ud[1]])
                dma('sp', vt.t[0:64, :], s_VB[a][0:64, :], writes=[vd[0]])
                dma('sp', vt.t[64:128, :], s_VB[a][64:128, :], writes=[vd[1]])
                p = pA[a % 2]

                def mA(e):
                    r = None
                    for dc in range(8):
                        r = e.matmul(p.t[:, 0:256], lhsT=ut.t[:, dc * 128:(dc + 1) * 128], rhs=h2T.t[:, dc, tcols], start=(dc == 0), stop=(dc == 7))
                    return r
                op('pe', mA, reads=ud + h2Td, writes=[p.d])

            def emit_O(a):
                vt = vtb[a % 3]
                vd = vtd[a % 3]
                p = pA[a % 2]
                ac = actb[a % 2]
                wa = wab[a % 2]
                op('act', lambda e: e.activation(out=ac.t[:], in_=p.t[:, 0:256], func=AF.Gelu), reads=[p.d], writes=[ac.d])
                op('dve', lambda e: e.tensor_tensor(out=wa.t[:], in0=ac.t[:], in1=WT.t[:, :, a], op=ALU.mult),
                   reads=[ac.d, WT.d], writes=[wa.d])

                def mO(e):
                    r = None
                    for ts in range(2):
                        for hf in range(2):
                            r = e.matmul(po[ts][hf].t[:], lhsT=wa.t[:, ts * 128:(ts + 1) * 128], rhs=vt.t[:, hf * 512:(hf + 1) * 512],
                                         start=(a == 0), stop=(a == NEXP_CH - 1))
                    return r
                op('pe', mO, reads=[wa.d] + vd, writes=[po[0][0].d, po[0][1].d, po[1][0].d, po[1][1].d])
            emit_A(0)
            for a in range(NEXP_CH):
                if a + 1 < NEXP_CH:
                    emit_A(a + 1)
                emit_O(a)
            for ts in range(2):
                jt = tb * 2 + ts
                rt = r2t[ts]
                yo_ = yo[ts]
                dma('sp', rt.t[:], s_r2m[jt * 128:(jt + 1) * 128, :], reads=[dscrd[jt]], writes=[rt.d])
                for hf in range(2):
                    cs = slice(hf * 512, (hf + 1) * 512)
                    op('dve', lambda e, ts=ts, hf=hf, cs=cs, rt=rt, yo_=yo_: e.tensor_tensor(out=yo_.t[:, cs], in0=po[ts][hf].t[:], in1=rt.t[:, cs], op=ALU.add),
                       reads=[po[ts][hf].d, rt.d], writes=[yo_.d])
                dma('sp', y_out[jt * 128:(jt + 1) * 128, :], yo_.t[:], reads=[yo_.d], writes=[yd])
        S.barrier()
    ph5.close()
    return nc, S, es


def _consts():
    cf = np.zeros((128, C_END), np.float32)
    s = np.arange(128)[:, None]
    c = np.arange(128)[None, :]
    cf[:, C_IDENT:C_IDENT + 128] = np.eye(128)
    cf[:, C_TRI:C_TRI + 128] = (s <= c)
    cf[:, C_ONES:C_ONES + 128] = 1.0
    cf[:, C_MSTRICT:C_MSTRICT + 128] = (c > s)
    cf[:, C_MINCL:C_MINCL + 128] = (c >= s)
    cf[:, C_IOTA:C_IOTA + 128] = np.broadcast_to(np.arange(128, dtype=np.float32)[None, :], (128, 128))
    cf[:, C_BD:C_BD + 128] = ((s // 16) == (c // 16))
    cf[:, C_PADNEG] = np.where(np.arange(128) < PADN, -30000.0, 0.0)
    cm = np.zeros((128, 4 * 512), np.float32)
    t = np.arange(512)[None, :]
    for q in range(4):
        cm[:, q * 512:(q + 1) * 512] = (t >= 128 * q + s)
    return cf, cm.astype(ml_dtypes.bfloat16)


def prep_inputs(x, meta_tokens, norm_mix, w_in, conv_w, a_log, dt_bias, o_norm_a, q_norm_b, k_norm_b, f_bias,
                w_branch, w_out, norm_ffn, peer_wq, peer_sub_keys, expert_u, expert_v, cores=range(8)):
    f = lambda a: np.ascontiguousarray(np.asarray(a, dtype=np.float32))
    w = f(w_in[0])
    w_fm = np.ascontiguousarray(np.concatenate([w[:, 0:3072], w[:, 4112:5136], w[:, 5136:6160]], axis=1))
    w_tm = np.ascontiguousarray(np.concatenate([w[:, 6160:7184], w[:, 3072:4096], w[:, 7192:9240], w[:, 4096:4104],
                                                w[:, 4104:4112], w[:, 7184:7192]], axis=1))
    cw = f(conv_w[0])
    convw = np.ascontiguousarray(cw.reshape(4, 24, 128).transpose(2, 1, 0).reshape(128, 96))
    hvec = np.ascontiguousarray(np.tile(np.concatenate([f(a_log[0]), f(dt_bias[0]), f(f_bias[0])])[None, :], (128, 1)))
    onw = np.ascontiguousarray(np.tile(f(o_norm_a[0])[None, :], (128, 1)))
    qkw = np.ascontiguousarray(np.stack([f(q_norm_b[0]), f(k_norm_b[0])], axis=1))
    sk = f(peer_sub_keys[0])
    subk = np.ascontiguousarray(sk.transpose(1, 0, 2, 3).reshape(16, 128, 128))
    cf, cm = _consts()
    shared = dict(meta=f(meta_tokens), nmix=np.ascontiguousarray(np.tile(f(norm_mix[0])[None, :], (128, 1))),
                  nffn=np.ascontiguousarray(np.tile(f(norm_ffn[0])[None, :], (128, 1))),
                  w_fm=w_fm, w_tm=w_tm, convw=convw, hvec=hvec, onw=onw, qkw=qkw, wbr=f(w_branch[0]), wout=f(w_out[0]),
                  wq=f(peer_wq[0]), subk=subk, eu=f(expert_u[0]), ev=f(expert_v[0]), cmask=cm)
    maps = []
    xs = np.asarray(x, dtype=np.float32)
    for c in cores:
        cfc = cf.copy()
        cfc[:, C_P] = float(c % 2)
        cfc[:, C_OMP] = 1.0 - float(c % 2)
        m = dict(shared)
        m["x"] = np.ascontiguousarray(xs[c // 2])
        m["cf32"] = cfc
        maps.append(m)
    return maps


_NC_CACHE = {}


def kernel(**inputs):
    if "nc" not in _NC_CACHE:
        _NC_CACHE["nc"] = build_program(debug=False)[0]
    nc = _NC_CACHE["nc"]
    maps = prep_inputs(**inputs)
    res = run_bass_kernel_spmd(nc, maps, core_ids=list(range(8)))
    out = np.zeros((4, SEQ, D), np.float32)
    for c in range(8):
        out[c // 2, (c % 2) * 2048:(c % 2) * 2048 + 2048] = np.asarray(res.results[c]["y"], dtype=np.float32)
    return out
```

```python
import numpy as np
import ml_dtypes
from contextlib import ExitStack
import concourse.bass as bass
import concourse.mybir as mybir
from concourse.bass_utils import run_bass_kernel_spmd

F32 = mybir.dt.float32
BF16 = mybir.dt.bfloat16
U32 = mybir.dt.uint32
ALU = mybir.AluOpType
AF = mybir.ActivationFunctionType
AX = mybir.AxisListType

D = 1024
SEQ = 4096
NMETA = 16
PADN = 112
LP = 4224
NB = 33
NH = 8
EPS = 1e-6
NEXP_CH = 128
TBLK = [(i * 512, 512) for i in range(8)] + [(4096, 128)]

O_CONV, O_Z, O_BA, O_AA, O_QKVB, O_FB, O_G = 0, 3072, 4096, 4104, 4112, 7184, 7192

C_IDENT, C_TRI, C_ONES, C_MSTRICT, C_MINCL, C_IOTA, C_BD, C_PADNEG, C_P, C_OMP, C_END = (
    0, 128, 256, 384, 512, 640, 768, 896, 897, 898, 899)


class Dep:
    __slots__ = ("w", "r", "excl")

    def __init__(self, excl=False):
        self.w = {}
        self.r = {}
        self.excl = excl


class Sched:
    def __init__(self, nc, es):
        self.nc = nc
        self.eh = {'pe': nc.tensor, 'act': nc.scalar, 'dve': nc.vector, 'pool': nc.gpsimd, 'sp': nc.sync}
        self.esem = {k: es.enter_context(nc.semaphore("S_" + k)) for k in self.eh}
        self.ecnt = {k: 0 for k in self.eh}
        self.known = {k: {} for k in self.eh}
        self.rings = {}
        for q, n in (('sp', 28), ('pool', 12)):
            self.rings[q] = dict(sems=[es.enter_context(nc.semaphore(f"R_{q}{i}")) for i in range(n)],
                                 cnt=[0] * n, nxt=0)
        self.ninst = 0
        self.nwait = {k: 0 for k in self.eh}

    def _wait(self, eng, toks):
        kn = self.known[eng]
        for sem, v in toks.items():
            if kn.get(sem, 0) < v:
                self.eh[eng].wait_ge(sem, v)
                self.nwait[eng] += 1
                kn[sem] = v

    @staticmethod
    def _deps(reads, writes):
        toks = {}

        def add(d):
            for s, v in d.items():
                if toks.get(s, 0) < v:
                    toks[s] = v
        for d in reads:
            add(d.w)
        for d in writes:
            add(d.w)
            add(d.r)
        return toks

    @staticmethod
    def _finish(tok, reads, writes):
        s, v = tok
        for d in reads:
            if d.r.get(s, 0) < v:
                d.r[s] = v
        for d in writes:
            d.w = {s: v}
            d.r = {}

    def op(self, eng, fn, reads=(), writes=()):
        ex = [d for d in reads if d.excl]
        if ex:
            reads = [d for d in reads if not d.excl]
            writes = list(writes) + ex
        toks = self._deps(reads, writes)
        if eng == 'pe':
            toks.pop(self.esem['pe'], None)
        self._wait(eng, toks)
        inst = fn(self.eh[eng])
        self.ecnt[eng] += 1
        inst.then_inc(self.esem[eng], 1)
        self._finish((self.esem[eng], self.ecnt[eng]), reads, writes)
        self.ninst += 1

    def dma(self, q, out, in_, reads=(), writes=()):
        ring = self.rings[q]
        i = ring['nxt']
        ring['nxt'] = (i + 1) % len(ring['sems'])
        sem = ring['sems'][i]
        toks = self._deps(reads, writes)
        if ring['cnt'][i] > 0 and toks.get(sem, 0) < ring['cnt'][i]:
            toks[sem] = ring['cnt'][i]
        self._wait(q, toks)
        self.eh[q].dma_start(out=out, in_=in_).then_inc(sem, 16)
        ring['cnt'][i] += 16
        self._finish((sem, ring['cnt'][i]), reads, writes)
        self.ninst += 1

    def all_tokens(self):
        toks = {self.esem[k]: self.ecnt[k] for k in self.eh if self.ecnt[k] > 0}
        for ring in self.rings.values():
            for s, c in zip(ring['sems'], ring['cnt']):
                if c > 0:
                    toks[s] = c
        return toks

    def barrier(self):
        toks = self.all_tokens()
        for eng in self.eh:
            self._wait(eng, dict(toks))


class TD:
    def __init__(self, t):
        self.t = t
        self.d = Dep()


KNOB = dict(skip=0, dh=NH, dn=NB, feat=99, skipE=0, dump=1023)


def build_program(debug=False, upto=99):
    _sk = KNOB['skip']
    nc = bass.Bass("TRN2", target_bir_lowering=False)
    es = ExitStack()

    def din(name, shape, dt=F32):
        return nc.dram_tensor(name, list(shape), dt, kind="ExternalInput").ap()

    def dscr(name, shape, dt=F32):
        kind = "ExternalOutput" if (debug and name in ("s_oa", "s_obT", "s_res2", "s_qTa", "s_kTa", "s_vTa",
                                                       "s_qTb", "s_kTb")) else "Internal"
        return nc.dram_tensor(name, list(shape), dt, kind=kind).ap()

    x_in = din("x", [SEQ, D])
    meta_in = din("meta", [NMETA, D])
    nmix_in = din("nmix", [128, D])
    nffn_in = din("nffn", [128, D])
    wfm_in = din("w_fm", [D, 5120])
    wtm_in = din("w_tm", [D, 4120])
    convw_in = din("convw", [128, 24 * 4])
    hv_in = din("hvec", [128, 24])
    onw_in = din("onw", [128, 128])
    qkw_in = din("qkw", [128, 2])
    wbr_in = din("wbr", [2, D, D])
    wout_in = din("wout", [D, D])
    wq_in = din("wq", [D, 2048])
    sk_in = din("subk", [16, 128, 128])
    eu_in = din("eu", [16384, D])
    ev_in = din("ev", [16384, D])
    cf_in = din("cf32", [128, C_END])
    cm_in = din("cmask", [128, 4 * 512], BF16)
    y_out = nc.dram_tensor("y", [2048, D], F32, kind="ExternalOutput").ap()

    s_qTa = dscr("s_qTa", [NH, 128, LP])
    s_kTa = dscr("s_kTa", [NH, 128, LP])
    s_vTa = dscr("s_vTa", [NH, 128, LP])
    s_qTb = dscr("s_qTb", [NH, 128, LP], BF16)
    s_kTb = dscr("s_kTb", [NH, 128, LP], BF16)
    s_vb = dscr("s_vb", [LP, D], BF16)
    s_zs = dscr("s_zs", [LP, D])
    s_gt = dscr("s_gt", [LP, 2 * D])
    s_oa = dscr("s_oa", [LP, D])
    s_obT = dscr("s_obT", [NH, 128, LP], BF16)
    s_res2 = dscr("s_res2", [SEQ, D])
    s_r2m = dscr("s_r2m", [2048, D])
    s_UT = dscr("s_UT", [NEXP_CH, 128, 8 * 128], BF16)
    s_VB = dscr("s_VB", [NEXP_CH, 128, D], BF16)
    s_GT = dscr("s_GT", [8, LP])
    s_NGT = dscr("s_NGT", [8, LP])
    dGT = Dep()
    s_i1T = dscr("s_i1T", [16, 128, 128])
    s_i2T = dscr("s_i2T", [16, 128, 128])
    s_gwT = dscr("s_gwT", [16, 128, 128 * 16], BF16)

    S = Sched(nc, es)
    op, dma = S.op, S.dma

    def sb(name, shape, dt=F32, stack=None):
        return TD((stack or es).enter_context(nc.sbuf_tensor("sb_" + name, list(shape), dt)))

    def ps(name, shape, dt=F32, stack=None):
        t_ = TD((stack or es).enter_context(nc.psum_tensor("ps_" + name, list(shape), dt)))
        t_.d.excl = True
        return t_

    _pad0 = sb("pad0", [128, 16])
    cf = sb("cf", [128, C_END])
    cmk = sb("cmk", [128, 4 * 512], BF16)
    identb = sb("identb", [128, 128], BF16)
    onesb = sb("onesb", [128, 128], BF16)
    hv = sb("hv", [128, 24])
    dma('sp', cf.t[:], cf_in, writes=[cf.d])
    dma('sp', cmk.t[:], cm_in, writes=[cmk.d])
    dma('sp', hv.t[:], hv_in, writes=[hv.d])
    ident = cf.t[:, C_IDENT:C_IDENT + 128]
    tri = cf.t[:, C_TRI:C_TRI + 128]
    ones = cf.t[:, C_ONES:C_ONES + 128]
    mstrict = cf.t[:, C_MSTRICT:C_MSTRICT + 128]
    mincl = cf.t[:, C_MINCL:C_MINCL + 128]
    iota = cf.t[:, C_IOTA:C_IOTA + 128]
    bdm = cf.t[:, C_BD:C_BD + 128]
    padneg = cf.t[:, C_PADNEG:C_PADNEG + 1]
    pcol = cf.t[:, C_P:C_P + 1]
    ompcol = cf.t[:, C_OMP:C_OMP + 1]
    op('dve', lambda e: e.tensor_copy(out=identb.t[:], in_=ident), reads=[cf.d], writes=[identb.d])
    op('dve', lambda e: e.tensor_copy(out=onesb.t[:], in_=ones), reads=[cf.d], writes=[onesb.d])

    beta = sb("beta", [128, NB, 8])
    nbeta = sb("nbeta", [128, NB, 8])
    gtok = sb("gtok", [128, NB, 8])
    lf = sb("lf", [128, NB, 8])

    def rsqrt_act(out, in_, scale, eps, rd, wr, post_bias=0.0):
        op('act', lambda e: e.activation(out=out, in_=in_, func=AF.Ln, bias=eps, scale=scale), reads=rd, writes=wr)
        op('act', lambda e: e.activation(out=out, in_=out, func=AF.Exp, bias=post_bias, scale=-0.5),
           reads=wr, writes=wr)

    with ExitStack() as ph:
        ub = [sb(f"e_ub{i}", [128, D], F32, ph) for i in range(2)]
        utb = [sb(f"e_utb{i}", [128, 8 * 128], BF16, ph) for i in range(2)]
        vbb = [sb(f"e_vbb{i}", [128, D], BF16, ph) for i in range(3)]
        pt = [ps(f"e_pt{i}", [128, 512], F32, ph) for i in range(4)]
        for a in range(0 if (_sk or KNOB['skipE']) else NEXP_CH):
            u = ub[a % 2]
            dma('sp', u.t[:], eu_in[a * 128:(a + 1) * 128, :], writes=[u.d])
            ut = utb[a % 2]
            for half in range(2):
                p = pt[(a * 2 + half) % 4]

                def tr4(e, p=p, u=u, half=half):
                    r = None
                    for k in range(4):
                        dc = half * 4 + k
                        r = e.transpose(out=p.t[:, k * 128:(k + 1) * 128], in_=u.t[:, dc * 128:(dc + 1) * 128],
                                        identity=ident)
                    return r
                op('pe', tr4, reads=[u.d, cf.d], writes=[p.d])
                eng = 'act' if half == 0 else 'dve'
                if eng == 'act':
                    op('act', lambda e, p=p, ut=ut, half=half: e.copy(out=ut.t[:, half * 512:(half + 1) * 512], in_=p.t[:]),
                       reads=[p.d], writes=[ut.d])
                else:
                    op('dve', lambda e, p=p, ut=ut, half=half: e.tensor_copy(out=ut.t[:, half * 512:(half + 1) * 512], in_=p.t[:]),
                       reads=[p.d], writes=[ut.d])
            dma('sp', s_UT[a], ut.t[:], reads=[ut.d])
            v = vbb[a % 3]
            dma('pool', v.t[:], ev_in[a * 128:(a + 1) * 128, :], writes=[v.d])
            dma('sp', s_VB[a], v.t[:], reads=[v.d])
        S.barrier()
        if upto <= 0:
            return nc, S, es

    ph1 = ExitStack()
    hT = sb("hT", [128, 8, LP], BF16, ph1)
    hTd = [Dep() for _ in range(NB)]
    with ExitStack() as ph:
        nmix = sb("nmix", [128, D], F32, ph)
        dma('sp', nmix.t[:], nmix_in, writes=[nmix.d])
        xt = [sb(f"xt{i}", [128, D], F32, ph) for i in range(2)]
        junk = [sb(f"junk{i}", [128, D], F32, ph) for i in range(2)]
        hn = [sb(f"hn{i}", [128, D], BF16, ph) for i in range(2)]
        ssq = [sb(f"ssq{i}", [128, 1], F32, ph) for i in range(2)]
        ptr = [ps(f"ptr{i}", [128, 1024], BF16, ph) for i in range(2)]
        op('dve', lambda e: e.memset(xt[0].t[:], 0.0), writes=[xt[0].d])
        for i in range(0 if _sk else NB):
            x_ = xt[i % 2]
            if i == 0:
                dma('sp', x_.t[PADN:128, :], meta_in, writes=[x_.d])
            else:
                dma('sp', x_.t[:], x_in[(i - 1) * 128:i * 128, :], writes=[x_.d])
            sq = ssq[i % 2]
            jk = junk[i % 2]
            op('act', lambda e, x_=x_, jk=jk, sq=sq: e.activation(out=jk.t[:], in_=x_.t[:], func=AF.Square, accum_out=sq.t[:]),
               reads=[x_.d], writes=[jk.d, sq.d])
            rsqrt_act(sq.t[:], sq.t[:], 1.0 / D, EPS, [sq.d], [sq.d])
            h_ = hn[i % 2]
            op('dve', lambda e, x_=x_, sq=sq, h_=h_: e.scalar_tensor_tensor(out=h_.t[:], in0=x_.t[:], scalar=sq.t[:, 0:1], in1=nmix.t[:],
                                                                       op0=ALU.mult, op1=ALU.mult),
               reads=[x_.d, sq.d, nmix.d], writes=[h_.d])
            p = ptr[i % 2]

            def tr8(e, p=p, h_=h_):
                r = None
                for dc in range(8):
                    r = e.transpose(out=p.t[:, dc * 128:(dc + 1) * 128], in_=h_.t[:, dc * 128:(dc + 1) * 128],
                                    identity=identb.t[:])
                return r
            op('pe', tr8, reads=[h_.d, identb.d], writes=[p.d])
            op('act', lambda e, p=p, i=i: e.copy(out=hT.t[:, :, i * 128:(i + 1) * 128],
                                                 in_=p.t[:].rearrange("p (c t) -> p c t", c=8)),
               reads=[p.d], writes=[hTd[i]])
        S.barrier()
        if upto <= 1:
            return nc, S, es

    with ExitStack() as ph:
        wtm = sb("wtm", [128, 8, 4120], BF16, ph)
        wtmd = [Dep() for _ in range(9)]
        for c in range(9):
            c0 = c * 512
            cw = min(512, 4120 - c0)
            dma('pool', wtm.t[:, :, c0:c0 + cw],
                wtm_in[:, c0:c0 + cw].rearrange("(dc p) c -> p dc c", p=128), writes=[wtmd[c]])
        pp = [ps(f"pp{i}", [128, 512], F32, ph) for i in range(4)]
        ob = [sb(f"ob{i}", [128, 512], F32, ph) for i in range(4)]
        obb = [sb(f"obb{i}", [128, 512], BF16, ph) for i in range(2)]
        sm = sb("sm", [128, 24], F32, ph)
        nea = sb("nea", [128, 8], F32, ph)
        op('act', lambda e: e.activation(out=nea.t[:], in_=hv.t[:, 0:8], func=AF.Exp), reads=[hv.d], writes=[nea.d])
        op('dve', lambda e: e.tensor_scalar(out=nea.t[:], in0=nea.t[:], scalar1=-1.0, scalar2=None, op0=ALU.mult),
           reads=[nea.d], writes=[nea.d])
        k = 0
        for i in range(0 if _sk else NB):
            for c in range(9):
                c0 = c * 512
                cw = min(512, 4120 - c0)
                p = pp[k % 4]

                def mm(e, p=p, i=i, c0=c0, cw=cw):
                    r = None
                    for dc in range(8):
                        r = e.matmul(p.t[:, 0:cw], lhsT=hT.t[:, dc, i * 128:(i + 1) * 128], rhs=wtm.t[:, dc, c0:c0 + cw],
                                     start=(dc == 0), stop=(dc == 7))
                    return r
                op('pe', mm, reads=[hTd[i], wtmd[c]], writes=[p.d])
                rows = slice(i * 128, (i + 1) * 128)
                if c < 2:
                    o = obb[k % 2]
                    op('act', lambda e, o=o, p=p: e.copy(out=o.t[:], in_=p.t[:]), reads=[p.d], writes=[o.d])
                    dma('sp', s_vb[rows, c0:c0 + 512], o.t[:], reads=[o.d])
                elif c < 4:
                    o = ob[k % 4]
                    op('act', lambda e, o=o, p=p: e.activation(out=o.t[:], in_=p.t[:], func=AF.Silu), reads=[p.d], writes=[o.d])
                    dma('sp', s_zs[rows, c0 - 1024:c0 - 1024 + 512], o.t[:], reads=[o.d])
                elif c < 8:
                    o = ob[k % 4]
                    op('act', lambda e, o=o, p=p: e.activation(out=o.t[:], in_=p.t[:], func=AF.Sigmoid), reads=[p.d], writes=[o.d])
                    dma('sp', s_gt[rows, c0 - 2048:c0 - 2048 + 512], o.t[:], reads=[o.d])
                else:
                    op('act', lambda e, p=p, i=i: e.activation(out=beta.t[:, i, :], in_=p.t[:, 0:8], func=AF.Sigmoid),
                       reads=[p.d], writes=[beta.d])
                    op('dve', lambda e, p=p: e.tensor_tensor(out=sm.t[:, 8:24], in0=p.t[:, 8:24], in1=hv.t[:, 8:24], op=ALU.add),
                       reads=[p.d, hv.d], writes=[sm.d])
                    op('act', lambda e: e.activation(out=sm.t[:, 8:16], in_=sm.t[:, 8:16], func=AF.Exp), reads=[sm.d], writes=[sm.d])
                    op('act', lambda e: e.activation(out=sm.t[:, 16:24], in_=sm.t[:, 16:24], func=AF.Exp, scale=-1.0),
                       reads=[sm.d], writes=[sm.d])
                    op('act', lambda e: e.activation(out=sm.t[:, 8:24], in_=sm.t[:, 8:24], func=AF.Ln, bias=1.0),
                       reads=[sm.d], writes=[sm.d])
                    op('dve', lambda e, i=i: e.tensor_scalar(out=lf.t[:, i, :], in0=sm.t[:, 16:24], scalar1=-1.0, scalar2=None, op0=ALU.mult),
                       reads=[sm.d], writes=[lf.d])
                    op('dve', lambda e, i=i: e.tensor_tensor(out=gtok.t[:, i, :], in0=sm.t[:, 8:16], in1=nea.t[:], op=ALU.mult),
                       reads=[sm.d, nea.d], writes=[gtok.d])
                    op('dve', lambda e, i=i: e.tensor_scalar(out=nbeta.t[:, i, :], in0=beta.t[:, i, :], scalar1=-1.0, scalar2=None, op0=ALU.mult),
                       reads=[beta.d], writes=[nbeta.d])
                k += 1
        S.barrier()
        if upto <= 2:
            return nc, S, es
    with ExitStack() as ph:
        convw = sb("convw", [128, 96], F32, ph)
        qkw = sb("qkw", [128, 2], F32, ph)
        dma('sp', convw.t[:], convw_in, writes=[convw.d])
        dma('sp', qkw.t[:], qkw_in, writes=[qkw.d])
        wsl = [sb(f"wsl{i}", [128, 8, 128], BF16, ph) for i in range(2)]
        raw = [sb(f"raw{i}", [128, LP], F32, ph) for i in range(2)]
        acc = sb("acc", [128, LP], F32, ph)
        sqb = sb("sqb", [128, LP], F32, ph)
        rsb = sb("rsb", [128, LP], F32, ph)
        outb = sb("outb", [128, LP], BF16, ph)
        pp = [ps(f"fp{i}", [128, 512], F32, ph) for i in range(4)]
        pn = [ps(f"fn{i}", [128, 512], F32, ph) for i in range(2)]
        k = 0
        for g in range(0 if _sk else 40):
            w = wsl[g % 2]
            dma('pool', w.t[:], wfm_in[:, g * 128:(g + 1) * 128].rearrange("(dc p) c -> p dc c", p=128), writes=[w.d])
            r_ = raw[g % 2]
            for (t0, tw) in TBLK:
                p = pp[k % 4]
                k += 1

                def mm(e, p=p, w=w, t0=t0, tw=tw):
                    r = None
                    for dc in range(8):
                        r = e.matmul(p.t[:, 0:tw], lhsT=w.t[:, dc, :], rhs=hT.t[:, dc, t0:t0 + tw], start=(dc == 0), stop=(dc == 7))
                    return r
                op('pe', mm, reads=[w.d] + hTd, writes=[p.d])
                op('act', lambda e, p=p, r_=r_, t0=t0, tw=tw: e.copy(out=r_.t[:, t0:t0 + tw], in_=p.t[:, 0:tw]),
                   reads=[p.d], writes=[r_.d])
            if g < 24:
                cw = lambda i: convw.t[:, g * 4 + i:g * 4 + i + 1]
                op('dve', lambda e, r_=r_, c=cw(3): e.tensor_scalar(out=acc.t[:], in0=r_.t[:], scalar1=c, scalar2=None, op0=ALU.mult),
                   reads=[r_.d, convw.d], writes=[acc.d])
                for s_ in (1, 2, 3):
                    op('dve', lambda e, r_=r_, s_=s_, c=cw(3 - s_): e.scalar_tensor_tensor(
                        out=acc.t[:, s_:], in0=r_.t[:, 0:LP - s_], scalar=c, in1=acc.t[:, s_:], op0=ALU.mult, op1=ALU.add),
                       reads=[r_.d, convw.d], writes=[acc.d])
                op('act', lambda e: e.activation(out=acc.t[:], in_=acc.t[:], func=AF.Silu), reads=[acc.d], writes=[acc.d])
                src = acc
            else:
                src = r_
            if g >= 16 and g < 24:
                dma('sp', s_vTa[g - 16], acc.t[:], reads=[acc.d])
                continue
            op('pool', lambda e, src=src: e.tensor_tensor(out=sqb.t[:], in0=src.t[:], in1=src.t[:], op=ALU.mult),
               reads=[src.d], writes=[sqb.d])
            for bi, (t0, tw) in enumerate(TBLK):
                p = pn[bi % 2]
                op('pe', lambda e, p=p, t0=t0, tw=tw: e.matmul(p.t[:, 0:tw], lhsT=ones, rhs=sqb.t[:, t0:t0 + tw], start=True, stop=True),
                   reads=[sqb.d, cf.d], writes=[p.d])
                if g < 8:
                    rsqrt_act(rsb.t[:, t0:t0 + tw], p.t[:, 0:tw], 1.0, EPS, [p.d], [rsb.d], post_bias=float(np.log(128.0 ** -0.5)))
                elif g < 16:
                    rsqrt_act(rsb.t[:, t0:t0 + tw], p.t[:, 0:tw], 1.0, EPS, [p.d], [rsb.d])
                else:
                    rsqrt_act(rsb.t[:, t0:t0 + tw], p.t[:, 0:tw], 1.0 / 128.0, EPS, [p.d], [rsb.d])
            if g < 16:
                op('dve', lambda e: e.tensor_tensor(out=sqb.t[:], in0=acc.t[:], in1=rsb.t[:], op=ALU.mult),
                   reads=[acc.d, rsb.d], writes=[sqb.d])
                dst = s_qTa[g] if g < 8 else s_kTa[g - 8]
                dma('sp', dst, sqb.t[:], reads=[sqb.d])
            else:
                j = 0 if g < 32 else 1
                op('dve', lambda e, r_=r_, j=j: e.scalar_tensor_tensor(out=outb.t[:], in0=r_.t[:], scalar=qkw.t[:, j:j + 1], in1=rsb.t[:],
                                                                      op0=ALU.mult, op1=ALU.mult),
                   reads=[r_.d, rsb.d, qkw.d], writes=[outb.d])
                dst = s_qTb[g - 24] if g < 32 else s_kTb[g - 32]
                dma('sp', dst, outb.t[:], reads=[outb.d])
        S.barrier()
        if upto <= 3:
            return nc, S, es
    ph1.close()

    Gc = sb("Gc", [128, NB, 8])
    Gl = sb("Gl", [128, NB, 8])
    eG = sb("eG", [128, NB, 8])
    negeG = sb("negeG", [128, NB, 8])
    etail = sb("etail", [128, NB, 8])
    eGl = sb("eGl", [128, NB, 8])
    cumT = sb("cumT", [128, NB, 8])
    carry = sb("carry", [128, NB + 1, 8])
    with ExitStack() as ph:
        pw = [ps(f"pw{i}", [128, 512], F32, ph) for i in range(2)]
        op('dve', lambda e: e.memset(carry.t[:, 0, :], 0.0), writes=[carry.d])
        for b in range(NB):
            p = pw[b % 2]

            def mm(e, p=p, b=b):
                e.matmul(p.t[:, 0:8], lhsT=tri, rhs=gtok.t[:, b, :], start=True, stop=True)
                e.matmul(p.t[:, 8:16], lhsT=ones, rhs=gtok.t[:, b, :], start=True, stop=True)
                e.matmul(p.t[:, 16:24], lhsT=tri, rhs=lf.t[:, b, :], start=True, stop=True)
                return e.matmul(p.t[:, 24:32], lhsT=ones, rhs=lf.t[:, b, :], start=True, stop=True)
            op('pe', mm, reads=[gtok.d, lf.d, cf.d], writes=[p.d])
            op('act', lambda e, p=p, b=b: e.copy(out=Gc.t[:, b, :], in_=p.t[:, 0:8]), reads=[p.d], writes=[Gc.d])
            op('act', lambda e, p=p, b=b: e.copy(out=Gl.t[:, b, :], in_=p.t[:, 8:16]), reads=[p.d], writes=[Gl.d])
            op('dve', lambda e, p=p, b=b: e.tensor_tensor(out=cumT.t[:, b, :], in0=p.t[:, 16:24], in1=carry.t[:, b, :], op=ALU.add),
               reads=[p.d, carry.d], writes=[cumT.d])
            op('dve', lambda e, p=p, b=b: e.tensor_tensor(out=carry.t[:, b + 1, :], in0=p.t[:, 24:32], in1=carry.t[:, b, :], op=ALU.add),
               reads=[p.d, carry.d], writes=[carry.d])
        op('act', lambda e: e.activation(out=eG.t[:], in_=Gc.t[:], func=AF.Exp), reads=[Gc.d], writes=[eG.d])
        op('dve', lambda e: e.tensor_scalar(out=negeG.t[:], in0=eG.t[:], scalar1=-1.0, scalar2=None, op0=ALU.mult),
           reads=[eG.d], writes=[negeG.d])
        op('dve', lambda e: e.tensor_tensor(out=etail.t[:], in0=Gl.t[:], in1=Gc.t[:], op=ALU.subtract),
           reads=[Gl.d, Gc.d], writes=[etail.d])
        op('act', lambda e: e.activation(out=etail.t[:], in_=etail.t[:], func=AF.Exp), reads=[etail.d], writes=[etail.d])
        op('act', lambda e: e.activation(out=eGl.t[:], in_=Gl.t[:], func=AF.Exp), reads=[Gl.d], writes=[eGl.d])
        if debug:
            dbg_small = nc.dram_tensor("dbg_small", [10, 128, NB * 8], F32, kind="ExternalOutput").ap()
            for ii, tt in enumerate([beta, nbeta, gtok, lf, Gc, Gl, eG, etail, eGl, cumT]):
                if not (KNOB['dump'] >> ii) & 1:
                    continue
                dma('sp', dbg_small[ii], tt.t[:].rearrange("p b h -> p (b h)"), reads=[tt.d])
        S.barrier()
        if upto <= 4:
            return nc, S, es

    with ExitStack() as ph:
        GH = 4
        onw = sb("onw", [128, 128], F32, ph)
        dma('sp', onw.t[:], onw_in, writes=[onw.d])
        pbank = [ps(f"dp{i}", [128, 512], F32, ph) for i in range(2 * GH)]
        names = ["E", "EmS", "EmI", "X0", "X1", "Y0", "Y1", "P0", "P1", "attnT", "Ktail", "Vtok", "R", "vnew", "tmp", "o", "zs", "oa"]
        HS = []
        for gi in range(GH):
            HS.append(dict(
                KT=[sb(f"dKT{gi}_{p_}", [128, 128], F32, ph) for p_ in range(2)],
                QT=[sb(f"dQT{gi}_{p_}", [128, 128], F32, ph) for p_ in range(2)],
                VT=[sb(f"dVT{gi}_{p_}", [128, 128], F32, ph) for p_ in range(2)],
                S=sb(f"dS{gi}", [128, 128], F32, ph),
                W=[{n_: sb(f"dw{gi}_{p_}_{n_}", [128, 128], F32, ph) for n_ in names} for p_ in range(2)],
                ssq=[sb(f"dssq{gi}_{p_}", [128, 1], F32, ph) for p_ in range(2)],
                slots=[(pbank[2 * gi + q_].t[:, 0:256], pbank[2 * gi + q_].d) for q_ in range(2)], si=[0]))

        def mm(out, od, lhsT, rhs, rd):
            op('pe', lambda e: e.matmul(out, lhsT=lhsT, rhs=rhs, start=True, stop=True), reads=rd, writes=[od])

        def head_gen(h, hs):
            Sst = hs['S']

            def slot():
                s_ = hs['slots'][hs['si'][0] % 2]
                hs['si'][0] += 1
                return s_
            op('dve', lambda e: e.memset(Sst.t[:], 0.0), writes=[Sst.d])
            yield
            for n in range(NB):
                par = n % 2
                w = hs['W'][par]
                KT, QT, VT = hs['KT'][par], hs['QT'][par], hs['VT'][par]
                bs = slice(n * 128, (n + 1) * 128)
                dma('sp', KT.t[:], s_kTa[h][:, bs], writes=[KT.d])
                dma('sp', QT.t[:], s_qTa[h][:, bs], writes=[QT.d])
                dma('sp', VT.t[:], s_vTa[h][:, bs], writes=[VT.d])
                if n >= 1:
                    dma('sp', w["zs"].t[:], s_zs[bs, h * 128:(h + 1) * 128], writes=[w["zs"].d])
                col = lambda T: T.t[:, n, h:h + 1]
                op('pool', lambda e: e.tensor_scalar(out=w["tmp"].t[:], in0=ident, scalar1=col(Gc), scalar2=None, op0=ALU.mult),
                   reads=[cf.d, Gc.d], writes=[w["tmp"].d])
                pk, pkd = slot()

                def mkk(e):
                    e.matmul(pk[:, 0:128], lhsT=KT.t[:], rhs=KT.t[:], start=True, stop=True)
                    return e.matmul(pk[:, 128:256], lhsT=KT.t[:], rhs=QT.t[:], start=True, stop=True)
                op('pe', mkk, reads=[KT.d, QT.d], writes=[pkd])
                pd_, pdd = slot()
                mm(pd_[:, 0:128], pdd, ones, w["tmp"].t[:], [cf.d, w["tmp"].d])
                yield
                op('dve', lambda e: e.tensor_scalar(out=w["E"].t[:], in0=pd_[:, 0:128], scalar1=col(Gc), scalar2=0.0,
                                                   op0=ALU.subtract, op1=ALU.min), reads=[pdd, Gc.d], writes=[w["E"].d])
                op('act', lambda e: e.activation(out=w["E"].t[:], in_=w["E"].t[:], func=AF.Exp), reads=[w["E"].d], writes=[w["E"].d])
                op('dve', lambda e: e.tensor_scalar(out=w["R"].t[:], in0=pk[:, 0:128], scalar1=col(beta), scalar2=-1.0, op0=ALU.mult, op1=ALU.mult),
                   reads=[pkd, beta.d], writes=[w["R"].d])
                yield
                op('pool', lambda e: e.tensor_tensor(out=w["EmS"].t[:], in0=w["E"].t[:], in1=mstrict, op=ALU.mult),
                   reads=[w["E"].d, cf.d], writes=[w["EmS"].d])
                op('pool', lambda e: e.tensor_tensor(out=w["EmI"].t[:], in0=w["E"].t[:], in1=mincl, op=ALU.mult),
                   reads=[w["E"].d, cf.d], writes=[w["EmI"].d])
                op('pool', lambda e: e.tensor_tensor(out=w["X0"].t[:], in0=w["R"].t[:], in1=w["EmS"].t[:], op=ALU.mult),
                   reads=[w["R"].d, w["EmS"].d], writes=[w["X0"].d])
                yield
                op('dve', lambda e: e.tensor_tensor(out=w["attnT"].t[:], in0=pk[:, 128:256], in1=w["EmI"].t[:], op=ALU.mult),
                   reads=[pkd, w["EmI"].d], writes=[w["attnT"].d])
                py, pyd = slot()
                mm(py[:, 0:128], pyd, w["X0"].t[:], ident, [w["X0"].d, cf.d])
                op('pool', lambda e: e.tensor_tensor(out=w["P0"].t[:], in0=w["X0"].t[:], in1=ident, op=ALU.add),
                   reads=[w["X0"].d, cf.d], writes=[w["P0"].d])
                yield
                op('act', lambda e: e.copy(out=w["Y0"].t[:], in_=py[:, 0:128]), reads=[pyd], writes=[w["Y0"].d])
                pt_, ptd = slot()

                def mtr(e):
                    e.matmul(pt_[:, 0:128], lhsT=KT.t[:], rhs=ident, start=True, stop=True)
                    return e.matmul(pt_[:, 128:256], lhsT=VT.t[:], rhs=ident, start=True, stop=True)
                op('pe', mtr, reads=[KT.d, VT.d, cf.d], writes=[ptd])
                yield
                op('dve', lambda e: e.tensor_scalar(out=w["Ktail"].t[:], in0=pt_[:, 0:128], scalar1=col(etail), scalar2=None, op0=ALU.mult),
                   reads=[ptd, etail.d], writes=[w["Ktail"].d])
                op('dve', lambda e: e.tensor_copy(out=w["Vtok"].t[:], in_=pt_[:, 128:256]), reads=[ptd], writes=[w["Vtok"].d])
                yield
                Xc, Yc, Pc = w["X0"], w["Y0"], w["P0"]
                for j in range(1, 7):
                    Xn, Yn, Pn = w[f"X{j % 2}"], w[f"Y{j % 2}"], w[f"P{j % 2}"]
                    p1, p1d = slot()
                    mm(p1[:, 0:128], p1d, Xc.t[:], Yc.t[:], [Xc.d, Yc.d])
                    if j <= 5:
                        mm(p1[:, 128:256], p1d, Yc.t[:], Xc.t[:], [Xc.d, Yc.d])
                    yield
                    op('act', lambda e: e.copy(out=Yn.t[:], in_=p1[:, 0:128]), reads=[p1d], writes=[Yn.d])
                    if j <= 5:
                        op('dve', lambda e: e.tensor_copy(out=Xn.t[:], in_=p1[:, 128:256]), reads=[p1d], writes=[Xn.d])
                    p2, p2d = slot()
                    mm(p2[:, 0:128], p2d, Yn.t[:], Pc.t[:], [Yn.d, Pc.d])
                    yield
                    op('dve', lambda e: e.tensor_tensor(out=Pn.t[:], in0=p2[:, 0:128], in1=Pc.t[:], op=ALU.add),
                       reads=[p2d, Pc.d], writes=[Pn.d])
                    Xc, Yc, Pc = Xn, Yn, Pn
                TT = Pc
                pr, prd = slot()

                def mks(e):
                    e.matmul(pr[:, 0:128], lhsT=KT.t[:], rhs=Sst.t[:], start=True, stop=True)
                    return e.matmul(pr[:, 128:256], lhsT=QT.t[:], rhs=Sst.t[:], start=True, stop=True)
                op('pe', mks, reads=[KT.d, QT.d, Sst.d], writes=[prd])
                yield
                op('dve', lambda e: e.scalar_tensor_tensor(out=w["R"].t[:], in0=pr[:, 0:128], scalar=col(negeG), in1=w["Vtok"].t[:],
                                                          op0=ALU.mult, op1=ALU.add),
                   reads=[prd, negeG.d, w["Vtok"].d], writes=[w["R"].d])
                if n >= 1:
                    op('act', lambda e: e.activation(out=w["tmp"].t[:], in_=pr[:, 128:256], func=AF.Copy, scale=col(eG)),
                       reads=[prd, eG.d], writes=[w["tmp"].d])
                pv, pvd = slot()
                mm(pv[:, 0:128], pvd, TT.t[:], w["R"].t[:], [TT.d, w["R"].d])
                yield
                op('dve', lambda e: e.tensor_scalar(out=w["vnew"].t[:], in0=pv[:, 0:128], scalar1=col(beta), scalar2=None, op0=ALU.mult),
                   reads=[pvd, beta.d], writes=[w["vnew"].d])
                pu, pud = slot()
                mm(pu[:, 0:128], pud, w["Ktail"].t[:], w["vnew"].t[:], [w["Ktail"].d, w["vnew"].d])
                if n >= 1:
                    mm(pu[:, 128:256], pud, w["attnT"].t[:], w["vnew"].t[:], [w["attnT"].d, w["vnew"].d])
                yield
                op('dve', lambda e: e.scalar_tensor_tensor(out=Sst.t[:], in0=Sst.t[:], scalar=col(eGl), in1=pu[:, 0:128],
                                                          op0=ALU.mult, op1=ALU.add),
                   reads=[pud, eGl.d, Sst.d], writes=[Sst.d])
                if n >= 1:
                    op('dve', lambda e: e.tensor_tensor(out=w["o"].t[:], in0=pu[:, 128:256], in1=w["tmp"].t[:], op=ALU.add),
                       reads=[pud, w["tmp"].d], writes=[w["o"].d])
                    sq = hs['ssq'][par]
                    op('act', lambda e: e.activation(out=w["tmp"].t[:], in_=w["o"].t[:], func=AF.Square, accum_out=sq.t[:]),
                       reads=[w["o"].d], writes=[w["tmp"].d, sq.d])
                    yield
                    rsqrt_act(sq.t[:], sq.t[:], 1.0 / 128.0, EPS, [sq.d], [sq.d])
                    yield
                    op('dve', lambda e: e.scalar_tensor_tensor(out=w["oa"].t[:], in0=w["o"].t[:], scalar=sq.t[:, 0:1], in1=onw.t[:],
                                                              op0=ALU.mult, op1=ALU.mult),
                       reads=[w["o"].d, sq.d, onw.d], writes=[w["oa"].d])
                    op('pool', lambda e: e.tensor_tensor(out=w["oa"].t[:], in0=w["oa"].t[:], in1=w["zs"].t[:], op=ALU.mult),
                       reads=[w["oa"].d, w["zs"].d], writes=[w["oa"].d])
                    dma('sp', s_oa[bs, h * 128:(h + 1) * 128], w["oa"].t[:], reads=[w["oa"].d])
                yield

        for g0 in range(0, NH, GH):
            gens = [head_gen(g0 + gi, HS[gi]) for gi in range(GH)]
            while gens:
                for g_ in list(gens):
                    try:
                        next(g_)
                    except StopIteration:
                        gens.remove(g_)
        S.barrier()
        if upto <= 5:
            return nc, S, es
    with ExitStack() as ph:
        QBs = [sb(f"QB{i}", [128, LP], BF16, ph) for i in range(2)]
        KBs = [sb(f"KB{i}", [128, LP], BF16, ph) for i in range(2)]
        VBts = [sb(f"VBt{i}", [128, NB, 128], BF16, ph) for i in range(2)]
        biasT = [sb(f"biasT{i}", [128, NB + 3], F32, ph) for i in range(2)]
        pS = [ps(f"aS{i}", [128, 512], F32, ph) for i in range(3)]
        pO = [ps(f"aO{i}", [128, 512], F32, ph) for i in range(2)]
        pZ = [ps(f"aZ{i}", [128, 512], F32, ph) for i in range(2)]
        PT = [sb(f"PT{i}", [128, 512], BF16, ph) for i in range(4)]
        rec = [sb(f"rec{i}", [128, 512], F32, ph) for i in range(2)]
        obt = [sb(f"obt{i}", [128, 512], BF16, ph) for i in range(2)]
        sc_ = float(128.0 ** -0.5)
        steps = [(h, r, j) for h in range(NH) for r in range(8) for j in range(5 + 4 * r)]
        loaded = set()

        def ensure_head(h):
            if h in loaded:
                return
            loaded.add(h)
            dma('sp', QBs[h % 2].t[:], s_qTb[h], writes=[QBs[h % 2].d])
            dma('sp', KBs[h % 2].t[:], s_kTb[h], writes=[KBs[h % 2].d])
            dma('sp', VBts[h % 2].t[:], s_vb[:, h * 128:(h + 1) * 128].rearrange("(b p) e -> p b e", p=128), writes=[VBts[h % 2].d])

        def emit_S(i):
            h, r, j = steps[i]
            ensure_head(h)
            t0 = 128 * (1 + 4 * r)
            pS_ = pS[i % 3]
            QB, KB = QBs[h % 2], KBs[h % 2]
            op('pe', lambda e: e.matmul(pS_.t[:], lhsT=KB.t[:, j * 128:(j + 1) * 128], rhs=QB.t[:, t0:t0 + 512], start=True, stop=True),
               reads=[KB.d, QB.d], writes=[pS_.d])

        def emit_rest(i):
            h, r, j = steps[i]
            jq = 1 + 4 * r
            t0 = 128 * jq
            nkb = jq + 4
            rr = (h * 8 + r) % 2
            bt = biasT[rr]
            if j == 0:
                op('dve', lambda e: e.tensor_scalar(out=bt.t[:, 0:nkb], in0=cumT.t[:, 0:nkb, h], scalar1=carry.t[:, jq + 2, h:h + 1], scalar2=-1.0,
                                                   op0=ALU.subtract, op1=ALU.mult), reads=[cumT.d, carry.d], writes=[bt.d])
                op('dve', lambda e: e.tensor_tensor(out=bt.t[:, 0:1], in0=bt.t[:, 0:1], in1=padneg, op=ALU.add),
                   reads=[bt.d, cf.d], writes=[bt.d])
            pS_ = pS[i % 3]
            PT_ = PT[i % 4]
            pO_, pZ_ = pO[rr], pZ[rr]
            VBt = VBts[h % 2]
            op('act', lambda e: e.activation(out=PT_.t[:], in_=pS_.t[:], func=AF.Exp, bias=bt.t[:, j:j + 1], scale=sc_),
               reads=[pS_.d, bt.d], writes=[PT_.d])
            if j >= jq:
                q_ = j - jq
                op('dve', lambda e: e.tensor_tensor(out=PT_.t[:], in0=PT_.t[:], in1=cmk.t[:, q_ * 512:(q_ + 1) * 512], op=ALU.mult),
                   reads=[PT_.d, cmk.d], writes=[PT_.d])

            def mpv(e):
                e.matmul(pO_.t[:], lhsT=VBt.t[:, j, :], rhs=PT_.t[:], start=(j == 0), stop=(j == nkb - 1))
                return e.matmul(pZ_.t[:], lhsT=onesb.t[:], rhs=PT_.t[:], start=(j == 0), stop=(j == nkb - 1))
            op('pe', mpv, reads=[PT_.d, VBt.d, onesb.d], writes=[pO_.d, pZ_.d])
            if j == nkb - 1:
                rc = rec[rr]
                ob_ = obt[rr]
                op('dve', lambda e: e.reciprocal(out=rc.t[:], in_=pZ_.t[:]), reads=[pZ_.d], writes=[rc.d])
                op('dve', lambda e: e.tensor_tensor(out=ob_.t[:], in0=pO_.t[:], in1=rc.t[:], op=ALU.mult),
                   reads=[pO_.d, rc.d], writes=[ob_.d])
                dma('sp', s_obT[h][:, t0:t0 + 512], ob_.t[:], reads=[ob_.d])

        LA = 2
        for i in range(LA):
            emit_S(i)
        for i in range(len(steps)):
            if i + LA < len(steps):
                emit_S(i + LA)
            emit_rest(i)
        S.barrier()
        if upto <= 6:
            return nc, S, es

    with ExitStack() as ph:
        wba = sb("wba", [128, 8, D], BF16, ph)
        wbb = sb("wbb", [128, 8, D], BF16, ph)
        wo = sb("wo", [128, 8, D], BF16, ph)
        for hh in range(2):
            cs = slice(hh * 512, (hh + 1) * 512)
            dma('pool', wba.t[:, :, cs], wbr_in[0][:, cs].rearrange("(h p) c -> p h c", p=128), writes=[wba.d])
            dma('pool', wbb.t[:, :, cs], wbr_in[1][:, cs].rearrange("(h p) c -> p h c", p=128), writes=[wbb.d])
            dma('pool', wo.t[:, :, cs], wout_in[:, cs].rearrange("(h p) c -> p h c", p=128), writes=[wo.d])
        oat = [sb(f"oat{i}", [128, D], F32, ph) for i in range(2)]
        oab = [sb(f"oab{i}", [128, D], BF16, ph) for i in range(2)]
        oaT = [sb(f"oaT{i}", [128, 8, 128], BF16, ph) for i in range(2)]
        obT = [sb(f"obT{i}", [128, 8, 128], BF16, ph) for i in range(2)]
        gt = [sb(f"gt{i}", [128, 2 * D], F32, ph) for i in range(2)]
        xr = [sb(f"xr{i}", [128, D], F32, ph) for i in range(2)]
        t1 = [sb(f"t1_{i}", [128, 512], F32, ph) for i in range(2)]
        t2 = [sb(f"t2_{i}", [128, 512], F32, ph) for i in range(2)]
        mb = [sb(f"mb{i}", [128, D], BF16, ph) for i in range(2)]
        mT = [sb(f"mT{i}", [128, 8, 128], BF16, ph) for i in range(2)]
        r2 = [sb(f"r2_{i}", [128, D], F32, ph) for i in range(2)]
        ptr = ps("mptr", [128, 1024], BF16, ph)
        pya = [ps(f"pya{i}", [128, 512], F32, ph) for i in range(2)]
        pyb = [ps(f"pyb{i}", [128, 512], F32, ph) for i in range(2)]
        pmx = [ps(f"pmx{i}", [128, 512], F32, ph) for i in range(2)]
        ptr2 = ps("mptr2", [128, 1024], BF16, ph)
        def mstage_A(i):
            b_ = i % 2
            rows = slice(i * 128, (i + 1) * 128)
            dma('sp', oat[b_].t[:], s_oa[rows, :], writes=[oat[b_].d])
            dma('sp', obT[b_].t[:], s_obT[:, :, i * 128:(i + 1) * 128].rearrange("h e t -> e h t"), writes=[obT[b_].d])
            dma('sp', gt[b_].t[:], s_gt[rows, :], writes=[gt[b_].d])
            dma('sp', xr[b_].t[:], x_in[(i - 1) * 128:i * 128, :], writes=[xr[b_].d])
            op('act', lambda e, b_=b_: e.copy(out=oab[b_].t[:], in_=oat[b_].t[:]), reads=[oat[b_].d], writes=[oab[b_].d])

            def tr8(e, src):
                r = None
                for dc in range(8):
                    r = e.transpose(out=ptr.t[:, dc * 128:(dc + 1) * 128], in_=src.t[:, dc * 128:(dc + 1) * 128], identity=identb.t[:])
                return r
            op('pe', lambda e, b_=b_: tr8(e, oab[b_]), reads=[oab[b_].d, identb.d], writes=[ptr.d])
            op('dve', lambda e, b_=b_: e.tensor_copy(out=oaT[b_].t[:], in_=ptr.t[:].rearrange("p (c t) -> p c t", c=8)),
               reads=[ptr.d], writes=[oaT[b_].d])
            for hf in range(2):
                cs = slice(hf * 512, (hf + 1) * 512)

                def mmy(e, hf=hf, cs=cs, b_=b_):
                    r = None
                    for h in range(8):
                        e.matmul(pya[hf].t[:], lhsT=oaT[b_].t[:, h, :], rhs=wba.t[:, h, cs], start=(h == 0), stop=(h == 7))
                    for h in range(8):
                        r = e.matmul(pyb[hf].t[:], lhsT=obT[b_].t[:, h, :], rhs=wbb.t[:, h, cs], start=(h == 0), stop=(h == 7))
                    return r
                op('pe', mmy, reads=[oaT[b_].d, obT[b_].d, wba.d, wbb.d], writes=[pya[hf].d, pyb[hf].d])
                op('dve', lambda e, hf=hf, cs=cs, b_=b_: e.tensor_tensor(out=t1[hf].t[:], in0=pya[hf].t[:], in1=gt[b_].t[:, cs], op=ALU.mult),
                   reads=[pya[hf].d, gt[b_].d], writes=[t1[hf].d])
                op('dve', lambda e, hf=hf, b_=b_: e.tensor_tensor(out=t2[hf].t[:], in0=pyb[hf].t[:],
                                                                 in1=gt[b_].t[:, D + hf * 512:D + (hf + 1) * 512], op=ALU.mult),
                   reads=[pyb[hf].d, gt[b_].d], writes=[t2[hf].d])
                op('pool', lambda e, hf=hf, cs=cs, b_=b_: e.tensor_tensor(out=mb[b_].t[:, cs], in0=t1[hf].t[:], in1=t2[hf].t[:], op=ALU.add),
                   reads=[t1[hf].d, t2[hf].d], writes=[mb[b_].d])

        def tr8b(e, src):
            r = None
            for dc in range(8):
                r = e.transpose(out=ptr2.t[:, dc * 128:(dc + 1) * 128], in_=src.t[:, dc * 128:(dc + 1) * 128], identity=identb.t[:])
            return r

        def mstage_B(i):
            b_ = i % 2
            op('pe', lambda e, b_=b_: tr8b(e, mb[b_]), reads=[mb[b_].d, identb.d], writes=[ptr2.d])
            op('act', lambda e, b_=b_: e.copy(out=mT[b_].t[:], in_=ptr2.t[:].rearrange("p (c t) -> p c t", c=8)),
               reads=[ptr2.d], writes=[mT[b_].d])
            for hf in range(2):
                cs = slice(hf * 512, (hf + 1) * 512)

                def mmo(e, hf=hf, cs=cs, b_=b_):
                    r = None
                    for dc in range(8):
                        r = e.matmul(pmx[hf].t[:], lhsT=mT[b_].t[:, dc, :], rhs=wo.t[:, dc, cs], start=(dc == 0), stop=(dc == 7))
                    return r
                op('pe', mmo, reads=[mT[b_].d, wo.d], writes=[pmx[hf].d])
                op('dve', lambda e, hf=hf, cs=cs, b_=b_: e.tensor_tensor(out=r2[b_].t[:, cs], in0=pmx[hf].t[:], in1=xr[b_].t[:, cs], op=ALU.add),
                   reads=[pmx[hf].d, xr[b_].d], writes=[r2[b_].d])
            dma('sp', s_res2[(i - 1) * 128:i * 128, :], r2[b_].t[:], reads=[r2[b_].d])

        mstage_A(1)
        for i in range(1, NB):
            if i + 1 < NB:
                mstage_A(i + 1)
            mstage_B(i)
        S.barrier()
        if upto <= 7:
            return nc, S, es
    if upto < 5:
        S.barrier()
        if upto <= 8:
            return nc, S, es
        return nc, S, es
    ph5 = ExitStack()
    h2T = sb("h2T", [128, 8, 2048], BF16, ph5)
    h2Td = [Dep() for _ in range(16)]
    dscrd = [Dep() for _ in range(16)]
    with ExitStack() as ph:
        nffn = sb("nffn", [128, D], F32, ph)
        dma('sp', nffn.t[:], nffn_in, writes=[nffn.d])
        wq = sb("wq", [128, 8, 2048], BF16, ph)
        wqd = [Dep() for _ in range(4)]
        for c in range(4):
            dma('pool', wq.t[:, :, c * 512:(c + 1) * 512], wq_in[:, c * 512:(c + 1) * 512].rearrange("(dc p) c -> p dc c", p=128),
                writes=[wqd[c]])
        ksT = sb("ksT", [128, 16, 128], F32, ph)
        skl = [sb(f"skl{i}", [128, 128], F32, ph) for i in range(2)]
        pq = [ps(f"pq{i}", [128, 512], F32, ph) for i in range(2)]
        psc = [ps(f"psc{i}", [128, 512], F32, ph) for i in range(2)]
        ptr5 = ps("ptr5", [128, 1024], BF16, ph)
        for g in range(16):
            sk_ = skl[g % 2]
            dma('sp', sk_.t[:], sk_in[g], writes=[sk_.d])
            p = pq[g % 2]
            op('pe', lambda e, p=p, sk_=sk_: e.matmul(p.t[:, 0:128], lhsT=sk_.t[:], rhs=ident, start=True, stop=True),
               reads=[sk_.d, cf.d], writes=[p.d])
            op('act', lambda e, p=p, g=g: e.copy(out=ksT.t[:, g, :], in_=p.t[:, 0:128]), reads=[p.d], writes=[ksT.d])
        ta = [sb(f"fa{i}", [128, D], F32, ph) for i in range(2)]
        tb_ = [sb(f"fb{i}", [128, D], F32, ph) for i in range(2)]
        junk5 = sb("junk5", [128, D], F32, ph)
        ssq5 = [sb(f"ssq5_{i}", [128, 1], F32, ph) for i in range(2)]
        h2b = [sb(f"h2b{i}", [128, D], BF16, ph) for i in range(2)]
        qg4 = [sb(f"qg4_{i}", [128, 512], F32, ph) for i in range(2)]
        scs = [sb(f"sc{i}", [128, 16, 128], F32, ph) for i in range(2)]
        sc2s = sb("sc2s", [128, 16, 128], F32, ph)
        v16d = [Dep() for _ in range(16)]
        i16d = [Dep() for _ in range(16)]
        sc2d = [Dep() for _ in range(16)]
        c8d = [Dep() for _ in range(8)]
        ctd = [Dep() for _ in range(8)]
        v16 = sb("v16", [128, 16, 16], F32, ph)
        i16 = sb("i16", [128, 16, 16], U32, ph)
        i16f = sb("i16f", [128, 16, 16], F32, ph)
        cand = sb("cand", [128, 8, 256], F32, ph)
        c8 = sb("c8", [128, 8, 16], F32, ph)
        tau = sb("tau", [128, 8], F32, ph)
        mx = sb("mx", [128, 8], F32, ph)
        zz = sb("zz", [128, 8], F32, ph)
        msk = sb("msk", [128, 8, 256], F32, ph)
        gw = sb("gw", [128, 8, 256], F32, ph)
        gwc = sb("gwc", [128, 16, 128], F32, ph)
        i1c = sb("i1c", [128, 128], F32, ph)
        i2c = sb("i2c", [128, 128], F32, ph)
        iT = [sb(f"iT{i}", [128, 256], F32, ph) for i in range(2)]
        gwTs = [sb(f"gwTs{i}", [128, 128, 16], BF16, ph) for i in range(2)]
        def stage_A(j):
            sc = scs[j % 2]
            a_, b_ = ta[j % 2], tb_[j % 2]
            dma('sp', a_.t[:], s_res2[j * 128:(j + 1) * 128, :], writes=[a_.d])
            dma('sp', b_.t[:], s_res2[(16 + j) * 128:(17 + j) * 128, :], writes=[b_.d])
            op('dve', lambda e, a_=a_: e.tensor_scalar(out=a_.t[:], in0=a_.t[:], scalar1=ompcol, scalar2=None, op0=ALU.mult),
               reads=[a_.d, cf.d], writes=[a_.d])
            op('dve', lambda e, a_=a_, b_=b_: e.scalar_tensor_tensor(out=a_.t[:], in0=b_.t[:], scalar=pcol, in1=a_.t[:], op0=ALU.mult, op1=ALU.add),
               reads=[a_.d, b_.d, cf.d], writes=[a_.d])
            dma('sp', s_r2m[j * 128:(j + 1) * 128, :], a_.t[:], reads=[a_.d], writes=[dscrd[j]])
            sq = ssq5[j % 2]
            op('act', lambda e, a_=a_, sq=sq: e.activation(out=junk5.t[:], in_=a_.t[:], func=AF.Square, accum_out=sq.t[:]),
               reads=[a_.d], writes=[junk5.d, sq.d])
            rsqrt_act(sq.t[:], sq.t[:], 1.0 / D, EPS, [sq.d], [sq.d])
            hb = h2b[j % 2]
            op('dve', lambda e, a_=a_, sq=sq, hb=hb: e.scalar_tensor_tensor(out=hb.t[:], in0=a_.t[:], scalar=sq.t[:, 0:1], in1=nffn.t[:],
                                                                          op0=ALU.mult, op1=ALU.mult),
               reads=[a_.d, sq.d, nffn.d], writes=[hb.d])

            def tr8(e, hb=hb):
                r = None
                for dc in range(8):
                    r = e.transpose(out=ptr5.t[:, dc * 128:(dc + 1) * 128], in_=hb.t[:, dc * 128:(dc + 1) * 128], identity=identb.t[:])
                return r
            op('pe', tr8, reads=[hb.d, identb.d], writes=[ptr5.d])
            tcs = slice(j * 128, (j + 1) * 128)
            op('act', lambda e, tcs=tcs: e.copy(out=h2T.t[:, :, tcs], in_=ptr5.t[:].rearrange("p (c t) -> p c t", c=8)),
               reads=[ptr5.d], writes=[h2Td[j]])
            for g4 in range(4):
                p = pq[g4 % 2]

                def mq(e, p=p, g4=g4, tcs=tcs):
                    r = None
                    for gi in range(4):
                        g = g4 * 4 + gi
                        for dc in range(8):
                            r = e.matmul(p.t[:, gi * 128:(gi + 1) * 128], lhsT=wq.t[:, dc, g * 128:(g + 1) * 128], rhs=h2T.t[:, dc, tcs],
                                         start=(dc == 0), stop=(dc == 7))
                    return r
                op('pe', mq, reads=[h2Td[j], wqd[g4]], writes=[p.d])
                q_ = qg4[g4 % 2]
                op('act', lambda e, p=p, q_=q_: e.copy(out=q_.t[:], in_=p.t[:]), reads=[p.d], writes=[q_.d])
                p2 = psc[g4 % 2]

                def msc(e, p2=p2, q_=q_, g4=g4):
                    r = None
                    for gi in range(4):
                        r = e.matmul(p2.t[:, gi * 128:(gi + 1) * 128], lhsT=q_.t[:, gi * 128:(gi + 1) * 128], rhs=ksT.t[:, g4 * 4 + gi, :],
                                     start=True, stop=True)
                    return r
                op('pe', msc, reads=[q_.d, ksT.d], writes=[p2.d])
                op('dve', lambda e, p2=p2, g4=g4: e.tensor_copy(out=sc.t[:, g4 * 4:(g4 + 1) * 4, :],
                                                                in_=p2.t[:].rearrange("p (g n) -> p g n", g=4)),
                   reads=[p2.d], writes=[sc.d])

        def stage_B(j):
            sc = scs[j % 2]
            for g in range(16):
                op('dve', lambda e, g=g: e.max(out=v16.t[:, g, 0:8], in_=sc.t[:, g, :]), reads=[sc.d], writes=[v16d[g]])
            for g in range(16):
                op('dve', lambda e, g=g: e.max_index(out=i16.t[:, g, 0:8], in_max=v16.t[:, g, 0:8], in_values=sc.t[:, g, :]),
                   reads=[sc.d, v16d[g]], writes=[i16d[g]])
            for g in range(16):
                op('dve', lambda e, g=g: e.match_replace(out=sc2s.t[:, g, :], in_to_replace=v16.t[:, g, 0:8], in_values=sc.t[:, g, :], imm_value=-1e30),
                   reads=[sc.d, v16d[g]], writes=[sc2d[g]])
            for g in range(16):
                op('dve', lambda e, g=g: e.max(out=v16.t[:, g, 8:16], in_=sc2s.t[:, g, :]), reads=[sc2d[g]], writes=[v16d[g]])
            for g in range(16):
                op('dve', lambda e, g=g: e.max_index(out=i16.t[:, g, 8:16], in_max=v16.t[:, g, 8:16], in_values=sc2s.t[:, g, :]),
                   reads=[sc2d[g], v16d[g]], writes=[i16d[g]])
            op('dve', lambda e: e.tensor_copy(out=i16f.t[:], in_=i16.t[:]), reads=i16d, writes=[i16f.d])
            v16r = v16.t[:].rearrange("p (h two) k -> p h two k", two=2)
            i16r = i16f.t[:].rearrange("p (h two) k -> p h two k", two=2)
            cand4 = cand.t[:].rearrange("p h (a b) -> p h a b", a=16)
            op('dve', lambda e: e.tensor_tensor(out=cand4, in0=v16r[:, :, 0, :].unsqueeze(3).to_broadcast([128, 8, 16, 16]),
                                               in1=v16r[:, :, 1, :].unsqueeze(2).to_broadcast([128, 8, 16, 16]), op=ALU.add),
               reads=v16d, writes=[cand.d])
            for hh in range(8):
                op('dve', lambda e, hh=hh: e.max(out=c8.t[:, hh, 0:8], in_=cand.t[:, hh, :]), reads=[cand.d], writes=[c8d[hh]])
            for hh in range(8):
                op('dve', lambda e, hh=hh: e.match_replace(out=msk.t[:, hh, :], in_to_replace=c8.t[:, hh, 0:8], in_values=cand.t[:, hh, :], imm_value=-1e30),
                   reads=[cand.d, c8d[hh]], writes=[ctd[hh]])
            for hh in range(8):
                op('dve', lambda e, hh=hh: e.max(out=c8.t[:, hh, 8:16], in_=msk.t[:, hh, :]), reads=[ctd[hh]], writes=[c8d[hh]])
            op('dve', lambda e: e.tensor_reduce(out=tau.t[:], in_=c8.t[:, :, 8:16], axis=AX.X, op=ALU.min), reads=c8d, writes=[tau.d])
            op('dve', lambda e: e.tensor_reduce(out=mx.t[:], in_=c8.t[:, :, 0:8], axis=AX.X, op=ALU.max), reads=c8d, writes=[mx.d])
            op('dve', lambda e: e.tensor_tensor(out=msk.t[:], in0=cand.t[:], in1=tau.t[:].unsqueeze(2).to_broadcast([128, 8, 256]), op=ALU.is_ge),
               reads=[cand.d, tau.d], writes=[msk.d] + ctd)
            op('dve', lambda e: e.tensor_tensor(out=gw.t[:], in0=cand.t[:], in1=mx.t[:].unsqueeze(2).to_broadcast([128, 8, 256]), op=ALU.subtract),
               reads=[cand.d, mx.d], writes=[gw.d])
            op('act', lambda e: e.activation(out=gw.t[:], in_=gw.t[:], func=AF.Exp), reads=[gw.d], writes=[gw.d])
            op('dve', lambda e: e.tensor_tensor(out=gw.t[:], in0=gw.t[:], in1=msk.t[:], op=ALU.mult), reads=[gw.d, msk.d], writes=[gw.d])
            op('dve', lambda e: e.tensor_reduce(out=zz.t[:], in_=gw.t[:], axis=AX.X, op=ALU.add), reads=[gw.d], writes=[zz.d])
            op('dve', lambda e: e.reciprocal(out=zz.t[:], in_=zz.t[:]), reads=[zz.d], writes=[zz.d])
            op('dve', lambda e: e.tensor_tensor(out=gw.t[:], in0=gw.t[:], in1=zz.t[:].unsqueeze(2).to_broadcast([128, 8, 256]), op=ALU.mult),
               reads=[gw.d, zz.d], writes=[gw.d])
            op('dve', lambda e: e.tensor_copy(out=i1c.t[:].rearrange("p (h k) -> p h k", h=8), in_=i16r[:, :, 0, :]), reads=[i16f.d], writes=[i1c.d])
            op('dve', lambda e: e.tensor_copy(out=i2c.t[:].rearrange("p (h k) -> p h k", h=8), in_=i16r[:, :, 1, :]), reads=[i16f.d], writes=[i2c.d])
            op('pool', lambda e: e.tensor_copy(out=gwc.t[:].rearrange("p a (h b) -> p a h b", h=8),
                                              in_=gw.t[:].rearrange("p h (a b) -> p a h b", a=16)), reads=[gw.d], writes=[gwc.d])
            p = pq[0]

            def mti(e, p=p):
                e.matmul(p.t[:, 0:128], lhsT=i1c.t[:], rhs=ident, start=True, stop=True)
                return e.matmul(p.t[:, 128:256], lhsT=i2c.t[:], rhs=ident, start=True, stop=True)
            op('pe', mti, reads=[i1c.d, i2c.d, cf.d], writes=[p.d])
            it_ = iT[j % 2]
            op('act', lambda e, p=p, it_=it_: e.copy(out=it_.t[:], in_=p.t[:, 0:256]), reads=[p.d], writes=[it_.d])
            dma('sp', s_i1T[j], it_.t[:, 0:128], reads=[it_.d], writes=[dscrd[j]])
            dma('sp', s_i2T[j], it_.t[:, 128:256], reads=[it_.d], writes=[dscrd[j]])
            gts = gwTs[j % 2]
            for k4 in range(4):
                p = pq[1] if k4 % 2 else psc[1]

                def mtg(e, p=p, k4=k4):
                    r = None
                    for ki in range(4):
                        r = e.matmul(p.t[:, ki * 128:(ki + 1) * 128], lhsT=gwc.t[:, k4 * 4 + ki, :], rhs=ident, start=True, stop=True)
                    return r
                op('pe', mtg, reads=[gwc.d, cf.d], writes=[p.d])
                op('act', lambda e, p=p, k4=k4, gts=gts: e.copy(out=gts.t[:, :, k4 * 4:(k4 + 1) * 4].rearrange("p t k -> p k t"),
                                                               in_=p.t[:].rearrange("p (k t) -> p k t", k=4)),
                   reads=[p.d], writes=[gts.d])
            dma('sp', s_gwT[j], gts.t[:].rearrange("p t k -> p (t k)"), reads=[gts.d], writes=[dscrd[j]])

        stage_A(0)
        for j in range(16):
            if j + 1 < 16:
                stage_A(j + 1)
            stage_B(j)
        S.barrier()
        if upto <= 9:
            return nc, S, es

    with ExitStack() as ph:
        WT = sb("WT", [128, 256, 128], BF16, ph)
        i1s_ = [sb(f"i1s{i}", [128, 32], F32, ph) for i in range(2)]
        i2s_ = [sb(f"i2s{i}", [128, 32], F32, ph) for i in range(2)]
        gws_ = [sb(f"gws{i}", [128, 32, 16], BF16, ph) for i in range(2)]
        OH1_ = [sb(f"OH1{i}", [128, 32, 128], BF16, ph) for i in range(2)]
        OH2_ = [sb(f"OH2{i}", [128, 32, 128], BF16, ph) for i in range(2)]
        GWbd_ = [sb(f"GWbd{i}", [128, 32, 128], BF16, ph) for i in range(2)]
        Msb_ = [sb(f"Msb{i}", [128, 32, 128], BF16, ph) for i in range(1)] * 2
        utb = [sb(f"p_ut{i}", [128, 8 * 128], BF16, ph) for i in range(3)]
        vtb = [sb(f"p_vt{i}", [128, D], BF16, ph) for i in range(3)]
        actb = [sb(f"p_act{i}", [128, 256], F32, ph) for i in range(2)]
        wab = [sb(f"p_wa{i}", [128, 256], BF16, ph) for i in range(2)]
        r2t = [sb(f"p_r2t{i}", [128, D], F32, ph) for i in range(1)] * 2
        yo = [sb(f"p_yo{i}", [128, D], F32, ph) for i in range(1)] * 2
        pA = [ps(f"pA{i}", [128, 512], F32, ph) for i in range(2)]
        po = [[ps(f"po{ts}{hf}", [128, 512], F32, ph) for hf in range(2)] for ts in range(2)]
        yd = Dep()
        bd3 = bdm.rearrange("p (h k) -> p h k", h=8)
        for tb in range(8):
            tcols = slice(tb * 256, (tb + 1) * 256)
            for sub in range(8):
                j = tb * 2 + sub // 4
                toff = (sub % 4) * 32
                i1s, i2s, gws, OH1, OH2, GWbd, Msb = (x_[sub % 2] for x_ in (i1s_, i2s_, gws_, OH1_, OH2_, GWbd_, Msb_))
                dma('sp', i1s.t[:], s_i1T[j][:, toff:toff + 32], reads=[dscrd[j]], writes=[i1s.d])
                dma('sp', i2s.t[:], s_i2T[j][:, toff:toff + 32], reads=[dscrd[j]], writes=[i2s.d])
                dma('sp', gws.t[:], s_gwT[j][:, toff * 16:(toff + 32) * 16].rearrange("p (t k) -> p t k", k=16),
                    reads=[dscrd[j]], writes=[gws.d])
                op('dve', lambda e: e.tensor_tensor(out=OH1.t[:], in0=iota.unsqueeze(1).to_broadcast([128, 32, 128]),
                                                   in1=i1s.t[:].unsqueeze(2).to_broadcast([128, 32, 128]), op=ALU.is_equal),
                   reads=[cf.d, i1s.d], writes=[OH1.d])
                op('dve', lambda e: e.tensor_tensor(out=OH2.t[:], in0=iota.unsqueeze(1).to_broadcast([128, 32, 128]),
                                                    in1=i2s.t[:].unsqueeze(2).to_broadcast([128, 32, 128]), op=ALU.is_equal),
                   reads=[cf.d, i2s.d], writes=[OH2.d])
                op('pool', lambda e: e.tensor_tensor(out=GWbd.t[:].rearrange("p t (h k) -> p t h k", h=8),
                                                   in0=gws.t[:].unsqueeze(2).to_broadcast([128, 32, 8, 16]),
                                                   in1=bd3.unsqueeze(1).to_broadcast([128, 32, 8, 16]), op=ALU.mult),
                   reads=[gws.d, cf.d], writes=[GWbd.d])
                for t4 in range(8):
                    p = pA[t4 % 2]

                    def mM(e, p=p, t4=t4):
                        r = None
                        for q_ in range(4):
                            t = t4 * 4 + q_
                            r = e.matmul(p.t[:, q_ * 128:(q_ + 1) * 128], lhsT=GWbd.t[:, t, :], rhs=OH2.t[:, t, :], start=True, stop=True)
                        return r
                    op('pe', mM, reads=[GWbd.d, OH2.d], writes=[p.d])
                    op('act', lambda e, p=p, t4=t4: e.copy(out=Msb.t[:, t4 * 4:(t4 + 1) * 4, :], in_=p.t[:].rearrange("p (q b) -> p q b", q=4)),
                       reads=[p.d], writes=[Msb.d])
                for t4 in range(8):
                    p = pA[t4 % 2]

                    def mW(e, p=p, t4=t4):
                        r = None
                        for q_ in range(4):
                            t = t4 * 4 + q_
                            r = e.matmul(p.t[:, q_ * 128:(q_ + 1) * 128], lhsT=Msb.t[:, t, :], rhs=OH1.t[:, t, :], start=True, stop=True)
                        return r
                    op('pe', mW, reads=[Msb.d, OH1.d], writes=[p.d])
                    tg = sub * 32 + t4 * 4
                    op('act', lambda e, p=p, tg=tg: e.copy(out=WT.t[:, tg:tg + 4, :], in_=p.t[:].rearrange("p (q a) -> p q a", q=4)),
                       reads=[p.d], writes=[WT.d])
            def emit_A(a):
                ut = utb[a % 3]
                vt = vtb[a % 3]
                dma('sp', ut.t[:], s_UT[a], writes=[ut.d])
                dma('sp', vt.t[:], s_VB[a], writes=[vt.d])
                p = pA[a % 2]

                def mA(e):
                    r = None
                    for dc in range(8):
                        r = e.matmul(p.t[:, 0:256], lhsT=ut.t[:, dc * 128:(dc + 1) * 128], rhs=h2T.t[:, dc, tcols], start=(dc == 0), stop=(dc == 7))
                    return r
                op('pe', mA, reads=[ut.d] + h2Td, writes=[p.d])

            def emit_O(a):
                vt = vtb[a % 3]
                p = pA[a % 2]
                ac = actb[a % 2]
                wa = wab[a % 2]
                op('act', lambda e: e.activation(out=ac.t[:], in_=p.t[:, 0:256], func=AF.Gelu), reads=[p.d], writes=[ac.d])
                op('dve', lambda e: e.tensor_tensor(out=wa.t[:], in0=ac.t[:], in1=WT.t[:, :, a], op=ALU.mult),
                   reads=[ac.d, WT.d], writes=[wa.d])

                def mO(e):
                    r = None
                    for ts in range(2):
                        for hf in range(2):
                            r = e.matmul(po[ts][hf].t[:], lhsT=wa.t[:, ts * 128:(ts + 1) * 128], rhs=vt.t[:, hf * 512:(hf + 1) * 512],
                                         start=(a == 0), stop=(a == NEXP_CH - 1))
                    return r
                op('pe', mO, reads=[wa.d, vt.d], writes=[po[0][0].d, po[0][1].d, po[1][0].d, po[1][1].d])
            emit_A(0)
            for a in range(NEXP_CH):
                if a + 1 < NEXP_CH:
                    emit_A(a + 1)
                emit_O(a)
            for ts in range(2):
                jt = tb * 2 + ts
                rt = r2t[ts]
                yo_ = yo[ts]
                dma('sp', rt.t[:], s_r2m[jt * 128:(jt + 1) * 128, :], reads=[dscrd[jt]], writes=[rt.d])
                for hf in range(2):
                    cs = slice(hf * 512, (hf + 1) * 512)
                    op('dve', lambda e, ts=ts, hf=hf, cs=cs, rt=rt, yo_=yo_: e.tensor_tensor(out=yo_.t[:, cs], in0=po[ts][hf].t[:], in1=rt.t[:, cs], op=ALU.add),
                       reads=[po[ts][hf].d, rt.d], writes=[yo_.d])
                dma('sp', y_out[jt * 128:(jt + 1) * 128, :], yo_.t[:], reads=[yo_.d], writes=[yd])
        S.barrier()
    ph5.close()
    return nc, S, es


def _consts():
    cf = np.zeros((128, C_END), np.float32)
    s = np.arange(128)[:, None]
    c = np.arange(128)[None, :]
    cf[:, C_IDENT:C_IDENT + 128] = np.eye(128)
    cf[:, C_TRI:C_TRI + 128] = (s <= c)
    cf[:, C_ONES:C_ONES + 128] = 1.0
    cf[:, C_MSTRICT:C_MSTRICT + 128] = (c > s)
    cf[:, C_MINCL:C_MINCL + 128] = (c >= s)
    cf[:, C_IOTA:C_IOTA + 128] = np.broadcast_to(np.arange(128, dtype=np.float32)[None, :], (128, 128))
    cf[:, C_BD:C_BD + 128] = ((s // 16) == (c // 16))
    cf[:, C_PADNEG] = np.where(np.arange(128) < PADN, -30000.0, 0.0)
    cm = np.zeros((128, 4 * 512), np.float32)
    t = np.arange(512)[None, :]
    for q in range(4):
        cm[:, q * 512:(q + 1) * 512] = (t >= 128 * q + s)
    return cf, cm.astype(ml_dtypes.bfloat16)


def prep_inputs(x, meta_tokens, norm_mix, w_in, conv_w, a_log, dt_bias, o_norm_a, q_norm_b, k_norm_b, f_bias,
                w_branch, w_out, norm_ffn, peer_wq, peer_sub_keys, expert_u, expert_v, cores=range(8)):
    f = lambda a: np.ascontiguousarray(np.asarray(a, dtype=np.float32))
    w = f(w_in[0])
    w_fm = np.ascontiguousarray(np.concatenate([w[:, 0:3072], w[:, 4112:5136], w[:, 5136:6160]], axis=1))
    w_tm = np.ascontiguousarray(np.concatenate([w[:, 6160:7184], w[:, 3072:4096], w[:, 7192:9240], w[:, 4096:4104],
                                                w[:, 4104:4112], w[:, 7184:7192]], axis=1))
    cw = f(conv_w[0])
    convw = np.ascontiguousarray(cw.reshape(4, 24, 128).transpose(2, 1, 0).reshape(128, 96))
    hvec = np.ascontiguousarray(np.tile(np.concatenate([f(a_log[0]), f(dt_bias[0]), f(f_bias[0])])[None, :], (128, 1)))
    onw = np.ascontiguousarray(np.tile(f(o_norm_a[0])[None, :], (128, 1)))
    qkw = np.ascontiguousarray(np.stack([f(q_norm_b[0]), f(k_norm_b[0])], axis=1))
    sk = f(peer_sub_keys[0])
    subk = np.ascontiguousarray(sk.transpose(1, 0, 2, 3).reshape(16, 128, 128))
    cf, cm = _consts()
    shared = dict(meta=f(meta_tokens), nmix=np.ascontiguousarray(np.tile(f(norm_mix[0])[None, :], (128, 1))),
                  nffn=np.ascontiguousarray(np.tile(f(norm_ffn[0])[None, :], (128, 1))),
                  w_fm=w_fm, w_tm=w_tm, convw=convw, hvec=hvec, onw=onw, qkw=qkw, wbr=f(w_branch[0]), wout=f(w_out[0]),
                  wq=f(peer_wq[0]), subk=subk, eu=f(expert_u[0]), ev=f(expert_v[0]), cmask=cm)
    maps = []
    xs = np.asarray(x, dtype=np.float32)
    for c in cores:
        cfc = cf.copy()
        cfc[:, C_P] = float(c % 2)
        cfc[:, C_OMP] = 1.0 - float(c % 2)
        m = dict(shared)
        m["x"] = np.ascontiguousarray(xs[c // 2])
        m["cf32"] = cfc
        maps.append(m)
    return maps


_NC_CACHE = {}


def kernel(**inputs):
    if "nc" not in _NC_CACHE:
        _NC_CACHE["nc"] = build_program(debug=False)[0]
    nc = _NC_CACHE["nc"]
    maps = prep_inputs(**inputs)
    res = run_bass_kernel_spmd(nc, maps, core_ids=list(range(8)))
    out = np.zeros((4, SEQ, D), np.float32)
    for c in range(8):
        out[c // 2, (c % 2) * 2048:(c % 2) * 2048 + 2048] = np.asarray(res.results[c]["y"], dtype=np.float32)
    return out
```
